# Optimizing a Trainium2 kernel written in Bass

```python
import jax, jax.numpy as jnp
from jax import lax
import numpy as np

D_MODEL = 1024
BATCH = 16
SEQ = 2048
DEPTH = 4

GRID_W = 64
MIX_WIDTH = 2 * D_MODEL
FOURIER_WIDTH = MIX_WIDTH // 2
N_FOURIER_GROUPS = 4
FOURIER_GROUP = FOURIER_WIDTH // N_FOURIER_GROUPS
ATTN_WIDTH = MIX_WIDTH - FOURIER_WIDTH
HEAD_DIM = 128
N_Q_HEADS = ATTN_WIDTH // HEAD_DIM
N_KV_HEADS = 2
Q_PER_KV = N_Q_HEADS // N_KV_HEADS
KV_WIDTH = N_KV_HEADS * HEAD_DIM
Q_BLOCK = 128
ROPE_THETA = 10000.0
ROPE_AXIS_DIM = HEAD_DIM // 2
RNN_WIDTH = MIX_WIDTH
N_RNN_BLOCKS = 16
RNN_BLOCK = RNN_WIDTH // N_RNN_BLOCKS
CONV_WIDTH = 4
CONV_LEFT = 2
RG_LRU_C = 8.0
NORM_EPS = 1e-6
EVEN_IN_WIDTH = 2 * FOURIER_WIDTH + ATTN_WIDTH + 2 * KV_WIDTH + ATTN_WIDTH
EVEN_SPLITS = (FOURIER_WIDTH,
               2 * FOURIER_WIDTH,
               2 * FOURIER_WIDTH + ATTN_WIDTH,
               2 * FOURIER_WIDTH + ATTN_WIDTH + KV_WIDTH,
               2 * FOURIER_WIDTH + ATTN_WIDTH + 2 * KV_WIDTH)

kernel_name = "fourier_gqa_rglru_hybrid_encoder"


def rms_norm(x, g):
    xf = x.astype(jnp.float32)
    y = xf * lax.rsqrt(jnp.mean(xf * xf, axis=-1, keepdims=True) + NORM_EPS)
    return (y * g.astype(jnp.float32)).astype(x.dtype)


def axial_rope_tables(seq_len):
    rows = seq_len // GRID_W
    row = jnp.broadcast_to(jnp.arange(rows)[:, None], (rows, GRID_W)).reshape(-1).astype(jnp.float32)
    col = jnp.broadcast_to(jnp.arange(GRID_W)[None, :], (rows, GRID_W)).reshape(-1).astype(jnp.float32)
    inv_freq = ROPE_THETA ** (-jnp.arange(0, ROPE_AXIS_DIM, 2, dtype=jnp.float32) / ROPE_AXIS_DIM)
    ang_r = row[:, None] * inv_freq[None, :]
    ang_c = col[:, None] * inv_freq[None, :]
    return (jnp.cos(ang_r), jnp.sin(ang_r), jnp.cos(ang_c), jnp.sin(ang_c))


def rope_half(x, cos, sin):
    x1, x2 = jnp.split(x, 2, axis=-1)
    c = cos[:, None, :]
    s = sin[:, None, :]
    return jnp.concatenate([x1 * c - x2 * s, x1 * s + x2 * c], axis=-1)


def axial_rope(x, tables):
    cos_r, sin_r, cos_c, sin_c = tables
    xf = x.astype(jnp.float32)
    x_row, x_col = jnp.split(xf, 2, axis=-1)
    out = jnp.concatenate([rope_half(x_row, cos_r, sin_r), rope_half(x_col, cos_c, sin_c)], axis=-1)
    return out.astype(x.dtype)


def blocked_attention(q, k, v):
    b, s = q.shape[0], q.shape[1]
    n_blocks = s // Q_BLOCK
    qb = q.reshape(b, n_blocks, Q_BLOCK, N_KV_HEADS, Q_PER_KV, HEAD_DIM).transpose(1, 0, 2, 3, 4, 5)
    scale = HEAD_DIM ** -0.5

    def one_block(q_blk):
        scores = jnp.einsum('bqhgd,bkhd->bhgqk', q_blk, k,
                            preferred_element_type=jnp.float32) * scale
        probs = jax.nn.softmax(scores, axis=-1).astype(v.dtype)
        return jnp.einsum('bhgqk,bkhd->bqhgd', probs, v)

    out = lax.map(one_block, qb)
    return out.transpose(1, 0, 2, 3, 4, 5).reshape(b, s, N_Q_HEADS * HEAD_DIM)


def fourier_attention_layer(x, norm_g, w_in, fourier_w, q_gain, k_gain, w_out, rope):
    b, s, _ = x.shape
    h = rms_norm(x, norm_g)
    proj = h @ w_in
    f_in, f_gate, q, k, v, a_gate = jnp.split(proj, EVEN_SPLITS, axis=-1)
    fg = f_in.astype(jnp.float32).reshape(b, s, N_FOURIER_GROUPS, FOURIER_GROUP)
    fmix = jnp.fft.fftn(fg, axes=(1, 3), norm="ortho").real.astype(x.dtype)
    fmix = jnp.einsum('bsgc,gce->bsge', fmix, fourier_w).reshape(b, s, FOURIER_WIDTH)
    f_out = fmix * jax.nn.silu(f_gate)
    q = rms_norm(q.reshape(b, s, N_Q_HEADS, HEAD_DIM), q_gain)
    k = rms_norm(k.reshape(b, s, N_KV_HEADS, HEAD_DIM), k_gain)
    v = v.reshape(b, s, N_KV_HEADS, HEAD_DIM)
    q = axial_rope(q, rope)
    k = axial_rope(k, rope)
    a_out = blocked_attention(q, k, v) * jax.nn.silu(a_gate)
    return x + jnp.concatenate([f_out, a_out], axis=-1) @ w_out


def rglru_scan(xc, w_a, b_a, w_x, b_x, lam, reverse):
    b, s, _ = xc.shape
    xb = xc.astype(jnp.float32).reshape(b, s, N_RNN_BLOCKS, RNN_BLOCK)
    r = jax.nn.sigmoid(jnp.einsum('bshc,hce->bshe', xb, w_a.astype(jnp.float32)) + b_a).reshape(b, s, RNN_WIDTH)
    i = jax.nn.sigmoid(jnp.einsum('bshc,hce->bshe', xb, w_x.astype(jnp.float32)) + b_x).reshape(b, s, RNN_WIDTH)
    log_a = -RG_LRU_C * r * jax.nn.softplus(-lam.astype(jnp.float32))
    a = jnp.exp(log_a)
    u = jnp.sqrt(-jnp.expm1(2.0 * log_a)) * (i * xb.reshape(b, s, RNN_WIDTH))

    def combine(left, right):
        a1, h1 = left
        a2, h2 = right
        return a1 * a2, a2 * h1 + h2

    _, hseq = lax.associative_scan(combine, (a, u), reverse=reverse, axis=1)
    return hseq


def rglru_layer(x, norm_g, w_in, conv_w, conv_b, w_a, b_a, w_x, b_x, lam, w_out):
    s = x.shape[1]
    h = rms_norm(x, norm_g)
    xr, gate = jnp.split(h @ w_in, 2, axis=-1)
    xp = jnp.pad(xr, ((0, 0), (CONV_LEFT, CONV_WIDTH - 1 - CONV_LEFT), (0, 0)))
    xc = conv_b + xp[:, 0:s] * conv_w[0]
    for j in range(1, CONV_WIDTH):
        xc = xc + xp[:, j:j + s] * conv_w[j]
    y = (rglru_scan(xc, w_a[0], b_a[0], w_x[0], b_x[0], lam[0], reverse=False)
         + rglru_scan(xc, w_a[1], b_a[1], w_x[1], b_x[1], lam[1], reverse=True))
    return x + (y.astype(x.dtype) * jax.nn.silu(gate)) @ w_out


def setup_inputs(seed: int = 0) -> dict:
    key = jax.random.key(seed)
    ks = jax.random.split(key, 20)
    n_even = (DEPTH + 1) // 2
    n_odd = DEPTH // 2

    def nrm(k, shape, scale):
        return jax.random.normal(k, shape, jnp.float32) * scale

    x = nrm(ks[0], (BATCH, SEQ, D_MODEL), 1.0)
    even_norm = 1.0 + nrm(ks[1], (n_even, D_MODEL), 0.02)
    even_w_in = nrm(ks[2], (n_even, D_MODEL, EVEN_IN_WIDTH), D_MODEL ** -0.5)
    fourier_w = nrm(ks[3], (n_even, N_FOURIER_GROUPS, FOURIER_GROUP, FOURIER_GROUP), FOURIER_GROUP ** -0.5)
    q_gain = 1.0 + nrm(ks[4], (n_even, HEAD_DIM), 0.02)
    k_gain = 1.0 + nrm(ks[5], (n_even, HEAD_DIM), 0.02)
    even_w_out = nrm(ks[6], (n_even, MIX_WIDTH, D_MODEL), MIX_WIDTH ** -0.5)
    odd_norm = 1.0 + nrm(ks[7], (n_odd, D_MODEL), 0.02)
    odd_w_in = nrm(ks[8], (n_odd, D_MODEL, 2 * RNN_WIDTH), D_MODEL ** -0.5)
    conv_w = nrm(ks[9], (n_odd, CONV_WIDTH, RNN_WIDTH), CONV_WIDTH ** -0.5)
    conv_b = nrm(ks[10], (n_odd, RNN_WIDTH), 0.01)
    gate_a_w = nrm(ks[11], (n_odd, 2, N_RNN_BLOCKS, RNN_BLOCK, RNN_BLOCK), RNN_BLOCK ** -0.5)
    gate_a_b = nrm(ks[12], (n_odd, 2, N_RNN_BLOCKS, RNN_BLOCK), 0.01)
    gate_x_w = nrm(ks[13], (n_odd, 2, N_RNN_BLOCKS, RNN_BLOCK, RNN_BLOCK), RNN_BLOCK ** -0.5)
    gate_x_b = nrm(ks[14], (n_odd, 2, N_RNN_BLOCKS, RNN_BLOCK), 0.01)
    a_pow = jax.random.uniform(ks[15], (n_odd, 2, RNN_WIDTH), jnp.float32, minval=0.9, maxval=0.999)
    a0 = a_pow ** (1.0 / RG_LRU_C)
    rglru_lambda = jnp.log(a0) - jnp.log1p(-a0)
    odd_w_out = nrm(ks[16], (n_odd, RNN_WIDTH, D_MODEL), RNN_WIDTH ** -0.5)
    final_norm = 1.0 + nrm(ks[17], (D_MODEL,), 0.02)
    return {"x": x, "even_norm": even_norm, "even_w_in": even_w_in, "fourier_w": fourier_w,
            "q_gain": q_gain, "k_gain": k_gain, "even_w_out": even_w_out,
            "odd_norm": odd_norm, "odd_w_in": odd_w_in, "conv_w": conv_w, "conv_b": conv_b,
            "gate_a_w": gate_a_w, "gate_a_b": gate_a_b, "gate_x_w": gate_x_w, "gate_x_b": gate_x_b,
            "rglru_lambda": rglru_lambda, "odd_w_out": odd_w_out, "final_norm": final_norm}


def reference(x, even_norm, even_w_in, fourier_w, q_gain, k_gain, even_w_out,
              odd_norm, odd_w_in, conv_w, conv_b, gate_a_w, gate_a_b, gate_x_w, gate_x_b,
              rglru_lambda, odd_w_out, final_norm):
    rope = axial_rope_tables(x.shape[1])
    h = x
    for layer in range(DEPTH):
        i = layer // 2
        if layer % 2 == 0:
            h = fourier_attention_layer(h, even_norm[i], even_w_in[i], fourier_w[i],
                                        q_gain[i], k_gain[i], even_w_out[i], rope)
        else:
            h = rglru_layer(h, odd_norm[i], odd_w_in[i], conv_w[i], conv_b[i],
                            gate_a_w[i], gate_a_b[i], gate_x_w[i], gate_x_b[i],
                            rglru_lambda[i], odd_w_out[i])
    return rms_norm(h, final_norm)
```

```python
import math
from contextlib import ExitStack

import numpy as np
import ml_dtypes

import concourse.bass as bass
import concourse.mybir as mybir
from concourse.bass_utils import run_bass_kernel_spmd

F32 = mybir.dt.float32
BF16 = mybir.dt.bfloat16
AF = mybir.ActivationFunctionType
ALU = mybir.AluOpType

S = 2048
D = 1024
NCORES = 8
NB = 2
EPS = 1e-6
ATT_SCALE = 128 ** -0.5
ENGS = ("pe", "act", "dve", "pool", "sp")
ATTACH = True
NPS = 8
DBG = set()
NTT = 16


class Sched:
    def __init__(self):
        self.ops = {e: [] for e in ENGS}
        self.n = {e: 0 for e in ENGS}
        self.seen = {e: {} for e in ENGS}
        self.st = {}
        self.dcnt = {}
        self.window = 1 << 30

    def _need(self, eng, ev, raw):
        sem, val = ev
        if sem == eng:
            if eng == "pe" or not raw:
                return False
            return self.n[eng] - val < self.window
        return self.seen[eng].get(sem, 0) < val

    def op(self, eng, fn, reads=(), writes=(), dsem=None, attach=True):
        waits = {}

        def add(ev, raw):
            if ev is not None and self._need(eng, ev, raw):
                if waits.get(ev[0], 0) < ev[1]:
                    waits[ev[0]] = ev[1]

        for k in reads:
            s = self.st.get(k)
            if s is not None:
                add(s[0], True)
                if k[0] == "ps":
                    for sem, val in s[1].items():
                        if sem != eng:
                            add((sem, val), False)
        for k in writes:
            s = self.st.get(k)
            if s is not None:
                add(s[0], False)
                for sem, val in s[1].items():
                    add((sem, val), False)
        for sem, val in waits.items():
            if self.seen[eng].get(sem, 0) < val:
                self.seen[eng][sem] = val
        if dsem is None:
            self.n[eng] += 1
            ev = (eng, self.n[eng])
        else:
            self.dcnt[dsem] = self.dcnt.get(dsem, 0) + 16
            ev = (dsem, self.dcnt[dsem])
        self.ops[eng].append((list(waits.items()), fn, ev, ATTACH and attach))
        for k in reads:
            s = self.st.setdefault(k, [None, {}])
            if s[1].get(ev[0], 0) < ev[1]:
                s[1][ev[0]] = ev[1]
        for k in writes:
            self.st[k] = [ev, {}]
        return ev

    def barrier(self, engines=ENGS, clear=False):
        evs = [(e, self.n[e]) for e in ENGS if self.n[e] > 0]
        evs += list(self.dcnt.items())
        for e in engines:
            waits = []
            for sem, val in evs:
                if sem == e:
                    if e != "pe":
                        waits.append((sem, val))
                elif self.seen[e].get(sem, 0) < val:
                    waits.append((sem, val))
                    self.seen[e][sem] = val
            if waits:
                self.ops[e].append((waits, None, None, False))
        if clear:
            self.st = {k: v for k, v in self.st.items() if k[0] in ("w", "wh")}

    def final_wait(self, eng, events):
        waits = [(s, v) for s, v in events]
        self.ops[eng].append((waits, None, None, False))

    def emit(self, nc, sems):
        with nc.Block() as block:
            decos = {"pe": block.tensor, "act": block.scalar, "dve": block.vector,
                     "pool": block.gpsimd, "sp": block.sync}
            for e in ENGS:
                ops = self.ops[e]

                def body(engine, ops=ops):
                    for waits, fn, ev, attach in ops:
                        if fn is None:
                            for s, v in waits:
                                engine.wait_ge(sems[s], v)
                            continue
                        rest = waits
                        first = None
                        if attach and waits:
                            first = waits[0]
                            rest = waits[1:]
                        for s, v in rest:
                            engine.wait_ge(sems[s], v)
                        ins = fn(engine)
                        if first is not None:
                            ins._wait_ge(sems[first[0]], first[1])
                        ins.then_inc(sems[ev[0]], 1 if ev[0] in ENGS else 16)

                decos[e](body)


class Mem:
    def __init__(self, nc, limit):
        self.nc, self.off, self.limit, self.cnt = nc, 16512, limit, 0

    def alloc(self, name, free_shape, dtype):
        nbytes = int(np.prod(free_shape)) * (4 if dtype == F32 else 2)
        off = (self.off + 31) // 32 * 32
        assert off + nbytes <= self.limit, f"SBUF overflow for {name}: {off}+{nbytes} > {self.limit}"
        self.cnt += 1
        t = self.nc.alloc_sbuf_tensor_at(f"{name}_{self.cnt}", [128] + list(free_shape), dtype, offset=off)
        self.off = off + nbytes
        return t


class Ring:
    def __init__(self, mem, name, n, free_shape, dtype):
        self.t = [mem.alloc(f"{name}{i}", free_shape, dtype) for i in range(n)]
        self.name, self.i = name, 0

    def next(self):
        j = self.i % len(self.t)
        self.i += 1
        return self.t[j], (self.name, j)


DRAM_INPUTS = [
    ("x", [NB, S, D], F32),
    ("even_norm", [2, 128, 8], F32), ("even_w_in", [2, 1024, 4608], F32),
    ("fourier_w", [2, 4, 256, 256], F32), ("q_gain", [2, 128, 1], F32), ("k_gain", [2, 128, 1], F32),
    ("even_w_out", [2, 2048, 1024], F32),
    ("odd_norm", [2, 128, 8], F32), ("odd_w_in", [2, 1024, 4096], F32),
    ("conv_w", [2, 128, 16, 4], F32), ("conv_b", [2, 128, 16], F32),
    ("gate_w", [2, 16, 128, 4, 128], F32), ("gate_a_b", [2, 128, 2, 16], F32),
    ("gate_x_b", [2, 128, 2, 16], F32),
    ("rglru_lambda", [2, 128, 2, 16], F32), ("odd_w_out", [2, 2048, 1024], F32),
    ("final_norm", [1024], F32),
    ("c_ident", [128, 128], F32), ("c_perm", [128, 128], F32),
    ("c_cos", [128, S], F32), ("c_sin", [128, S], F32),
    ("c_dftc", [S, S], BF16), ("c_dfts", [S, S], BF16),
    ("c_cc", [256, 256], BF16), ("c_sc", [256, 256], BF16),
]


class Builder:
    def __init__(self, layers, do_final):
        self.layers = layers
        self.do_final = do_final
        nc = self.nc = bass.Bass("TRN2", target_bir_lowering=False)
        self.d = {}
        for name, shape, dt in DRAM_INPUTS:
            self.d[name] = nc.dram_tensor(name, shape, dt, kind="ExternalInput").ap()
        self.y = nc.dram_tensor("y", [NB, S, D], F32, kind="ExternalOutput").ap()
        self.c = Sched()
        limit = 229376
        self.mem = m = Mem(nc, limit)
        self.x = m.alloc("x", [8, S], F32)
        self.h = m.alloc("h", [8, S], BF16)
        self.ident_f = m.alloc("identf", [128], F32)
        self.ident_b = m.alloc("identb", [128], BF16)
        self.ones_b = m.alloc("onesb", [128], BF16)
        self.perm_b = m.alloc("permb", [128], BF16)
        self.gcol = m.alloc("gcol", [8], F32)
        self.qg = m.alloc("qg", [1], F32)
        self.kg = m.alloc("kg", [1], F32)
        self.cw = m.alloc("cw", [16, 4], F32)
        self.cb = m.alloc("cb", [16], F32)
        self.hba = m.alloc("hba", [2, 16], F32)
        self.hbx = m.alloc("hbx", [2, 16], F32)
        self.lam = m.alloc("lam", [2, 16], F32)
        self.cc1 = m.alloc("cc1", [2, 16], F32)
        self.cch = m.alloc("cch", [2, 16], F32)
        self.wslots = [m.alloc(f"w{i}", [2048], BF16) for i in range(3)]
        self.wi = 0
        self.arena0 = m.off
        self.psw = [nc.alloc_psum_tensor(f"psw{i}", [128, 1024], F32) for i in range(4)]
        self.ps = [self.psw[i // 2][:, (i % 2) * 512:(i % 2 + 1) * 512] for i in range(8)]
        self.psi = 0
        self.out_events = []
        self.build()
        with ExitStack() as es:
            sems = {}
            for e in ENGS:
                sems[e] = es.enter_context(nc.semaphore(f"s_{e}"))
            for dn in self.c.dcnt:
                sems[dn] = es.enter_context(nc.semaphore(f"d_{dn}"))
            self.c.emit(nc, sems)

    def psum(self):
        i = self.psi % NPS
        self.psi += 1
        return self.ps[i], ("ps", i)

    def arena_reset(self, full=False):
        self.c.barrier(engines=ENGS if full else ("pe", "act", "dve", "sp"), clear=True)
        self.mem.off = self.arena0

    def wring(self, n):
        pass

    def wload(self, dram_view, nk, cols):
        j = self.wi % len(self.wslots)
        self.wi += 1
        t = self.wslots[j]
        dst = t[:, 0:nk * cols].rearrange("p (k c) -> p k c", k=nk)
        self.c.op("pool", lambda e: e.dma_start(out=dst, in_=dram_view), writes=[("w", j)], dsem=f"w{j}")
        return dst, ("w", j)

    def dma_in(self, out_ap, in_ap, key, dsem, eng="sp"):
        self.c.op(eng, lambda e: e.dma_start(out=out_ap, in_=in_ap), writes=[key], dsem=dsem)

    def xk(self, kc, tb):
        return ("x", kc, tb)

    def hk(self, kc, tb):
        return ("h", kc, tb)

    def build(self):
        c = self.c
        self.dma_in(self.ident_f[:, :], self.d["c_ident"][:, :], ("identf",), "c0")
        self.dma_in(self.ident_b[:, :], self.d["c_ident"][:, :], ("identb",), "c1", eng="pool")
        self.dma_in(self.perm_b[:, :], self.d["c_perm"][:, :], ("permb",), "c2", eng="pool")
        ones_b = self.ones_b
        c.op("dve", lambda e: e.memset(ones_b[:, :], 1.0), writes=[("onesb",)])
        c.barrier(clear=False)
        for b in range(NB):
            self.phase_load_x(b)
            for L in self.layers:
                if L % 2 == 0:
                    self.even_layer(L // 2)
                else:
                    self.odd_layer(L // 2)
            self.phase_store(b)
        c.barrier(engines=("sp",))
        c.final_wait("sp", self.out_events)

    def phase_load_x(self, b):
        c = self.c
        self.arena_reset()
        stage = Ring(self.mem, "xin", 2, [D], F32)
        x, ident = self.x, self.ident_f
        for tt in range(NTT):
            st, sk = stage.next()
            self.dma_in(st[:, :], self.d["x"][b, tt * 128:(tt + 1) * 128, :], sk, f"xin{sk[1]}")
            for half in range(2):
                ps, pk = self.psum()
                for j in range(4):
                    kc = half * 4 + j
                    c.op("pe", lambda e, ps=ps, st=st, j=j, kc=kc: e.transpose(
                        ps[:, j * 128:(j + 1) * 128], st[:, kc * 128:(kc + 1) * 128], ident[:, :]),
                        reads=[sk, ("identf",)], writes=[pk])
                dst = x[:, half * 4:half * 4 + 4, tt * 128:(tt + 1) * 128]
                src = ps[:, :].rearrange("p (a b) -> p a b", a=4)
                wk = [self.xk(half * 4 + j, tt // 4) for j in range(4)]
                if half == 0:
                    c.op("act", lambda e, dst=dst, src=src: e.activation(out=dst, in_=src, func=AF.Copy),
                         reads=[pk], writes=wk)
                else:
                    c.op("dve", lambda e, dst=dst, src=src: e.tensor_copy(out=dst, in_=src),
                         reads=[pk], writes=wk)

    def phase_store(self, b):
        c = self.c
        self.arena_reset()
        m = self.mem
        stage = Ring(m, "ot", 2, [D], F32)
        x, ident = self.x, self.ident_f
        if self.do_final:
            gfin = m.alloc("gfin", [D], F32)
            junk = m.alloc("junk", [D], BF16)
            ssr = Ring(m, "ss", 2, [1], F32)
            sdr = Ring(m, "sd", 2, [1], F32)
            rsr = Ring(m, "rs", 2, [1], F32)
            self.dma_in(gfin[:, :], self.d["final_norm"].partition_broadcast(128), ("gfin",), "gfin")
        for tt in range(16):
            ot, ok = stage.next()
            for half in range(2):
                ps, pk = self.psum()
                for j in range(4):
                    kc = half * 4 + j
                    c.op("pe", lambda e, ps=ps, j=j, kc=kc, tt=tt: e.transpose(
                        ps[:, j * 128:(j + 1) * 128], x[:, kc, tt * 128:(tt + 1) * 128], ident[:, :]),
                        reads=[self.xk(kc, tt // 4), ("identf",)], writes=[pk])
                dst = ot[:, half * 512:(half + 1) * 512]
                if half == 0 and "noactcopy" not in DBG:
                    c.op("act", lambda e, dst=dst, ps=ps: e.activation(out=dst, in_=ps[:, :], func=AF.Copy),
                         reads=[pk], writes=[ok])
                else:
                    c.op("dve", lambda e, dst=dst, ps=ps: e.tensor_copy(out=dst, in_=ps[:, :]),
                         reads=[pk], writes=[ok])
            if self.do_final:
                ss, ssk = ssr.next()
                sd, sdk = sdr.next()
                rs, rsk = rsr.next()
                c.op("act", lambda e, ot=ot, ss=ss: e.activation(out=junk[:, :], in_=ot[:, :], func=AF.Square,
                                                                accum_out=ss[:, 0:1]),
                     reads=[ok], writes=[("junk",), ssk], attach=False)
                c.op("act", lambda e, sd=sd, ss=ss: e.activation(out=sd[:, :], in_=ss[:, :], func=AF.Sqrt,
                                                                scale=1.0 / D, bias=EPS),
                     reads=[ssk], writes=[sdk])
                c.op("dve", lambda e, rs=rs, sd=sd: e.reciprocal(out=rs[:, :], in_=sd[:, :]),
                     reads=[sdk], writes=[rsk])
                c.op("dve", lambda e, ot=ot, rs=rs: e.scalar_tensor_tensor(
                    out=ot[:, :], in0=ot[:, :], scalar=rs[:, 0:1], in1=gfin[:, :], op0=ALU.mult, op1=ALU.mult),
                    reads=[ok, rsk, ("gfin",)], writes=[ok])
            ev = c.op("sp", lambda e, ot=ot, tt=tt, b=b: e.dma_start(out=self.y[b, tt * 128:(tt + 1) * 128, :], in_=ot[:, :]),
                      reads=[ok], dsem=f"out{ok[1]}")
            self.out_events = [x_ for x_ in self.out_events if x_[0] != ev[0]] + [ev]

    def phase_norm(self, gsrc):
        c = self.c
        self.arena_reset()
        m = self.mem
        x, h, gcol, ones = self.x, self.h, self.gcol, self.ones_b
        self.dma_in(gcol[:, :], gsrc, ("gcol",), "gcol")
        sqr = Ring(m, "sq", 4, [512], BF16)
        sdr = Ring(m, "sd", 2, [512], F32)
        rsr = Ring(m, "rs", 2, [512], F32)
        for tb in range(4):
            ts = slice(tb * 512, (tb + 1) * 512)
            ps, pk = self.psum()
            for kc in range(8):
                sq, sqk = sqr.next()
                c.op("act", lambda e, sq=sq, kc=kc, ts=ts: e.activation(out=sq[:, :], in_=x[:, kc, ts], func=AF.Square),
                     reads=[self.xk(kc, tb)], writes=[sqk])
                c.op("pe", lambda e, ps=ps, sq=sq, kc=kc: e.matmul(ps[:, :], lhsT=ones[:, :], rhs=sq[:, :],
                                                                   start=(kc == 0), stop=(kc == 7)),
                     reads=[sqk, ("onesb",)], writes=[pk])
            sd, sdk = sdr.next()
            rs, rsk = rsr.next()
            c.op("act", lambda e, sd=sd, ps=ps: e.activation(out=sd[:, :], in_=ps[:, :], func=AF.Ln,
                                                            scale=1.0 / D, bias=EPS),
                 reads=[pk], writes=[sdk])
            c.op("act", lambda e, rs=rs, sd=sd: e.activation(out=rs[:, :], in_=sd[:, :], func=AF.Exp, scale=-0.5),
                 reads=[sdk], writes=[rsk])
            for kc in range(8):
                c.op("dve", lambda e, kc=kc, ts=ts, rs=rs: e.scalar_tensor_tensor(
                    out=h[:, kc, ts], in0=x[:, kc, ts], scalar=gcol[:, kc:kc + 1], in1=rs[:, :],
                    op0=ALU.mult, op1=ALU.mult),
                    reads=[self.xk(kc, tb), rsk, ("gcol",)], writes=[self.hk(kc, tb)])

    def proj_fm(self, w, wk, col0, tb, ps, pk):
        h = self.h
        ts = slice(tb * 512, (tb + 1) * 512)
        for kc in range(8):
            self.c.op("pe", lambda e, kc=kc: e.matmul(ps[:, :], lhsT=w[:, kc, col0:col0 + 128], rhs=h[:, kc, ts],
                                                      start=(kc == 0), stop=(kc == 7)),
                      reads=[wk, self.hk(kc, tb)], writes=[pk])

    def wout_partial(self, wsrc, r0, z, zkey):
        c = self.c
        x = self.x
        for ch in range(2):
            view = wsrc[r0:r0 + 512, ch * 512:(ch + 1) * 512].rearrange("(k p) c -> p k c", p=128)
            w, wk = self.wload(view, 4, 512)
            for nl in range(4):
                n = ch * 4 + nl
                for tb in range(4):
                    ts = slice(tb * 512, (tb + 1) * 512)
                    ps, pk = self.psum()
                    for ec in range(4):
                        c.op("pe", lambda e, ps=ps, w=w, ec=ec, nl=nl, ts=ts: e.matmul(
                            ps[:, :], lhsT=w[:, ec, nl * 128:(nl + 1) * 128], rhs=z[:, ec, ts],
                            start=(ec == 0), stop=(ec == 3)),
                            reads=[wk, zkey(ec, tb)], writes=[pk])
                    c.op("dve", lambda e, ps=ps, n=n, ts=ts: e.tensor_tensor(
                        out=x[:, n, ts], in0=ps[:, :], in1=x[:, n, ts], op=ALU.add),
                        reads=[pk, self.xk(n, tb)], writes=[self.xk(n, tb)])

    def gate_silu(self, wsrc_cols, z, zkey):
        c = self.c
        for u in range(2):
            w, wk = self.wload(wsrc_cols(u), 8, 256)
            for el in range(2):
                ec = u * 2 + el
                for tb in range(4):
                    ts = slice(tb * 512, (tb + 1) * 512)
                    ps, pk = self.psum()
                    self.proj_fm(w, wk, el * 128, tb, ps, pk)
                    c.op("act", lambda e, ps=ps, ec=ec, ts=ts: e.activation(out=z[:, ec, ts], in_=ps[:, :], func=AF.Silu),
                         reads=[pk], writes=[zkey(ec, tb)])

    def even_layer(self, i):
        d = self.d
        self.phase_norm(d["even_norm"][i, :, :])
        for p in range(2):
            self.phase_fourier(i, p)
        for g in range(2):
            self.phase_attn(i, g)

    def phase_fourier(self, i, p):
        c = self.c
        d = self.d
        self.arena_reset()
        m = self.mem
        h = self.h
        z = m.alloc("z", [4, S], BF16)
        AB = m.alloc("AB", [16, 1024], BF16)
        tabC = [m.alloc(f"tabC{j}", [16, 256], BF16) for j in range(2)]
        tabS = [m.alloc(f"tabS{j}", [16, 256], BF16) for j in range(2)]
        finr = Ring(m, "fin", 2, [2, 512], BF16)
        MAB = m.alloc("MAB", [2, 2, 512], BF16)
        ccb = m.alloc("ccb", [2, 256], BF16)
        scb = m.alloc("scb", [2, 256], BF16)
        self.dma_in(ccb[:, :, :], d["c_cc"].rearrange("(k p) c -> p k c", p=128), ("ccb",), "ccb")
        self.dma_in(scb[:, :, :], d["c_sc"].rearrange("(k p) c -> p k c", p=128), ("scb",), "scb")
        self.wring(3)
        zkey = lambda ec, tb: ("z", ec, tb)
        win = d["even_w_in"]
        for gl in range(2):
            g = 2 * p + gl
            fw, fwk = self.wload(d["fourier_w"][i, g, :, :].rearrange("(k p) c -> p k c", p=128), 2, 256)
            for mb in range(2):
                ps, pk = self.psum()
                for which, src, sk in ((0, ccb, ("ccb",)), (1, scb, ("scb",))):
                    for kc in range(2):
                        c.op("pe", lambda e, ps=ps, src=src, kc=kc, mb=mb, fw=fw, which=which: e.matmul(
                            ps[:, which * 256:(which + 1) * 256], lhsT=src[:, kc, mb * 128:(mb + 1) * 128],
                            rhs=fw[:, kc, :], start=(kc == 0), stop=(kc == 1)),
                            reads=[sk, fwk], writes=[pk])
                c.op("act", lambda e, ps=ps, gl=gl, mb=mb: e.activation(out=MAB[:, gl, mb, :], in_=ps[:, :], func=AF.Copy),
                     reads=[pk], writes=[("MAB", gl)])
        self.gate_silu(lambda u: win[i, :, 1024 + p * 512 + u * 256: 1024 + p * 512 + (u + 1) * 256]
                       .rearrange("(k p) c -> p k c", p=128), z, zkey)
        for gl in range(2):
            g = 2 * p + gl
            w, wk = self.wload(win[i, :, g * 256:(g + 1) * 256].rearrange("(k p) c -> p k c", p=128), 8, 256)
            for tb in range(4):
                fin, fk = finr.next()
                for half in range(2):
                    ps, pk = self.psum()
                    self.proj_fm(w, wk, half * 128, tb, ps, pk)
                    if half == 0:
                        c.op("act", lambda e, ps=ps, fin=fin: e.activation(out=fin[:, 0, :], in_=ps[:, :], func=AF.Copy),
                             reads=[pk], writes=[fk])
                    else:
                        c.op("dve", lambda e, ps=ps, fin=fin: e.tensor_copy(out=fin[:, 1, :], in_=ps[:, :]),
                             reads=[pk], writes=[fk])
                for tq in range(4):
                    tt = tb * 4 + tq
                    ps, pk = self.psum()
                    for half in range(2):
                        c.op("pe", lambda e, ps=ps, fin=fin, half=half, tq=tq, gl=gl: e.matmul(
                            ps[:, :], lhsT=fin[:, half, tq * 128:(tq + 1) * 128], rhs=MAB[:, gl, half, :],
                            start=(half == 0), stop=(half == 1)),
                            reads=[fk, ("MAB", gl)], writes=[pk])
                    dst = AB[:, tt, gl * 512:(gl + 1) * 512]
                    if tq % 2 == 0:
                        c.op("act", lambda e, ps=ps, dst=dst: e.activation(out=dst, in_=ps[:, :], func=AF.Copy),
                             reads=[pk], writes=[("AB", tt, gl)])
                    else:
                        c.op("dve", lambda e, ps=ps, dst=dst: e.tensor_copy(out=dst, in_=ps[:, :]),
                             reads=[pk], writes=[("AB", tt, gl)])
        for sb in range(8):
            tC, tS = tabC[sb % 2], tabS[sb % 2]
            kC, kS = ("tabC", sb % 2), ("tabS", sb % 2)
            self.dma_in(tC[:, :, :], d["c_dftc"][:, sb * 256:(sb + 1) * 256].rearrange("(j p) c -> p j c", p=128),
                        kC, f"tabC{sb % 2}")
            self.dma_in(tS[:, :, :], d["c_dfts"][:, sb * 256:(sb + 1) * 256].rearrange("(j p) c -> p j c", p=128),
                        kS, f"tabS{sb % 2}")
            ss = slice(sb * 256, (sb + 1) * 256)
            for ec in range(4):
                gl, half = ec // 2, ec % 2
                ps, pk = self.psum()
                for which, tab, tk in ((0, tC, kC), (1, tS, kS)):
                    c0 = gl * 512 + which * 256 + half * 128
                    for j in range(16):
                        c.op("pe", lambda e, ps=ps, tab=tab, j=j, c0=c0, which=which: e.matmul(
                            ps[:, 0:256], lhsT=AB[:, j, c0:c0 + 128], rhs=tab[:, j, :],
                            start=(which == 0 and j == 0), stop=(which == 1 and j == 15)),
                            reads=[("AB", j, gl), tk], writes=[pk])
                c.op("dve", lambda e, ps=ps, ec=ec, ss=ss: e.tensor_tensor(
                    out=z[:, ec, ss], in0=ps[:, 0:256], in1=z[:, ec, ss], op=ALU.mult),
                    reads=[pk, zkey(ec, sb // 2)], writes=[zkey(ec, sb // 2)])
        self.wout_partial(d["even_w_out"][i], p * 512, z, zkey)

    def normrope(self, ps, pk, gc, gk, out_ap, out_key, tb, R):
        c = self.c
        ts = slice(tb * 512, (tb + 1) * 512)
        ones, perm = self.ones_b, self.perm_b
        cos, sin = R["cos"], R["sin"]
        sq, sqk = R["sq"].next()
        qb, qbk = R["qb"].next()
        t1, t1k = R["t1"].next()
        t2, t2k = R["t2"].next()
        sd, sdk = R["sd"].next()
        rs, rsk = R["rs"].next()
        c.op("act", lambda e: e.activation(out=sq[:, :], in_=ps[:, :], func=AF.Square), reads=[pk], writes=[sqk])
        c.op("act", lambda e: e.activation(out=qb[:, :], in_=ps[:, :], func=AF.Identity, scale=gc[:, 0:1]),
             reads=[pk, gk], writes=[qbk])
        ps2, pk2 = self.psum()
        ps3, pk3 = self.psum()
        c.op("pe", lambda e: e.matmul(ps2[:, :], lhsT=ones[:, :], rhs=sq[:, :], start=True, stop=True),
             reads=[sqk, ("onesb",)], writes=[pk2])
        c.op("pe", lambda e: e.matmul(ps3[:, :], lhsT=perm[:, :], rhs=qb[:, :], start=True, stop=True),
             reads=[qbk, ("permb",)], writes=[pk3])
        c.op("dve", lambda e: e.scalar_tensor_tensor(out=t1[:, :], in0=ps[:, :], scalar=gc[:, 0:1], in1=cos[:, ts],
                                                     op0=ALU.mult, op1=ALU.mult),
             reads=[pk, gk, ("cos",)], writes=[t1k])
        c.op("act", lambda e: e.activation(out=sd[:, :], in_=ps2[:, :], func=AF.Ln, scale=1.0 / 128, bias=EPS),
             reads=[pk2], writes=[sdk])
        c.op("act", lambda e: e.activation(out=rs[:, :], in_=sd[:, :], func=AF.Exp, scale=-0.5),
             reads=[sdk], writes=[rsk])
        c.op("dve", lambda e: e.tensor_tensor(out=t2[:, :], in0=ps3[:, :], in1=sin[:, ts], op=ALU.mult),
             reads=[pk3, ("sin",)], writes=[t2k])
        c.op("pool", lambda e: e.tensor_tensor(out=t1[:, :], in0=t1[:, :], in1=t2[:, :], op=ALU.add),
             reads=[t1k, t2k], writes=[t1k])
        c.op("dve", lambda e: e.tensor_tensor(out=out_ap, in0=t1[:, :], in1=rs[:, :], op=ALU.mult),
             reads=[t1k, rsk], writes=[out_key])

    def phase_attn(self, i, g):
        c = self.c
        d = self.d
        self.arena_reset()
        m = self.mem
        h = self.h
        z = m.alloc("z", [4, S], BF16)
        q = m.alloc("q", [4, S], BF16)
        k = m.alloc("k", [S], BF16)
        V = m.alloc("V", [16, 128], BF16)
        cos = m.alloc("cos", [S], F32)
        sin = m.alloc("sin", [S], F32)
        rdr = Ring(m, "rden", 2, [512], F32)
        ogr = Ring(m, "og", 2, [512], F32)
        off_R = m.off
        R = {"cos": cos, "sin": sin,
             "sq": Ring(m, "sq", 2, [512], BF16), "qb": Ring(m, "qb", 2, [512], BF16),
             "t1": Ring(m, "t1", 2, [512], F32), "t2": Ring(m, "t2", 2, [512], F32),
             "sd": Ring(m, "sd", 2, [512], F32), "rs": Ring(m, "rs", 2, [512], F32)}
        self.wring(3)
        zkey = lambda ec, tb: ("z", ec, tb)
        win = d["even_w_in"]
        qg, kg = self.qg, self.kg
        self.dma_in(cos[:, :], d["c_cos"][:, :], ("cos",), "cos")
        self.dma_in(sin[:, :], d["c_sin"][:, :], ("sin",), "sin")
        self.dma_in(qg[:, :], d["q_gain"][i, :, :], ("qg",), "qg")
        self.dma_in(kg[:, :], d["k_gain"][i, :, :], ("kg",), "kg")
        a0 = 3584 + g * 512
        self.gate_silu(lambda u: win[i, :, a0 + u * 256: a0 + (u + 1) * 256].rearrange("(k p) c -> p k c", p=128),
                       z, zkey)
        w, wk = self.wload(win[i, :, 3072:3328].rearrange("(k p) c -> p k c", p=128), 8, 256)
        for tb in range(4):
            ps, pk = self.psum()
            self.proj_fm(w, wk, g * 128, tb, ps, pk)
            self.normrope(ps, pk, kg, ("kg",), k[:, tb * 512:(tb + 1) * 512], ("k", tb), tb, R)
        w, wk = self.wload(win[i, :, 3328:3584].rearrange("(k p) c -> p k c", p=128), 8, 256)
        for tt in range(16):
            ps, pk = self.psum()
            for kc in range(8):
                c.op("pe", lambda e, ps=ps, kc=kc, tt=tt, w=w: e.matmul(
                    ps[:, 0:128], lhsT=h[:, kc, tt * 128:(tt + 1) * 128], rhs=w[:, kc, g * 128:(g + 1) * 128],
                    start=(kc == 0), stop=(kc == 7)),
                    reads=[wk, self.hk(kc, tt // 4)], writes=[pk])
            if tt % 2 == 0:
                c.op("act", lambda e, ps=ps, tt=tt: e.activation(out=V[:, tt, :], in_=ps[:, 0:128], func=AF.Copy),
                     reads=[pk], writes=[("V", tt)])
            else:
                c.op("dve", lambda e, ps=ps, tt=tt: e.tensor_copy(out=V[:, tt, :], in_=ps[:, 0:128]),
                     reads=[pk], writes=[("V", tt)])
        q0 = 2048 + g * 512
        for u in range(2):
            w, wk = self.wload(win[i, :, q0 + u * 256: q0 + (u + 1) * 256].rearrange("(k p) c -> p k c", p=128), 8, 256)
            for el in range(2):
                hl = u * 2 + el
                for tb in range(4):
                    ps, pk = self.psum()
                    self.proj_fm(w, wk, el * 128, tb, ps, pk)
                    self.normrope(ps, pk, qg, ("qg",), q[:, hl, tb * 512:(tb + 1) * 512], ("q", hl, tb), tb, R)
        c.barrier(engines=("pe", "act", "dve", "pool"))
        m.off = off_R
        pTr = Ring(m, "pT", 4, [1024], BF16)
        qsr = Ring(m, "qsum", 12, [512], BF16)

        def finalize(fin):
            quads, o_ps, ok_, d_ps, dk_, hl, qb, qs = fin
            for qj, qsum, qsk in quads:
                c.op("pe", lambda e, qj=qj, qsum=qsum: e.matmul(
                    d_ps[:, :], lhsT=ones[:, :], rhs=qsum[:, :], start=(qj == 0), stop=(qj == 7)),
                    reads=[("onesb",), qsk], writes=[dk_])
            rden, rdk = rdr.next()
            og, ogk = ogr.next()
            c.op("act", lambda e: e.activation(out=rden[:, :], in_=d_ps[:, :], func=AF.Ln), reads=[dk_], writes=[rdk])
            c.op("act", lambda e: e.activation(out=rden[:, :], in_=rden[:, :], func=AF.Exp, scale=-1.0),
                 reads=[rdk], writes=[rdk])
            c.op("dve", lambda e: e.tensor_tensor(out=og[:, :], in0=o_ps[:, :], in1=rden[:, :], op=ALU.mult),
                 reads=[ok_, rdk], writes=[ogk])
            c.op("dve", lambda e: e.tensor_tensor(out=z[:, hl, qs], in0=og[:, :], in1=z[:, hl, qs], op=ALU.mult),
                 reads=[ogk, zkey(hl, qb)], writes=[zkey(hl, qb)])

        deferred = None
        ones = self.ones_b
        it = 0
        wcnt = 0
        for hl in range(4):
            for qb in range(4):
                qs = slice(qb * 512, (qb + 1) * 512)
                ob, db = (4, 5) if it % 2 == 0 else (6, 7)
                it += 1
                o_ps, ok_ = self.ps[ob], ("ps", ob)
                d_ps, dk_ = self.ps[db], ("ps", db)
                pend = None
                prevw = None
                quads = []
                for kp in range(9):
                    if kp < 8:
                        wi_ = wcnt % 2
                        wcnt += 1
                        wide = self.psw[wi_]
                        wkeys = [("ps", 2 * wi_), ("ps", 2 * wi_ + 1)]
                        for hh in range(2):
                            kt = 2 * kp + hh
                            c.op("pe", lambda e, wide=wide, kt=kt, hh=hh, hl=hl, qs=qs: e.matmul(
                                wide[:, hh * 512:(hh + 1) * 512], lhsT=k[:, kt * 128:(kt + 1) * 128], rhs=q[:, hl, qs],
                                start=True, stop=True),
                                reads=[("k", kt // 4), ("q", hl, qb)], writes=[wkeys[hh]])
                        pT, pTk = pTr.next()
                        c.op("act", lambda e, pT=pT, wide=wide: e.activation(out=pT[:, :], in_=wide[:, :], func=AF.Exp,
                                                                            scale=ATT_SCALE),
                             reads=wkeys, writes=[pTk])
                        qsum, qsk = qsr.next()
                        c.op("dve", lambda e, qsum=qsum, pT=pT: e.tensor_tensor(out=qsum[:, :], in0=pT[:, 0:512], in1=pT[:, 512:1024], op=ALU.add),
                             reads=[pTk], writes=[qsk])
                        quads.append((kp, qsum, qsk))
                    if pend is not None:
                        pkp, ppT, ppTk = pend
                        for hh in range(2):
                            pkt = 2 * pkp + hh
                            c.op("pe", lambda e, pkt=pkt, hh=hh, ppT=ppT, o_ps=o_ps: e.matmul(
                                o_ps[:, :], lhsT=V[:, pkt, :], rhs=ppT[:, hh * 512:(hh + 1) * 512],
                                start=(pkt == 0), stop=(pkt == 15)),
                                reads=[("V", pkt), ppTk], writes=[ok_])
                    if kp == 2 and deferred is not None:
                        finalize(deferred)
                        deferred = None
                    pend = (kp, pT, pTk) if kp < 8 else None
                deferred = (quads, o_ps, ok_, d_ps, dk_, hl, qb, qs)
        finalize(deferred)
        self.psi = 0
        c.barrier(engines=("pe", "act", "dve"))
        self.wout_partial(d["even_w_out"][i], 1024 + g * 512, z, zkey)

    def odd_layer(self, i):
        c = self.c
        d = self.d
        self.phase_norm(d["odd_norm"][i, :, :])
        cw, cb, hba, hbx, lam, cc1, cch = self.cw, self.cb, self.hba, self.hbx, self.lam, self.cc1, self.cch
        self.dma_in(cw[:, :, :], d["conv_w"][i], ("cw",), "cw")
        self.dma_in(cb[:, :], d["conv_b"][i], ("cb",), "cb")
        self.dma_in(hba[:, :, :], d["gate_a_b"][i], ("hba",), "hba")
        self.dma_in(hbx[:, :, :], d["gate_x_b"][i], ("hbx",), "hbx")
        self.dma_in(lam[:, :, :], d["rglru_lambda"][i], ("lam",), "lam")
        c.op("dve", lambda e: e.tensor_scalar(out=hba[:, :, :], in0=hba[:, :, :], scalar1=0.5, scalar2=None, op0=ALU.mult),
             reads=[("hba",)], writes=[("hba",)])
        c.op("dve", lambda e: e.tensor_scalar(out=hbx[:, :, :], in0=hbx[:, :, :], scalar1=0.5, scalar2=None, op0=ALU.mult),
             reads=[("hbx",)], writes=[("hbx",)])
        c.op("act", lambda e: e.activation(out=cc1[:, :, :], in_=lam[:, :, :], func=AF.Exp, scale=-1.0),
             reads=[("lam",)], writes=[("cc1",)])
        c.op("act", lambda e: e.activation(out=cc1[:, :, :], in_=cc1[:, :, :], func=AF.Ln, bias=1.0),
             reads=[("cc1",)], writes=[("cc1",)])
        c.op("dve", lambda e: e.tensor_scalar(out=cch[:, :, :], in0=cc1[:, :, :], scalar1=-4.0, scalar2=None, op0=ALU.mult),
             reads=[("cc1",)], writes=[("cch",)])
        c.op("dve", lambda e: e.tensor_scalar(out=cc1[:, :, :], in0=cc1[:, :, :], scalar1=-8.0, scalar2=None, op0=ALU.mult),
             reads=[("cc1",), ("cch",)], writes=[("cc1",)])
        self.phase_rglru_all(i)

    def phase_rglru_all(self, i):
        c = self.c
        d = self.d
        self.arena_reset(full=True)
        m = self.mem
        z = m.alloc("z", [2, S], BF16)
        xrb = m.alloc("xrb", [S + 16], BF16)
        sgs = [m.alloc(f"sg{j}", [S], BF16) for j in range(2)]
        xc = m.alloc("xc", [S], F32)
        xcb = m.alloc("xcb", [S], BF16)
        aa = [m.alloc(f"a{j}", [S], F32) for j in range(2)]
        uu = [m.alloc(f"u{j}", [S], F32) for j in range(2)]
        ixc = [m.alloc(f"ixc{j}", [S], F32) for j in range(2)]
        trr = Ring(m, "tr", 3, [512], F32)
        tir = Ring(m, "ti", 4, [512], F32)
        diag = m.alloc("diag", [4, 128], BF16)
        gwb = [m.alloc(f"gwb{j}", [4, 128], BF16) for j in range(2)]
        win = d["odd_w_in"]
        cw, cb, hba, hbx, cc1, cch = self.cw, self.cb, self.hba, self.hbx, self.cc1, self.cch
        identb = self.ident_b
        x = self.x
        c.op("dve", lambda e: e.memset(xrb[:, 0:2], 0.0), writes=[("xrb", 0)])
        c.op("dve", lambda e: e.memset(xrb[:, S + 2:S + 16], 0.0), writes=[("xrb", 3)])
        gws = {}
        st = c.st
        for j in range(3):
            old = st.pop(("w", j), None)
            for hh in range(2):
                if old is not None:
                    rd = dict(old[1])
                    if old[0] is not None and rd.get(old[0][0], 0) < old[0][1]:
                        rd[old[0][0]] = old[0][1]
                    st[("wh", 2 * j + hh)] = [None, rd]
        whi = [0]

        def wh_load(view, nk, cols, nh=1):
            if nh == 2 and whi[0] % 2 == 1:
                whi[0] += 1
            j = whi[0] % 6
            whi[0] += nh
            base = self.wslots[j // 2]
            off = (j % 2) * 1024
            dst = base[:, off:off + nk * cols].rearrange("p (k c) -> p k c", k=nk)
            keys = [("wh", j + t_) for t_ in range(nh)]
            c.op("pool", lambda e: e.dma_start(out=dst, in_=view), writes=keys, dsem=f"wh{j}")
            return dst, keys

        def proj_tile(w, wks, tb):
            ps, pk = self.psum()
            ts = slice(tb * 512, (tb + 1) * 512)
            h = self.h
            for kc in range(8):
                c.op("pe", lambda e, kc=kc: e.matmul(ps[:, :], lhsT=w[:, kc, 0:128], rhs=h[:, kc, ts],
                                                     start=(kc == 0), stop=(kc == 7)),
                     reads=wks + [self.hk(kc, tb)], writes=[pk])
            return ps, pk

        awts = {}

        def stage_a_load(k):
            wx = wh_load(win[i, :, k * 128:(k + 1) * 128].rearrange("(k p) c -> p k c", p=128), 8, 128)
            wg = wh_load(win[i, :, 2048 + k * 128: 2048 + (k + 1) * 128].rearrange("(k p) c -> p k c", p=128), 8, 128)
            awts[k] = (wx, wg)

        def stage_a_tile(k, t):
            (wx, wxk), (wg, wgk) = awts[k]
            tb = t % 4
            ts = slice(tb * 512, (tb + 1) * 512)
            sg = sgs[k % 2]
            if t < 4:
                ps, pk = proj_tile(wx, wxk, tb)
                c.op("dve", lambda e: e.tensor_copy(out=xrb[:, 2 + tb * 512: 2 + (tb + 1) * 512], in_=ps[:, :]),
                     reads=[pk], writes=[("xrb", tb)])
            else:
                ps, pk = proj_tile(wg, wgk, tb)
                tg, tgk = trr.next()
                c.op("act", lambda e: e.activation(out=tg[:, :], in_=ps[:, :], func=AF.Tanh, scale=0.5),
                     reads=[pk], writes=[tgk])
                c.op("dve", lambda e: e.scalar_tensor_tensor(out=sg[:, ts], in0=tg[:, :], scalar=1.0, in1=ps[:, :],
                                                             op0=ALU.add, op1=ALU.mult),
                     reads=[pk, tgk], writes=[("sg", k % 2, tb)])
            if t == 7:
                awts.pop(k)

        def stage_b(k):
            gw = gwb[k % 2]
            gwk = [("gw", k % 2)]
            c.op("pool", lambda e, gw=gw: e.dma_start(out=gw[:, :, :], in_=d["gate_w"][i, k, :, :, :]),
                 writes=gwk, dsem=f"gw{k % 2}")
            gws[k] = (gw, gwk)
            for jt in range(4):
                c.op("dve" if "diag_dve" in DBG else "pool", lambda e, jt=jt: e.tensor_scalar(
                    out=diag[:, jt, :], in0=identb[:, :], scalar1=cw[:, k, jt:jt + 1], scalar2=0.0,
                    op0=ALU.mult, op1=ALU.add),
                    writes=[("diag", jt)])
            for tb in range(4):
                ts = slice(tb * 512, (tb + 1) * 512)
                ps, pk = self.psum()
                rk = [("xrb", t_) for t_ in range(max(0, tb - 1), min(3, tb + 1) + 1)]
                for jt in range(4):
                    c.op("pe", lambda e, ps=ps, jt=jt, tb=tb: e.matmul(
                        ps[:, :], lhsT=diag[:, jt, :], rhs=xrb[:, tb * 512 + jt: tb * 512 + jt + 512],
                        start=(jt == 0), stop=(jt == 3)),
                        reads=[("diag", jt)] + rk, writes=[pk])
                c.op("act", lambda e, ps=ps, ts=ts: e.activation(out=xc[:, ts], in_=ps[:, :], func=AF.Identity,
                                                                 bias=cb[:, k:k + 1]),
                     reads=[pk], writes=[("xc", tb)])
                c.op("dve" if "xcb_dve" in DBG else "pool", lambda e, ts=ts: e.tensor_scalar(out=xcb[:, ts], in0=xc[:, ts], scalar1=1.0, scalar2=0.0,
                                                              op0=ALU.mult, op1=ALU.add),
                     reads=[("xc", tb)], writes=[("xcb", tb)])

        def stage_c_step(k, dr, tb):
            gw, gwk = gws[k]
            a_t, u_t, x_t = aa[dr], uu[dr], ixc[dr]
            ts = slice(tb * 512, (tb + 1) * 512)
            psa, pka = self.psum()
            psx, pkx = self.psum()
            c.op("pe", lambda e: e.matmul(psx[:, :], lhsT=gw[:, 2 + dr, :], rhs=xcb[:, ts], start=True, stop=True),
                 reads=gwk + [("xcb", tb)], writes=[pkx])
            c.op("pe", lambda e: e.matmul(psa[:, :], lhsT=gw[:, dr, :], rhs=xcb[:, ts], start=True, stop=True),
                 reads=gwk + [("xcb", tb)], writes=[pka])
            tr, trk = trr.next()
            ti, tik = tir.next()
            c.op("act", lambda e: e.activation(out=ti[:, :], in_=psx[:, :], func=AF.Tanh, scale=0.5,
                                               bias=hbx[:, dr, k:k + 1]),
                 reads=[pkx], writes=[tik])
            c.op("act", lambda e: e.activation(out=tr[:, :], in_=psa[:, :], func=AF.Tanh, scale=0.5,
                                               bias=hba[:, dr, k:k + 1]),
                 reads=[pka], writes=[trk])
            c.op("act", lambda e: e.activation(out=a_t[:, ts], in_=tr[:, :], func=AF.Exp, scale=cch[:, dr, k:k + 1],
                                               bias=cch[:, dr, k:k + 1]),
                 reads=[trk], writes=[("a", dr, tb)])
            c.op("act", lambda e: e.activation(out=u_t[:, ts], in_=tr[:, :], func=AF.Exp, scale=cc1[:, dr, k:k + 1],
                                               bias=cc1[:, dr, k:k + 1]),
                 reads=[trk], writes=[("u", dr, tb)])
            c.op("dve" if "i_dve" in DBG else "pool", lambda e: e.tensor_scalar(out=ti[:, :], in0=ti[:, :], scalar1=0.5, scalar2=0.5,
                                                   op0=ALU.mult, op1=ALU.add),
                 reads=[tik], writes=[tik])
            c.op("dve" if "i_dve" in DBG else "pool", lambda e: e.tensor_tensor(out=x_t[:, ts], in0=ti[:, :], in1=xc[:, ts], op=ALU.mult),
                 reads=[tik, ("xc", tb)], writes=[("ixc", dr, tb)])
            if dr == 1 and tb == 0:
                gws.pop(k)

        ks = lambda nm, dr: [(nm, dr, tb) for tb in range(4)]

        def d_sqrt(dr):
            u_t = uu[dr]
            c.op("act", lambda e: e.activation(out=u_t[:, :], in_=u_t[:, :], func=AF.Sqrt, scale=-1.0, bias=1.0),
                 reads=ks("u", dr), writes=ks("u", dr))

        def u_piece(dr, tb):
            u_t, x_t = uu[dr], ixc[dr]
            lo, hi = tb * 512, (tb + 1) * 512
            c.op("pool", lambda e: e.tensor_tensor(out=u_t[:, lo:hi], in0=u_t[:, lo:hi], in1=x_t[:, lo:hi], op=ALU.mult),
                 reads=[("u", dr, tb), ("ixc", dr, tb)], writes=[("u", dr, tb)])

        def d_piece(dr, tb):
            a_t, u_t, x_t = aa[dr], uu[dr], ixc[dr]
            lo, hi = tb * 512, (tb + 1) * 512
            if dr == 0:
                init = 0.0 if tb == 0 else x_t[:, lo - 1:lo]
                rd = [("u", dr, tb), ("a", dr, tb)] + ([("ixc", dr, tb - 1)] if tb > 0 else [])
                c.op("dve", lambda e: e.tensor_tensor_scan(
                    out=x_t[:, lo:hi], data0=a_t[:, lo:hi], data1=u_t[:, lo:hi], initial=init,
                    op0=ALU.mult, op1=ALU.add),
                    reads=rd, writes=[("ixc", dr, tb)])
            else:
                init = 0.0 if tb == 3 else x_t[:, hi:hi + 1]
                rd = [("u", dr, tb), ("a", dr, tb)] + ([("ixc", dr, tb + 1)] if tb < 3 else [])
                c.op("dve", lambda e: e.tensor_tensor_scan(
                    out=x_t[:, lo:hi][:, ::-1], data0=a_t[:, lo:hi][:, ::-1], data1=u_t[:, lo:hi][:, ::-1], initial=init,
                    op0=ALU.mult, op1=ALU.add),
                    reads=rd, writes=[("ixc", dr, tb)])

        def yz_piece(k, tb):
            sg = sgs[k % 2]
            ts = slice(tb * 512, (tb + 1) * 512)
            c.op("dve", lambda e: e.scalar_tensor_tensor(out=ixc[0][:, ts], in0=ixc[0][:, ts], scalar=1.0, in1=ixc[1][:, ts],
                                                         op0=ALU.mult, op1=ALU.add),
                 reads=[("ixc", 0, tb), ("ixc", 1, tb)], writes=[("ixc", 0, tb)])
            c.op("dve", lambda e: e.scalar_tensor_tensor(out=z[:, k % 2, ts], in0=ixc[0][:, ts], scalar=0.5, in1=sg[:, ts],
                                                         op0=ALU.mult, op1=ALU.mult),
                 reads=[("ixc", 0, tb), ("sg", k % 2, tb)], writes=[("z", k % 2, tb)])

        def w_tiles(kp):
            r0 = kp * 256
            view = d["odd_w_out"][i][r0:r0 + 256, :].rearrange("(k p) c -> p k c", p=128)
            w, wk = wh_load(view, 2, 1024, nh=2)
            tiles = []
            for tb in (3, 2, 1, 0):
                for n in range(8):
                    def tile(n=n, tb=tb):
                        ts = slice(tb * 512, (tb + 1) * 512)
                        ps, pk = self.psum()
                        for ec in range(2):
                            c.op("pe", lambda e, ec=ec: e.matmul(
                                ps[:, :], lhsT=w[:, ec, n * 128:(n + 1) * 128], rhs=z[:, ec, ts],
                                start=(ec == 0), stop=(ec == 1)),
                                reads=wk + [("z", ec, tb)], writes=[pk])
                        c.op("dve", lambda e: e.tensor_tensor(out=x[:, n, ts], in0=ps[:, :], in1=x[:, n, ts], op=ALU.add),
                             reads=[pk, self.xk(n, tb)], writes=[self.xk(n, tb)])
                    tiles.append(tile)
            return tiles

        NBLK = 16
        stage_a_load(0)
        stage_a_load(1)
        for t in range(8):
            stage_a_tile(0, t)
        stage_b(0)
        wq = []
        for k in range(NBLK):
            if k + 2 < NBLK:
                stage_a_load(k + 2)
            for s_ in range(8):
                if s_ < 4:
                    tbp = 3 - s_
                    if k >= 1:
                        if tbp > 0:
                            u_piece(1, tbp - 1)
                        d_piece(1, tbp)
                        yz_piece(k - 1, tbp)
                    stage_c_step(k, 0, tbp)
                else:
                    if s_ == 4 and k % 2 == 0 and k >= 2:
                        wq.extend(w_tiles(k // 2 - 1))
                    if s_ == 4:
                        d_sqrt(0)
                        u_piece(0, 0)
                    if s_ < 7:
                        u_piece(0, s_ - 3)
                    d_piece(0, s_ - 4)
                    stage_c_step(k, 1, 7 - s_)
                if k + 1 < NBLK:
                    stage_a_tile(k + 1, s_)
                for _ in range(5):
                    if wq:
                        wq.pop(0)()
            d_sqrt(1)
            u_piece(1, 3)
            if k + 1 < NBLK:
                stage_b(k + 1)
        for s_ in range(4):
            if s_ < 3:
                u_piece(1, 2 - s_)
            d_piece(1, 3 - s_)
            yz_piece(NBLK - 1, 3 - s_)
        while wq:
            wq.pop(0)()
        for tile in w_tiles(NBLK // 2 - 1):
            tile()
        for j in range(3):
            rd = {}
            for hh in range(2):
                o = st.pop(("wh", 2 * j + hh), None)
                if o is None:
                    continue
                for sem, val in o[1].items():
                    if rd.get(sem, 0) < val:
                        rd[sem] = val
                if o[0] is not None and rd.get(o[0][0], 0) < o[0][1]:
                    rd[o[0][0]] = o[0][1]
            st[("w", j)] = [None, rd]


def _constants():
    ident = np.eye(128, dtype=np.float32)
    perm = np.zeros((128, 128), np.float32)
    for mm_ in range(128):
        base = (mm_ // 64) * 64
        r = mm_ - base
        kk = base + (r + 32 if r < 32 else r - 32)
        perm[kk, mm_] = 1.0
    t = np.arange(S)
    row = (t // 64).astype(np.float32)
    col = (t % 64).astype(np.float32)
    inv_freq = (np.float32(10000.0) ** (-np.arange(0, 64, 2, dtype=np.float32) / np.float32(64))).astype(np.float32)
    ang_r = row[:, None] * inv_freq[None, :]
    ang_c = col[:, None] * inv_freq[None, :]
    cos = np.zeros((128, S), np.float32)
    sin = np.zeros((128, S), np.float32)
    cos[0:32] = np.cos(ang_r).T
    cos[32:64] = np.cos(ang_r).T
    cos[64:96] = np.cos(ang_c).T
    cos[96:128] = np.cos(ang_c).T
    sin[0:32] = -np.sin(ang_r).T
    sin[32:64] = np.sin(ang_r).T
    sin[64:96] = -np.sin(ang_c).T
    sin[96:128] = np.sin(ang_c).T
    ss = np.arange(S, dtype=np.int64)
    ph = (np.outer(ss, ss) % S).astype(np.float64) * (2.0 * np.pi / S)
    dftc = (np.cos(ph) / math.sqrt(S)).astype(ml_dtypes.bfloat16)
    dfts = (-np.sin(ph) / math.sqrt(S)).astype(ml_dtypes.bfloat16)
    cs = np.arange(256, dtype=np.int64)
    phc = (np.outer(cs, cs) % 256).astype(np.float64) * (2.0 * np.pi / 256)
    cc = (np.cos(phc) / 16.0).astype(ml_dtypes.bfloat16)
    sc = (np.sin(phc) / 16.0).astype(ml_dtypes.bfloat16)
    return {"c_ident": ident, "c_perm": perm, "c_cos": cos, "c_sin": sin,
            "c_dftc": dftc, "c_dfts": dfts, "c_cc": cc, "c_sc": sc}


def _layout(inputs):
    f = lambda a: np.ascontiguousarray(np.asarray(a, dtype=np.float32))
    o = {}
    o["even_norm"] = f(np.asarray(inputs["even_norm"]).reshape(2, 8, 128).transpose(0, 2, 1))
    o["odd_norm"] = f(np.asarray(inputs["odd_norm"]).reshape(2, 8, 128).transpose(0, 2, 1))
    o["q_gain"] = f(np.asarray(inputs["q_gain"]).reshape(2, 128, 1))
    o["k_gain"] = f(np.asarray(inputs["k_gain"]).reshape(2, 128, 1))
    o["conv_w"] = f(np.asarray(inputs["conv_w"]).reshape(2, 4, 16, 128).transpose(0, 3, 2, 1))
    o["conv_b"] = f(np.asarray(inputs["conv_b"]).reshape(2, 16, 128).transpose(0, 2, 1))
    for nm in ("gate_a_b", "gate_x_b"):
        o[nm] = f(np.asarray(inputs[nm]).transpose(0, 3, 1, 2))
    o["rglru_lambda"] = f(np.asarray(inputs["rglru_lambda"]).reshape(2, 2, 16, 128).transpose(0, 3, 1, 2))
    for nm in ("even_w_in", "fourier_w", "even_w_out", "odd_w_in", "odd_w_out", "final_norm"):
        o[nm] = f(inputs[nm])
    gw = np.stack([np.asarray(inputs["gate_a_w"]), np.asarray(inputs["gate_x_w"])], axis=0)
    o["gate_w"] = f(gw.transpose(1, 3, 4, 0, 2, 5).reshape(2, 16, 128, 4, 128))
    return o


_CACHE = {}


def _program(layers, do_final):
    key = (tuple(layers), do_final)
    if key not in _CACHE:
        _CACHE[key] = Builder(list(layers), do_final).nc
    return _CACHE[key]


def run_layers(x, shared, layers, do_final, core_ids):
    nc = _program(layers, do_final)
    in_maps = []
    for ci in core_ids:
        mp = dict(shared)
        mp["x"] = np.ascontiguousarray(x[ci * NB:(ci + 1) * NB])
        in_maps.append(mp)
    res = run_bass_kernel_spmd(nc, in_maps, core_ids=list(range(len(core_ids))))
    return [r["y"] for r in res.results]


def kernel(**inputs):
    x = np.asarray(inputs["x"], dtype=np.float32)
    shared = _layout(inputs)
    shared.update(_constants())
    outs = run_layers(x, shared, [0, 1, 2, 3], True, list(range(NCORES)))
    return np.concatenate(outs, axis=0).astype(np.float32)
```

```python
import math
from contextlib import ExitStack

import numpy as np
import ml_dtypes

import concourse.bass as bass
import concourse.mybir as mybir
from concourse.bass_utils import run_bass_kernel_spmd

F32 = mybir.dt.float32
BF16 = mybir.dt.bfloat16
AF = mybir.ActivationFunctionType
ALU = mybir.AluOpType

S = 2048
D = 1024
NCORES = 8
NB = 2
EPS = 1e-6
ATT_SCALE = 128 ** -0.5
ENGS = ("pe", "act", "dve", "pool", "sp")
ATTACH = True
NPS = 8
DBG = set()
NTT = 16


class Sched:
    def __init__(self):
        self.ops = {e: [] for e in ENGS}
        self.n = {e: 0 for e in ENGS}
        self.seen = {e: {} for e in ENGS}
        self.st = {}
        self.dcnt = {}
        self.window = 1 << 30

    def _need(self, eng, ev, raw):
        sem, val = ev
        if sem == eng:
            if eng == "pe" or not raw:
                return False
            return self.n[eng] - val < self.window
        return self.seen[eng].get(sem, 0) < val

    def op(self, eng, fn, reads=(), writes=(), dsem=None, attach=True):
        waits = {}

        def add(ev, raw):
            if ev is not None and self._need(eng, ev, raw):
                if waits.get(ev[0], 0) < ev[1]:
                    waits[ev[0]] = ev[1]

        for k in reads:
            s = self.st.get(k)
            if s is not None:
                add(s[0], True)
                if k[0] == "ps":
                    for sem, val in s[1].items():
                        if sem != eng:
                            add((sem, val), False)
        for k in writes:
            s = self.st.get(k)
            if s is not None:
                add(s[0], False)
                for sem, val in s[1].items():
                    add((sem, val), False)
        for sem, val in waits.items():
            if self.seen[eng].get(sem, 0) < val:
                self.seen[eng][sem] = val
        if dsem is None:
            self.n[eng] += 1
            ev = (eng, self.n[eng])
        else:
            self.dcnt[dsem] = self.dcnt.get(dsem, 0) + 16
            ev = (dsem, self.dcnt[dsem])
        self.ops[eng].append((list(waits.items()), fn, ev, ATTACH and attach))
        for k in reads:
            s = self.st.setdefault(k, [None, {}])
            if s[1].get(ev[0], 0) < ev[1]:
                s[1][ev[0]] = ev[1]
        for k in writes:
            self.st[k] = [ev, {}]
        return ev

    def barrier(self, engines=ENGS, clear=False):
        evs = [(e, self.n[e]) for e in ENGS if self.n[e] > 0]
        evs += list(self.dcnt.items())
        for e in engines:
            waits = []
            for sem, val in evs:
                if sem == e:
                    if e != "pe":
                        waits.append((sem, val))
                elif self.seen[e].get(sem, 0) < val:
                    waits.append((sem, val))
                    self.seen[e][sem] = val
            if waits:
                self.ops[e].append((waits, None, None, False))
        if clear:
            self.st = {k: v for k, v in self.st.items() if k[0] in ("w", "wh")}

    def final_wait(self, eng, events):
        waits = [(s, v) for s, v in events]
        self.ops[eng].append((waits, None, None, False))

    def emit(self, nc, sems):
        with nc.Block() as block:
            decos = {"pe": block.tensor, "act": block.scalar, "dve": block.vector,
                     "pool": block.gpsimd, "sp": block.sync}
            for e in ENGS:
                ops = self.ops[e]

                def body(engine, ops=ops):
                    for waits, fn, ev, attach in ops:
                        if fn is None:
                            for s, v in waits:
                                engine.wait_ge(sems[s], v)
                            continue
                        rest = waits
                        first = None
                        if attach and waits:
                            first = waits[0]
                            rest = waits[1:]
                        for s, v in rest:
                            engine.wait_ge(sems[s], v)
                        ins = fn(engine)
                        if first is not None:
                            ins._wait_ge(sems[first[0]], first[1])
                        ins.then_inc(sems[ev[0]], 1 if ev[0] in ENGS else 16)

                decos[e](body)


class Mem:
    def __init__(self, nc, limit):
        self.nc, self.off, self.limit, self.cnt = nc, 16512, limit, 0

    def alloc(self, name, free_shape, dtype):
        nbytes = int(np.prod(free_shape)) * (4 if dtype == F32 else 2)
        off = (self.off + 31) // 32 * 32
        assert off + nbytes <= self.limit, f"SBUF overflow for {name}: {off}+{nbytes} > {self.limit}"
        self.cnt += 1
        t = self.nc.alloc_sbuf_tensor_at(f"{name}_{self.cnt}", [128] + list(free_shape), dtype, offset=off)
        self.off = off + nbytes
        return t


class Ring:
    def __init__(self, mem, name, n, free_shape, dtype):
        self.t = [mem.alloc(f"{name}{i}", free_shape, dtype) for i in range(n)]
        self.name, self.i = name, 0

    def next(self):
        j = self.i % len(self.t)
        self.i += 1
        return self.t[j], (self.name, j)


DRAM_INPUTS = [
    ("x", [NB, S, D], F32),
    ("even_norm", [2, 128, 8], F32), ("even_w_in", [2, 1024, 4608], F32),
    ("fourier_w", [2, 4, 256, 256], F32), ("q_gain", [2, 128, 1], F32), ("k_gain", [2, 128, 1], F32),
    ("even_w_out", [2, 2048, 1024], F32),
    ("odd_norm", [2, 128, 8], F32), ("odd_w_in", [2, 1024, 4096], F32),
    ("conv_w", [2, 128, 16, 4], F32), ("conv_b", [2, 128, 16], F32),
    ("gate_w", [2, 16, 128, 4, 128], F32), ("gate_a_b", [2, 128, 2, 16], F32),
    ("gate_x_b", [2, 128, 2, 16], F32),
    ("rglru_lambda", [2, 128, 2, 16], F32), ("odd_w_out", [2, 2048, 1024], F32),
    ("final_norm", [1024], F32),
    ("c_ident", [128, 128], F32), ("c_perm", [128, 128], F32),
    ("c_cos", [128, S], F32), ("c_sin", [128, S], F32),
    ("c_dftc", [S, S], BF16), ("c_dfts", [S, S], BF16),
    ("c_cc", [256, 256], BF16), ("c_sc", [256, 256], BF16),
]


class Builder:
    def __init__(self, layers, do_final):
        self.layers = layers
        self.do_final = do_final
        nc = self.nc = bass.Bass("TRN2", target_bir_lowering=False)
        self.d = {}
        for name, shape, dt in DRAM_INPUTS:
            self.d[name] = nc.dram_tensor(name, shape, dt, kind="ExternalInput").ap()
        self.y = nc.dram_tensor("y", [NB, S, D], F32, kind="ExternalOutput").ap()
        self.c = Sched()
        limit = 229376
        self.mem = m = Mem(nc, limit)
        self.x = m.alloc("x", [8, S], F32)
        self.h = m.alloc("h", [8, S], BF16)
        self.ident_f = m.alloc("identf", [128], F32)
        self.ident_b = m.alloc("identb", [128], BF16)
        self.ones_b = m.alloc("onesb", [128], BF16)
        self.perm_b = m.alloc("permb", [128], BF16)
        self.gcol = m.alloc("gcol", [8], F32)
        self.qg = m.alloc("qg", [1], F32)
        self.kg = m.alloc("kg", [1], F32)
        self.cw = m.alloc("cw", [16, 4], F32)
        self.cb = m.alloc("cb", [16], F32)
        self.hba = m.alloc("hba", [2, 16], F32)
        self.hbx = m.alloc("hbx", [2, 16], F32)
        self.lam = m.alloc("lam", [2, 16], F32)
        self.cc1 = m.alloc("cc1", [2, 16], F32)
        self.cch = m.alloc("cch", [2, 16], F32)
        self.wslots = [m.alloc(f"w{i}", [2048], BF16) for i in range(3)]
        self.wi = 0
        self.arena0 = m.off
        self.psw = [nc.alloc_psum_tensor(f"psw{i}", [128, 1024], F32) for i in range(4)]
        self.ps = [self.psw[i // 2][:, (i % 2) * 512:(i % 2 + 1) * 512] for i in range(8)]
        self.psi = 0
        self.out_events = []
        self.build()
        with ExitStack() as es:
            sems = {}
            for e in ENGS:
                sems[e] = es.enter_context(nc.semaphore(f"s_{e}"))
            for dn in self.c.dcnt:
                sems[dn] = es.enter_context(nc.semaphore(f"d_{dn}"))
            self.c.emit(nc, sems)

    def psum(self):
        i = self.psi % NPS
        self.psi += 1
        return self.ps[i], ("ps", i)

    def arena_reset(self, full=False):
        self.c.barrier(engines=ENGS if full else ("pe", "act", "dve", "sp"), clear=True)
        self.mem.off = self.arena0

    def wring(self, n):
        pass

    def wload(self, dram_view, nk, cols):
        j = self.wi % len(self.wslots)
        self.wi += 1
        t = self.wslots[j]
        dst = t[:, 0:nk * cols].rearrange("p (k c) -> p k c", k=nk)
        self.c.op("pool", lambda e: e.dma_start(out=dst, in_=dram_view), writes=[("w", j)], dsem=f"w{j}")
        return dst, ("w", j)

    def dma_in(self, out_ap, in_ap, key, dsem, eng="sp"):
        self.c.op(eng, lambda e: e.dma_start(out=out_ap, in_=in_ap), writes=[key], dsem=dsem)

    def xk(self, kc, tb):
        return ("x", kc, tb)

    def hk(self, kc, tb):
        return ("h", kc, tb)

    def build(self):
        c = self.c
        self.dma_in(self.ident_f[:, :], self.d["c_ident"][:, :], ("identf",), "c0")
        self.dma_in(self.ident_b[:, :], self.d["c_ident"][:, :], ("identb",), "c1", eng="pool")
        self.dma_in(self.perm_b[:, :], self.d["c_perm"][:, :], ("permb",), "c2", eng="pool")
        ones_b = self.ones_b
        c.op("dve", lambda e: e.memset(ones_b[:, :], 1.0), writes=[("onesb",)])
        c.barrier(clear=False)
        for b in range(NB):
            self.phase_load_x(b)
            for L in self.layers:
                if L % 2 == 0:
                    self.even_layer(L // 2)
                else:
                    self.odd_layer(L // 2)
            self.phase_store(b)
        c.barrier(engines=("sp",))
        c.final_wait("sp", self.out_events)

    def phase_load_x(self, b):
        c = self.c
        self.arena_reset()
        stage = Ring(self.mem, "xin", 2, [D], F32)
        x, ident = self.x, self.ident_f
        for tt in range(NTT):
            st, sk = stage.next()
            self.dma_in(st[:, :], self.d["x"][b, tt * 128:(tt + 1) * 128, :], sk, f"xin{sk[1]}")
            for half in range(2):
                ps, pk = self.psum()
                for j in range(4):
                    kc = half * 4 + j
                    c.op("pe", lambda e, ps=ps, st=st, j=j, kc=kc: e.transpose(
                        ps[:, j * 128:(j + 1) * 128], st[:, kc * 128:(kc + 1) * 128], ident[:, :]),
                        reads=[sk, ("identf",)], writes=[pk])
                dst = x[:, half * 4:half * 4 + 4, tt * 128:(tt + 1) * 128]
                src = ps[:, :].rearrange("p (a b) -> p a b", a=4)
                wk = [self.xk(half * 4 + j, tt // 4) for j in range(4)]
                if half == 0:
                    c.op("act", lambda e, dst=dst, src=src: e.activation(out=dst, in_=src, func=AF.Copy),
                         reads=[pk], writes=wk)
                else:
                    c.op("dve", lambda e, dst=dst, src=src: e.tensor_copy(out=dst, in_=src),
                         reads=[pk], writes=wk)

    def phase_store(self, b):
        c = self.c
        self.arena_reset()
        m = self.mem
        stage = Ring(m, "ot", 2, [D], F32)
        x, ident = self.x, self.ident_f
        if self.do_final:
            gfin = m.alloc("gfin", [D], F32)
            junk = m.alloc("junk", [D], BF16)
            ssr = Ring(m, "ss", 2, [1], F32)
            sdr = Ring(m, "sd", 2, [1], F32)
            rsr = Ring(m, "rs", 2, [1], F32)
            self.dma_in(gfin[:, :], self.d["final_norm"].partition_broadcast(128), ("gfin",), "gfin")
        for tt in range(16):
            ot, ok = stage.next()
            for half in range(2):
                ps, pk = self.psum()
                for j in range(4):
                    kc = half * 4 + j
                    c.op("pe", lambda e, ps=ps, j=j, kc=kc, tt=tt: e.transpose(
                        ps[:, j * 128:(j + 1) * 128], x[:, kc, tt * 128:(tt + 1) * 128], ident[:, :]),
                        reads=[self.xk(kc, tt // 4), ("identf",)], writes=[pk])
                dst = ot[:, half * 512:(half + 1) * 512]
                if half == 0 and "noactcopy" not in DBG:
                    c.op("act", lambda e, dst=dst, ps=ps: e.activation(out=dst, in_=ps[:, :], func=AF.Copy),
                         reads=[pk], writes=[ok])
                else:
                    c.op("dve", lambda e, dst=dst, ps=ps: e.tensor_copy(out=dst, in_=ps[:, :]),
                         reads=[pk], writes=[ok])
            if self.do_final:
                ss, ssk = ssr.next()
                sd, sdk = sdr.next()
                rs, rsk = rsr.next()
                c.op("act", lambda e, ot=ot, ss=ss: e.activation(out=junk[:, :], in_=ot[:, :], func=AF.Square,
                                                                accum_out=ss[:, 0:1]),
                     reads=[ok], writes=[("junk",), ssk], attach=False)
                c.op("act", lambda e, sd=sd, ss=ss: e.activation(out=sd[:, :], in_=ss[:, :], func=AF.Sqrt,
                                                                scale=1.0 / D, bias=EPS),
                     reads=[ssk], writes=[sdk])
                c.op("dve", lambda e, rs=rs, sd=sd: e.reciprocal(out=rs[:, :], in_=sd[:, :]),
                     reads=[sdk], writes=[rsk])
                c.op("dve", lambda e, ot=ot, rs=rs: e.scalar_tensor_tensor(
                    out=ot[:, :], in0=ot[:, :], scalar=rs[:, 0:1], in1=gfin[:, :], op0=ALU.mult, op1=ALU.mult),
                    reads=[ok, rsk, ("gfin",)], writes=[ok])
            ev = c.op("sp", lambda e, ot=ot, tt=tt, b=b: e.dma_start(out=self.y[b, tt * 128:(tt + 1) * 128, :], in_=ot[:, :]),
                      reads=[ok], dsem=f"out{ok[1]}")
            self.out_events = [x_ for x_ in self.out_events if x_[0] != ev[0]] + [ev]

    def phase_norm(self, gsrc):
        c = self.c
        self.arena_reset()
        m = self.mem
        x, h, gcol, ones = self.x, self.h, self.gcol, self.ones_b
        self.dma_in(gcol[:, :], gsrc, ("gcol",), "gcol")
        sqr = Ring(m, "sq", 4, [512], BF16)
        sdr = Ring(m, "sd", 2, [512], F32)
        rsr = Ring(m, "rs", 2, [512], F32)
        for tb in range(4):
            ts = slice(tb * 512, (tb + 1) * 512)
            ps, pk = self.psum()
            for kc in range(8):
                sq, sqk = sqr.next()
                c.op("act", lambda e, sq=sq, kc=kc, ts=ts: e.activation(out=sq[:, :], in_=x[:, kc, ts], func=AF.Square),
                     reads=[self.xk(kc, tb)], writes=[sqk])
                c.op("pe", lambda e, ps=ps, sq=sq, kc=kc: e.matmul(ps[:, :], lhsT=ones[:, :], rhs=sq[:, :],
                                                                   start=(kc == 0), stop=(kc == 7)),
                     reads=[sqk, ("onesb",)], writes=[pk])
            sd, sdk = sdr.next()
            rs, rsk = rsr.next()
            c.op("act", lambda e, sd=sd, ps=ps: e.activation(out=sd[:, :], in_=ps[:, :], func=AF.Ln,
                                                            scale=1.0 / D, bias=EPS),
                 reads=[pk], writes=[sdk])
            c.op("act", lambda e, rs=rs, sd=sd: e.activation(out=rs[:, :], in_=sd[:, :], func=AF.Exp, scale=-0.5),
                 reads=[sdk], writes=[rsk])
            for kc in range(8):
                c.op("dve", lambda e, kc=kc, ts=ts, rs=rs: e.scalar_tensor_tensor(
                    out=h[:, kc, ts], in0=x[:, kc, ts], scalar=gcol[:, kc:kc + 1], in1=rs[:, :],
                    op0=ALU.mult, op1=ALU.mult),
                    reads=[self.xk(kc, tb), rsk, ("gcol",)], writes=[self.hk(kc, tb)])

    def proj_fm(self, w, wk, col0, tb, ps, pk):
        h = self.h
        ts = slice(tb * 512, (tb + 1) * 512)
        for kc in range(8):
            self.c.op("pe", lambda e, kc=kc: e.matmul(ps[:, :], lhsT=w[:, kc, col0:col0 + 128], rhs=h[:, kc, ts],
                                                      start=(kc == 0), stop=(kc == 7)),
                      reads=[wk, self.hk(kc, tb)], writes=[pk])

    def wout_partial(self, wsrc, r0, z, zkey):
        c = self.c
        x = self.x
        for ch in range(2):
            view = wsrc[r0:r0 + 512, ch * 512:(ch + 1) * 512].rearrange("(k p) c -> p k c", p=128)
            w, wk = self.wload(view, 4, 512)
            for nl in range(4):
                n = ch * 4 + nl
                for tb in range(4):
                    ts = slice(tb * 512, (tb + 1) * 512)
                    ps, pk = self.psum()
                    for ec in range(4):
                        c.op("pe", lambda e, ps=ps, w=w, ec=ec, nl=nl, ts=ts: e.matmul(
                            ps[:, :], lhsT=w[:, ec, nl * 128:(nl + 1) * 128], rhs=z[:, ec, ts],
                            start=(ec == 0), stop=(ec == 3)),
                            reads=[wk, zkey(ec, tb)], writes=[pk])
                    c.op("dve", lambda e, ps=ps, n=n, ts=ts: e.tensor_tensor(
                        out=x[:, n, ts], in0=ps[:, :], in1=x[:, n, ts], op=ALU.add),
                        reads=[pk, self.xk(n, tb)], writes=[self.xk(n, tb)])

    def gate_silu(self, wsrc_cols, z, zkey):
        c = self.c
        for u in range(2):
            w, wk = self.wload(wsrc_cols(u), 8, 256)
            for el in range(2):
                ec = u * 2 + el
                for tb in range(4):
                    ts = slice(tb * 512, (tb + 1) * 512)
                    ps, pk = self.psum()
                    self.proj_fm(w, wk, el * 128, tb, ps, pk)
                    c.op("act", lambda e, ps=ps, ec=ec, ts=ts: e.activation(out=z[:, ec, ts], in_=ps[:, :], func=AF.Silu),
                         reads=[pk], writes=[zkey(ec, tb)])

    def even_layer(self, i):
        d = self.d
        self.phase_norm(d["even_norm"][i, :, :])
        for p in range(2):
            self.phase_fourier(i, p)
        for g in range(2):
            self.phase_attn(i, g)

    def phase_fourier(self, i, p):
        c = self.c
        d = self.d
        m = self.mem
        h = self.h
        if p == 0:
            self.arena_reset()
            z = m.alloc("z", [4, S], BF16)
            AB = m.alloc("AB", [16, 1024], BF16)
            tabC = [m.alloc(f"tabC{j}", [16, 256], BF16) for j in range(2)]
            tabS = [m.alloc(f"tabS{j}", [16, 256], BF16) for j in range(2)]
            finr = Ring(m, "fin", 2, [2, 512], BF16)
            MAB = m.alloc("MAB", [2, 2, 512], BF16)
            ccb = m.alloc("ccb", [2, 256], BF16)
            scb = m.alloc("scb", [2, 256], BF16)
            self.dma_in(ccb[:, :, :], d["c_cc"].rearrange("(k p) c -> p k c", p=128), ("ccb",), "ccb")
            self.dma_in(scb[:, :, :], d["c_sc"].rearrange("(k p) c -> p k c", p=128), ("scb",), "scb")
            self._F = (z, AB, tabC, tabS, finr, MAB, ccb, scb)
        else:
            z, AB, tabC, tabS, finr, MAB, ccb, scb = self._F
        self.wring(3)
        zkey = lambda ec, tb: ("z", ec, tb)
        win = d["even_w_in"]
        for gl in range(2):
            g = 2 * p + gl
            fw, fwk = self.wload(d["fourier_w"][i, g, :, :].rearrange("(k p) c -> p k c", p=128), 2, 256)
            for mb in range(2):
                ps, pk = self.psum()
                for which, src, sk in ((0, ccb, ("ccb",)), (1, scb, ("scb",))):
                    for kc in range(2):
                        c.op("pe", lambda e, ps=ps, src=src, kc=kc, mb=mb, fw=fw, which=which: e.matmul(
                            ps[:, which * 256:(which + 1) * 256], lhsT=src[:, kc, mb * 128:(mb + 1) * 128],
                            rhs=fw[:, kc, :], start=(kc == 0), stop=(kc == 1)),
                            reads=[sk, fwk], writes=[pk])
                c.op("act", lambda e, ps=ps, gl=gl, mb=mb: e.activation(out=MAB[:, gl, mb, :], in_=ps[:, :], func=AF.Copy),
                     reads=[pk], writes=[("MAB", gl)])
        self.gate_silu(lambda u: win[i, :, 1024 + p * 512 + u * 256: 1024 + p * 512 + (u + 1) * 256]
                       .rearrange("(k p) c -> p k c", p=128), z, zkey)
        for gl in range(2):
            g = 2 * p + gl
            w, wk = self.wload(win[i, :, g * 256:(g + 1) * 256].rearrange("(k p) c -> p k c", p=128), 8, 256)
            for tb in range(4):
                fin, fk = finr.next()
                for half in range(2):
                    ps, pk = self.psum()
                    self.proj_fm(w, wk, half * 128, tb, ps, pk)
                    if half == 0:
                        c.op("act", lambda e, ps=ps, fin=fin: e.activation(out=fin[:, 0, :], in_=ps[:, :], func=AF.Copy),
                             reads=[pk], writes=[fk])
                    else:
                        c.op("dve", lambda e, ps=ps, fin=fin: e.tensor_copy(out=fin[:, 1, :], in_=ps[:, :]),
                             reads=[pk], writes=[fk])
                for tq in range(4):
                    tt = tb * 4 + tq
                    ps, pk = self.psum()
                    for half in range(2):
                        c.op("pe", lambda e, ps=ps, fin=fin, half=half, tq=tq, gl=gl: e.matmul(
                            ps[:, :], lhsT=fin[:, half, tq * 128:(tq + 1) * 128], rhs=MAB[:, gl, half, :],
                            start=(half == 0), stop=(half == 1)),
                            reads=[fk, ("MAB", gl)], writes=[pk])
                    dst = AB[:, tt, gl * 512:(gl + 1) * 512]
                    if tq % 2 == 0:
                        c.op("act", lambda e, ps=ps, dst=dst: e.activation(out=dst, in_=ps[:, :], func=AF.Copy),
                             reads=[pk], writes=[("AB", tt, gl)])
                    else:
                        c.op("dve", lambda e, ps=ps, dst=dst: e.tensor_copy(out=dst, in_=ps[:, :]),
                             reads=[pk], writes=[("AB", tt, gl)])
        for sb in range(8):
            tC, tS = tabC[sb % 2], tabS[sb % 2]
            kC, kS = ("tabC", sb % 2), ("tabS", sb % 2)
            self.dma_in(tC[:, :, :], d["c_dftc"][:, sb * 256:(sb + 1) * 256].rearrange("(j p) c -> p j c", p=128),
                        kC, f"tabC{sb % 2}")
            self.dma_in(tS[:, :, :], d["c_dfts"][:, sb * 256:(sb + 1) * 256].rearrange("(j p) c -> p j c", p=128),
                        kS, f"tabS{sb % 2}")
            ss = slice(sb * 256, (sb + 1) * 256)
            for ec in range(4):
                gl, half = ec // 2, ec % 2
                ps, pk = self.psum()
                for which, tab, tk in ((0, tC, kC), (1, tS, kS)):
                    c0 = gl * 512 + which * 256 + half * 128
                    for j in range(16):
                        c.op("pe", lambda e, ps=ps, tab=tab, j=j, c0=c0, which=which: e.matmul(
                            ps[:, 0:256], lhsT=AB[:, j, c0:c0 + 128], rhs=tab[:, j, :],
                            start=(which == 0 and j == 0), stop=(which == 1 and j == 15)),
                            reads=[("AB", j, gl), tk], writes=[pk])
                c.op("dve", lambda e, ps=ps, ec=ec, ss=ss: e.tensor_tensor(
                    out=z[:, ec, ss], in0=ps[:, 0:256], in1=z[:, ec, ss], op=ALU.mult),
                    reads=[pk, zkey(ec, sb // 2)], writes=[zkey(ec, sb // 2)])
        self.wout_partial(d["even_w_out"][i], p * 512, z, zkey)

    def normrope(self, ps, pk, gc, gk, out_ap, out_key, tb, R):
        c = self.c
        ts = slice(tb * 512, (tb + 1) * 512)
        ones, perm = self.ones_b, self.perm_b
        cos, sin = R["cos"], R["sin"]
        sq, sqk = R["sq"].next()
        qb, qbk = R["qb"].next()
        t1, t1k = R["t1"].next()
        t2, t2k = R["t2"].next()
        sd, sdk = R["sd"].next()
        rs, rsk = R["rs"].next()
        c.op("act", lambda e: e.activation(out=sq[:, :], in_=ps[:, :], func=AF.Square), reads=[pk], writes=[sqk])
        c.op("act", lambda e: e.activation(out=qb[:, :], in_=ps[:, :], func=AF.Identity, scale=gc[:, 0:1]),
             reads=[pk, gk], writes=[qbk])
        ps2, pk2 = self.psum()
        ps3, pk3 = self.psum()
        c.op("pe", lambda e: e.matmul(ps2[:, :], lhsT=ones[:, :], rhs=sq[:, :], start=True, stop=True),
             reads=[sqk, ("onesb",)], writes=[pk2])
        c.op("pe", lambda e: e.matmul(ps3[:, :], lhsT=perm[:, :], rhs=qb[:, :], start=True, stop=True),
             reads=[qbk, ("permb",)], writes=[pk3])
        c.op("dve", lambda e: e.scalar_tensor_tensor(out=t1[:, :], in0=ps[:, :], scalar=gc[:, 0:1], in1=cos[:, ts],
                                                     op0=ALU.mult, op1=ALU.mult),
             reads=[pk, gk, ("cos",)], writes=[t1k])
        c.op("act", lambda e: e.activation(out=sd[:, :], in_=ps2[:, :], func=AF.Ln, scale=1.0 / 128, bias=EPS),
             reads=[pk2], writes=[sdk])
        c.op("act", lambda e: e.activation(out=rs[:, :], in_=sd[:, :], func=AF.Exp, scale=-0.5),
             reads=[sdk], writes=[rsk])
        c.op("dve", lambda e: e.tensor_tensor(out=t2[:, :], in0=ps3[:, :], in1=sin[:, ts], op=ALU.mult),
             reads=[pk3, ("sin",)], writes=[t2k])
        c.op("pool", lambda e: e.tensor_tensor(out=t1[:, :], in0=t1[:, :], in1=t2[:, :], op=ALU.add),
             reads=[t1k, t2k], writes=[t1k])
        c.op("dve", lambda e: e.tensor_tensor(out=out_ap, in0=t1[:, :], in1=rs[:, :], op=ALU.mult),
             reads=[t1k, rsk], writes=[out_key])

    def phase_attn(self, i, g):
        c = self.c
        d = self.d
        m = self.mem
        h = self.h
        qg, kg = self.qg, self.kg
        if g == 0:
            self.arena_reset()
            z = m.alloc("z", [4, S], BF16)
            q = m.alloc("q", [4, S], BF16)
            k = m.alloc("k", [S], BF16)
            V = m.alloc("V", [16, 128], BF16)
            cos = m.alloc("cos", [S], F32)
            sin = m.alloc("sin", [S], F32)
            rdr = Ring(m, "rden", 2, [512], F32)
            ogr = Ring(m, "og", 2, [512], F32)
            off_R = m.off
            R = {"cos": cos, "sin": sin,
                 "sq": Ring(m, "sq", 2, [512], BF16), "qb": Ring(m, "qb", 2, [512], BF16),
                 "t1": Ring(m, "t1", 2, [512], F32), "t2": Ring(m, "t2", 2, [512], F32),
                 "sd": Ring(m, "sd", 2, [512], F32), "rs": Ring(m, "rs", 2, [512], F32)}
            self.dma_in(cos[:, :], d["c_cos"][:, :], ("cos",), "cos")
            self.dma_in(sin[:, :], d["c_sin"][:, :], ("sin",), "sin")
            self.dma_in(qg[:, :], d["q_gain"][i, :, :], ("qg",), "qg")
            self.dma_in(kg[:, :], d["k_gain"][i, :, :], ("kg",), "kg")
            self._A = (z, q, k, V, cos, sin, rdr, ogr, off_R, R)
        else:
            z, q, k, V, cos, sin, rdr, ogr, off_R, R = self._A
        zkey = lambda ec, tb: ("z", ec, tb)
        win = d["even_w_in"]
        a0 = 3584 + g * 512
        self.gate_silu(lambda u: win[i, :, a0 + u * 256: a0 + (u + 1) * 256].rearrange("(k p) c -> p k c", p=128),
                       z, zkey)
        w, wk = self.wload(win[i, :, 3072:3328].rearrange("(k p) c -> p k c", p=128), 8, 256)
        for tb in range(4):
            ps, pk = self.psum()
            self.proj_fm(w, wk, g * 128, tb, ps, pk)
            self.normrope(ps, pk, kg, ("kg",), k[:, tb * 512:(tb + 1) * 512], ("k", tb), tb, R)
        w, wk = self.wload(win[i, :, 3328:3584].rearrange("(k p) c -> p k c", p=128), 8, 256)
        for tt in range(16):
            ps, pk = self.psum()
            for kc in range(8):
                c.op("pe", lambda e, ps=ps, kc=kc, tt=tt, w=w: e.matmul(
                    ps[:, 0:128], lhsT=h[:, kc, tt * 128:(tt + 1) * 128], rhs=w[:, kc, g * 128:(g + 1) * 128],
                    start=(kc == 0), stop=(kc == 7)),
                    reads=[wk, self.hk(kc, tt // 4)], writes=[pk])
            if tt % 2 == 0:
                c.op("act", lambda e, ps=ps, tt=tt: e.activation(out=V[:, tt, :], in_=ps[:, 0:128], func=AF.Copy),
                     reads=[pk], writes=[("V", tt)])
            else:
                c.op("dve", lambda e, ps=ps, tt=tt: e.tensor_copy(out=V[:, tt, :], in_=ps[:, 0:128]),
                     reads=[pk], writes=[("V", tt)])
        q0 = 2048 + g * 512
        for u in range(2):
            w, wk = self.wload(win[i, :, q0 + u * 256: q0 + (u + 1) * 256].rearrange("(k p) c -> p k c", p=128), 8, 256)
            for el in range(2):
                hl = u * 2 + el
                for tb in range(4):
                    ps, pk = self.psum()
                    self.proj_fm(w, wk, el * 128, tb, ps, pk)
                    self.normrope(ps, pk, qg, ("qg",), q[:, hl, tb * 512:(tb + 1) * 512], ("q", hl, tb), tb, R)
        c.barrier(engines=("pe", "act", "dve", "pool"))
        m.off = off_R
        pTr = Ring(m, "pT", 4, [1024], BF16)
        qsr = Ring(m, "qsum", 12, [512], BF16)

        def finalize(fin):
            quads, o_ps, ok_, d_ps, dk_, hl, qb, qs = fin
            for qj, qsum, qsk in quads:
                c.op("pe", lambda e, qj=qj, qsum=qsum: e.matmul(
                    d_ps[:, :], lhsT=ones[:, :], rhs=qsum[:, :], start=(qj == 0), stop=(qj == 7)),
                    reads=[("onesb",), qsk], writes=[dk_])
            rden, rdk = rdr.next()
            og, ogk = ogr.next()
            c.op("act", lambda e: e.activation(out=rden[:, :], in_=d_ps[:, :], func=AF.Ln), reads=[dk_], writes=[rdk])
            c.op("act", lambda e: e.activation(out=rden[:, :], in_=rden[:, :], func=AF.Exp, scale=-1.0),
                 reads=[rdk], writes=[rdk])
            c.op("dve", lambda e: e.tensor_tensor(out=og[:, :], in0=o_ps[:, :], in1=rden[:, :], op=ALU.mult),
                 reads=[ok_, rdk], writes=[ogk])
            c.op("dve", lambda e: e.tensor_tensor(out=z[:, hl, qs], in0=og[:, :], in1=z[:, hl, qs], op=ALU.mult),
                 reads=[ogk, zkey(hl, qb)], writes=[zkey(hl, qb)])

        deferred = None
        ones = self.ones_b
        it = 0
        wcnt = 0
        for hl in range(4):
            for qb in range(4):
                qs = slice(qb * 512, (qb + 1) * 512)
                ob, db = (4, 5) if it % 2 == 0 else (6, 7)
                it += 1
                o_ps, ok_ = self.ps[ob], ("ps", ob)
                d_ps, dk_ = self.ps[db], ("ps", db)
                pend = None
                prevw = None
                quads = []
                for kp in range(9):
                    if kp < 8:
                        wi_ = wcnt % 2
                        wcnt += 1
                        wide = self.psw[wi_]
                        wkeys = [("ps", 2 * wi_), ("ps", 2 * wi_ + 1)]
                        for hh in range(2):
                            kt = 2 * kp + hh
                            c.op("pe", lambda e, wide=wide, kt=kt, hh=hh, hl=hl, qs=qs: e.matmul(
                                wide[:, hh * 512:(hh + 1) * 512], lhsT=k[:, kt * 128:(kt + 1) * 128], rhs=q[:, hl, qs],
                                start=True, stop=True),
                                reads=[("k", kt // 4), ("q", hl, qb)], writes=[wkeys[hh]])
                        pT, pTk = pTr.next()
                        c.op("act", lambda e, pT=pT, wide=wide: e.activation(out=pT[:, :], in_=wide[:, :], func=AF.Exp,
                                                                            scale=ATT_SCALE),
                             reads=wkeys, writes=[pTk])
                        qsum, qsk = qsr.next()
                        c.op("dve", lambda e, qsum=qsum, pT=pT: e.tensor_tensor(out=qsum[:, :], in0=pT[:, 0:512], in1=pT[:, 512:1024], op=ALU.add),
                             reads=[pTk], writes=[qsk])
                        quads.append((kp, qsum, qsk))
                    if pend is not None:
                        pkp, ppT, ppTk = pend
                        for hh in range(2):
                            pkt = 2 * pkp + hh
                            c.op("pe", lambda e, pkt=pkt, hh=hh, ppT=ppT, o_ps=o_ps: e.matmul(
                                o_ps[:, :], lhsT=V[:, pkt, :], rhs=ppT[:, hh * 512:(hh + 1) * 512],
                                start=(pkt == 0), stop=(pkt == 15)),
                                reads=[("V", pkt), ppTk], writes=[ok_])
                    if kp == 2 and deferred is not None:
                        finalize(deferred)
                        deferred = None
                    pend = (kp, pT, pTk) if kp < 8 else None
                deferred = (quads, o_ps, ok_, d_ps, dk_, hl, qb, qs)
        finalize(deferred)
        self.psi = 0
        c.barrier(engines=("pe", "act", "dve"))
        self.wout_partial(d["even_w_out"][i], 1024 + g * 512, z, zkey)

    def odd_layer(self, i):
        c = self.c
        d = self.d
        self.phase_norm(d["odd_norm"][i, :, :])
        cw, cb, hba, hbx, lam, cc1, cch = self.cw, self.cb, self.hba, self.hbx, self.lam, self.cc1, self.cch
        self.dma_in(cw[:, :, :], d["conv_w"][i], ("cw",), "cw")
        self.dma_in(cb[:, :], d["conv_b"][i], ("cb",), "cb")
        self.dma_in(hba[:, :, :], d["gate_a_b"][i], ("hba",), "hba")
        self.dma_in(hbx[:, :, :], d["gate_x_b"][i], ("hbx",), "hbx")
        self.dma_in(lam[:, :, :], d["rglru_lambda"][i], ("lam",), "lam")
        c.op("dve", lambda e: e.tensor_scalar(out=hba[:, :, :], in0=hba[:, :, :], scalar1=0.5, scalar2=None, op0=ALU.mult),
             reads=[("hba",)], writes=[("hba",)])
        c.op("dve", lambda e: e.tensor_scalar(out=hbx[:, :, :], in0=hbx[:, :, :], scalar1=0.5, scalar2=None, op0=ALU.mult),
             reads=[("hbx",)], writes=[("hbx",)])
        c.op("act", lambda e: e.activation(out=cc1[:, :, :], in_=lam[:, :, :], func=AF.Exp, scale=-1.0),
             reads=[("lam",)], writes=[("cc1",)])
        c.op("act", lambda e: e.activation(out=cc1[:, :, :], in_=cc1[:, :, :], func=AF.Ln, bias=1.0),
             reads=[("cc1",)], writes=[("cc1",)])
        c.op("dve", lambda e: e.tensor_scalar(out=cch[:, :, :], in0=cc1[:, :, :], scalar1=-4.0, scalar2=None, op0=ALU.mult),
             reads=[("cc1",)], writes=[("cch",)])
        c.op("dve", lambda e: e.tensor_scalar(out=cc1[:, :, :], in0=cc1[:, :, :], scalar1=-8.0, scalar2=None, op0=ALU.mult),
             reads=[("cc1",), ("cch",)], writes=[("cc1",)])
        self.phase_rglru_all(i)

    def phase_rglru_all(self, i):
        c = self.c
        d = self.d
        self.arena_reset(full=True)
        m = self.mem
        z = m.alloc("z", [2, S], BF16)
        xrb = m.alloc("xrb", [S + 16], BF16)
        sgs = [m.alloc(f"sg{j}", [S], BF16) for j in range(2)]
        xc = m.alloc("xc", [S], F32)
        xcb = m.alloc("xcb", [S], BF16)
        aa = [m.alloc(f"a{j}", [S], F32) for j in range(2)]
        uu = [m.alloc(f"u{j}", [S], F32) for j in range(2)]
        ixc = [m.alloc(f"ixc{j}", [S], F32) for j in range(2)]
        trr = Ring(m, "tr", 3, [512], F32)
        tir = Ring(m, "ti", 4, [512], F32)
        diag = m.alloc("diag", [4, 128], BF16)
        gwb = [m.alloc(f"gwb{j}", [4, 128], BF16) for j in range(2)]
        win = d["odd_w_in"]
        cw, cb, hba, hbx, cc1, cch = self.cw, self.cb, self.hba, self.hbx, self.cc1, self.cch
        identb = self.ident_b
        x = self.x
        c.op("dve", lambda e: e.memset(xrb[:, 0:2], 0.0), writes=[("xrb", 0)])
        c.op("dve", lambda e: e.memset(xrb[:, S + 2:S + 16], 0.0), writes=[("xrb", 3)])
        gws = {}
        st = c.st
        for j in range(3):
            old = st.pop(("w", j), None)
            for hh in range(2):
                if old is not None:
                    rd = dict(old[1])
                    if old[0] is not None and rd.get(old[0][0], 0) < old[0][1]:
                        rd[old[0][0]] = old[0][1]
                    st[("wh", 2 * j + hh)] = [None, rd]
        whi = [0]

        def wh_load(view, nk, cols, nh=1):
            if nh == 2 and whi[0] % 2 == 1:
                whi[0] += 1
            j = whi[0] % 6
            whi[0] += nh
            base = self.wslots[j // 2]
            off = (j % 2) * 1024
            dst = base[:, off:off + nk * cols].rearrange("p (k c) -> p k c", k=nk)
            keys = [("wh", j + t_) for t_ in range(nh)]
            c.op("pool", lambda e: e.dma_start(out=dst, in_=view), writes=keys, dsem=f"wh{j}")
            return dst, keys

        def proj_tile(w, wks, tb):
            ps, pk = self.psum()
            ts = slice(tb * 512, (tb + 1) * 512)
            h = self.h
            for kc in range(8):
                c.op("pe", lambda e, kc=kc: e.matmul(ps[:, :], lhsT=w[:, kc, 0:128], rhs=h[:, kc, ts],
                                                     start=(kc == 0), stop=(kc == 7)),
                     reads=wks + [self.hk(kc, tb)], writes=[pk])
            return ps, pk

        awts = {}

        def stage_a_load(k):
            wx = wh_load(win[i, :, k * 128:(k + 1) * 128].rearrange("(k p) c -> p k c", p=128), 8, 128)
            wg = wh_load(win[i, :, 2048 + k * 128: 2048 + (k + 1) * 128].rearrange("(k p) c -> p k c", p=128), 8, 128)
            awts[k] = (wx, wg)

        def stage_a_tile(k, t):
            (wx, wxk), (wg, wgk) = awts[k]
            tb = t % 4
            ts = slice(tb * 512, (tb + 1) * 512)
            sg = sgs[k % 2]
            if t < 4:
                ps, pk = proj_tile(wx, wxk, tb)
                c.op("dve", lambda e: e.tensor_copy(out=xrb[:, 2 + tb * 512: 2 + (tb + 1) * 512], in_=ps[:, :]),
                     reads=[pk], writes=[("xrb", tb)])
            else:
                ps, pk = proj_tile(wg, wgk, tb)
                tg, tgk = trr.next()
                c.op("act", lambda e: e.activation(out=tg[:, :], in_=ps[:, :], func=AF.Tanh, scale=0.5),
                     reads=[pk], writes=[tgk])
                c.op("dve", lambda e: e.scalar_tensor_tensor(out=sg[:, ts], in0=tg[:, :], scalar=1.0, in1=ps[:, :],
                                                             op0=ALU.add, op1=ALU.mult),
                     reads=[pk, tgk], writes=[("sg", k % 2, tb)])
            if t == 7:
                awts.pop(k)

        def stage_b(k):
            gw = gwb[k % 2]
            gwk = [("gw", k % 2)]
            c.op("pool", lambda e, gw=gw: e.dma_start(out=gw[:, :, :], in_=d["gate_w"][i, k, :, :, :]),
                 writes=gwk, dsem=f"gw{k % 2}")
            gws[k] = (gw, gwk)
            for jt in range(4):
                c.op("dve" if "diag_dve" in DBG else "pool", lambda e, jt=jt: e.tensor_scalar(
                    out=diag[:, jt, :], in0=identb[:, :], scalar1=cw[:, k, jt:jt + 1], scalar2=0.0,
                    op0=ALU.mult, op1=ALU.add),
                    writes=[("diag", jt)])
            for tb in range(4):
                ts = slice(tb * 512, (tb + 1) * 512)
                ps, pk = self.psum()
                rk = [("xrb", t_) for t_ in range(max(0, tb - 1), min(3, tb + 1) + 1)]
                for jt in range(4):
                    c.op("pe", lambda e, ps=ps, jt=jt, tb=tb: e.matmul(
                        ps[:, :], lhsT=diag[:, jt, :], rhs=xrb[:, tb * 512 + jt: tb * 512 + jt + 512],
                        start=(jt == 0), stop=(jt == 3)),
                        reads=[("diag", jt)] + rk, writes=[pk])
                c.op("act", lambda e, ps=ps, ts=ts: e.activation(out=xc[:, ts], in_=ps[:, :], func=AF.Identity,
                                                                 bias=cb[:, k:k + 1]),
                     reads=[pk], writes=[("xc", tb)])
                c.op("dve" if "xcb_dve" in DBG else "pool", lambda e, ts=ts: e.tensor_scalar(out=xcb[:, ts], in0=xc[:, ts], scalar1=1.0, scalar2=0.0,
                                                              op0=ALU.mult, op1=ALU.add),
                     reads=[("xc", tb)], writes=[("xcb", tb)])

        def stage_c_step(k, dr, tb):
            gw, gwk = gws[k]
            a_t, u_t, x_t = aa[dr], uu[dr], ixc[dr]
            ts = slice(tb * 512, (tb + 1) * 512)
            psa, pka = self.psum()
            psx, pkx = self.psum()
            c.op("pe", lambda e: e.matmul(psx[:, :], lhsT=gw[:, 2 + dr, :], rhs=xcb[:, ts], start=True, stop=True),
                 reads=gwk + [("xcb", tb)], writes=[pkx])
            c.op("pe", lambda e: e.matmul(psa[:, :], lhsT=gw[:, dr, :], rhs=xcb[:, ts], start=True, stop=True),
                 reads=gwk + [("xcb", tb)], writes=[pka])
            tr, trk = trr.next()
            ti, tik = tir.next()
            c.op("act", lambda e: e.activation(out=ti[:, :], in_=psx[:, :], func=AF.Tanh, scale=0.5,
                                               bias=hbx[:, dr, k:k + 1]),
                 reads=[pkx], writes=[tik])
            c.op("act", lambda e: e.activation(out=tr[:, :], in_=psa[:, :], func=AF.Tanh, scale=0.5,
                                               bias=hba[:, dr, k:k + 1]),
                 reads=[pka], writes=[trk])
            c.op("act", lambda e: e.activation(out=a_t[:, ts], in_=tr[:, :], func=AF.Exp, scale=cch[:, dr, k:k + 1],
                                               bias=cch[:, dr, k:k + 1]),
                 reads=[trk], writes=[("a", dr, tb)])
            c.op("act", lambda e: e.activation(out=u_t[:, ts], in_=tr[:, :], func=AF.Exp, scale=cc1[:, dr, k:k + 1],
                                               bias=cc1[:, dr, k:k + 1]),
                 reads=[trk], writes=[("u", dr, tb)])
            c.op("dve" if "i_dve" in DBG else "pool", lambda e: e.tensor_scalar(out=ti[:, :], in0=ti[:, :], scalar1=0.5, scalar2=0.5,
                                                   op0=ALU.mult, op1=ALU.add),
                 reads=[tik], writes=[tik])
            c.op("dve" if "i_dve" in DBG else "pool", lambda e: e.tensor_tensor(out=x_t[:, ts], in0=ti[:, :], in1=xc[:, ts], op=ALU.mult),
                 reads=[tik, ("xc", tb)], writes=[("ixc", dr, tb)])
            if dr == 1 and tb == 0:
                gws.pop(k)

        ks = lambda nm, dr: [(nm, dr, tb) for tb in range(4)]

        def d_sqrt(dr):
            u_t = uu[dr]
            c.op("act", lambda e: e.activation(out=u_t[:, :], in_=u_t[:, :], func=AF.Sqrt, scale=-1.0, bias=1.0),
                 reads=ks("u", dr), writes=ks("u", dr))

        def u_piece(dr, tb):
            u_t, x_t = uu[dr], ixc[dr]
            lo, hi = tb * 512, (tb + 1) * 512
            c.op("pool", lambda e: e.tensor_tensor(out=u_t[:, lo:hi], in0=u_t[:, lo:hi], in1=x_t[:, lo:hi], op=ALU.mult),
                 reads=[("u", dr, tb), ("ixc", dr, tb)], writes=[("u", dr, tb)])

        def d_piece(dr, tb):
            a_t, u_t, x_t = aa[dr], uu[dr], ixc[dr]
            lo, hi = tb * 512, (tb + 1) * 512
            if dr == 0:
                init = 0.0 if tb == 0 else x_t[:, lo - 1:lo]
                rd = [("u", dr, tb), ("a", dr, tb)] + ([("ixc", dr, tb - 1)] if tb > 0 else [])
                c.op("dve", lambda e: e.tensor_tensor_scan(
                    out=x_t[:, lo:hi], data0=a_t[:, lo:hi], data1=u_t[:, lo:hi], initial=init,
                    op0=ALU.mult, op1=ALU.add),
                    reads=rd, writes=[("ixc", dr, tb)])
            else:
                init = 0.0 if tb == 3 else x_t[:, hi:hi + 1]
                rd = [("u", dr, tb), ("a", dr, tb)] + ([("ixc", dr, tb + 1)] if tb < 3 else [])
                c.op("dve", lambda e: e.tensor_tensor_scan(
                    out=x_t[:, lo:hi][:, ::-1], data0=a_t[:, lo:hi][:, ::-1], data1=u_t[:, lo:hi][:, ::-1], initial=init,
                    op0=ALU.mult, op1=ALU.add),
                    reads=rd, writes=[("ixc", dr, tb)])

        def yz_piece(k, tb):
            sg = sgs[k % 2]
            ts = slice(tb * 512, (tb + 1) * 512)
            c.op("dve", lambda e: e.scalar_tensor_tensor(out=ixc[0][:, ts], in0=ixc[0][:, ts], scalar=1.0, in1=ixc[1][:, ts],
                                                         op0=ALU.mult, op1=ALU.add),
                 reads=[("ixc", 0, tb), ("ixc", 1, tb)], writes=[("ixc", 0, tb)])
            c.op("dve", lambda e: e.scalar_tensor_tensor(out=z[:, k % 2, ts], in0=ixc[0][:, ts], scalar=0.5, in1=sg[:, ts],
                                                         op0=ALU.mult, op1=ALU.mult),
                 reads=[("ixc", 0, tb), ("sg", k % 2, tb)], writes=[("z", k % 2, tb)])

        def w_tiles(kp):
            r0 = kp * 256
            view = d["odd_w_out"][i][r0:r0 + 256, :].rearrange("(k p) c -> p k c", p=128)
            w, wk = wh_load(view, 2, 1024, nh=2)
            tiles = []
            for tb in (3, 2, 1, 0):
                for n in range(8):
                    def tile(n=n, tb=tb):
                        ts = slice(tb * 512, (tb + 1) * 512)
                        ps, pk = self.psum()
                        for ec in range(2):
                            c.op("pe", lambda e, ec=ec: e.matmul(
                                ps[:, :], lhsT=w[:, ec, n * 128:(n + 1) * 128], rhs=z[:, ec, ts],
                                start=(ec == 0), stop=(ec == 1)),
                                reads=wk + [("z", ec, tb)], writes=[pk])
                        c.op("dve", lambda e: e.tensor_tensor(out=x[:, n, ts], in0=ps[:, :], in1=x[:, n, ts], op=ALU.add),
                             reads=[pk, self.xk(n, tb)], writes=[self.xk(n, tb)])
                    tiles.append(tile)
            return tiles

        NBLK = 16
        stage_a_load(0)
        stage_a_load(1)
        for t in range(8):
            stage_a_tile(0, t)
        stage_b(0)
        wq = []
        for k in range(NBLK):
            if k + 2 < NBLK:
                stage_a_load(k + 2)
            for s_ in range(8):
                if s_ < 4:
                    tbp = 3 - s_
                    if k >= 1:
                        if tbp > 0:
                            u_piece(1, tbp - 1)
                        d_piece(1, tbp)
                        yz_piece(k - 1, tbp)
                    stage_c_step(k, 0, tbp)
                else:
                    if s_ == 4 and k % 2 == 0 and k >= 2:
                        wq.extend(w_tiles(k // 2 - 1))
                    if s_ == 4:
                        d_sqrt(0)
                        u_piece(0, 0)
                    if s_ < 7:
                        u_piece(0, s_ - 3)
                    d_piece(0, s_ - 4)
                    stage_c_step(k, 1, 7 - s_)
                if k + 1 < NBLK:
                    stage_a_tile(k + 1, s_)
                for _ in range(5):
                    if wq:
                        wq.pop(0)()
            d_sqrt(1)
            u_piece(1, 3)
            if k + 1 < NBLK:
                stage_b(k + 1)
        for s_ in range(4):
            if s_ < 3:
                u_piece(1, 2 - s_)
            d_piece(1, 3 - s_)
            yz_piece(NBLK - 1, 3 - s_)
        while wq:
            wq.pop(0)()
        for tile in w_tiles(NBLK // 2 - 1):
            tile()
        for j in range(3):
            rd = {}
            for hh in range(2):
                o = st.pop(("wh", 2 * j + hh), None)
                if o is None:
                    continue
                for sem, val in o[1].items():
                    if rd.get(sem, 0) < val:
                        rd[sem] = val
                if o[0] is not None and rd.get(o[0][0], 0) < o[0][1]:
                    rd[o[0][0]] = o[0][1]
            st[("w", j)] = [None, rd]


def _constants():
    ident = np.eye(128, dtype=np.float32)
    perm = np.zeros((128, 128), np.float32)
    for mm_ in range(128):
        base = (mm_ // 64) * 64
        r = mm_ - base
        kk = base + (r + 32 if r < 32 else r - 32)
        perm[kk, mm_] = 1.0
    t = np.arange(S)
    row = (t // 64).astype(np.float32)
    col = (t % 64).astype(np.float32)
    inv_freq = (np.float32(10000.0) ** (-np.arange(0, 64, 2, dtype=np.float32) / np.float32(64))).astype(np.float32)
    ang_r = row[:, None] * inv_freq[None, :]
    ang_c = col[:, None] * inv_freq[None, :]
    cos = np.zeros((128, S), np.float32)
    sin = np.zeros((128, S), np.float32)
    cos[0:32] = np.cos(ang_r).T
    cos[32:64] = np.cos(ang_r).T
    cos[64:96] = np.cos(ang_c).T
    cos[96:128] = np.cos(ang_c).T
    sin[0:32] = -np.sin(ang_r).T
    sin[32:64] = np.sin(ang_r).T
    sin[64:96] = -np.sin(ang_c).T
    sin[96:128] = np.sin(ang_c).T
    ss = np.arange(S, dtype=np.int64)
    ph = (np.outer(ss, ss) % S).astype(np.float64) * (2.0 * np.pi / S)
    dftc = (np.cos(ph) / math.sqrt(S)).astype(ml_dtypes.bfloat16)
    dfts = (-np.sin(ph) / math.sqrt(S)).astype(ml_dtypes.bfloat16)
    cs = np.arange(256, dtype=np.int64)
    phc = (np.outer(cs, cs) % 256).astype(np.float64) * (2.0 * np.pi / 256)
    cc = (np.cos(phc) / 16.0).astype(ml_dtypes.bfloat16)
    sc = (np.sin(phc) / 16.0).astype(ml_dtypes.bfloat16)
    return {"c_ident": ident, "c_perm": perm, "c_cos": cos, "c_sin": sin,
            "c_dftc": dftc, "c_dfts": dfts, "c_cc": cc, "c_sc": sc}


def _layout(inputs):
    f = lambda a: np.ascontiguousarray(np.asarray(a, dtype=np.float32))
    o = {}
    o["even_norm"] = f(np.asarray(inputs["even_norm"]).reshape(2, 8, 128).transpose(0, 2, 1))
    o["odd_norm"] = f(np.asarray(inputs["odd_norm"]).reshape(2, 8, 128).transpose(0, 2, 1))
    o["q_gain"] = f(np.asarray(inputs["q_gain"]).reshape(2, 128, 1))
    o["k_gain"] = f(np.asarray(inputs["k_gain"]).reshape(2, 128, 1))
    o["conv_w"] = f(np.asarray(inputs["conv_w"]).reshape(2, 4, 16, 128).transpose(0, 3, 2, 1))
    o["conv_b"] = f(np.asarray(inputs["conv_b"]).reshape(2, 16, 128).transpose(0, 2, 1))
    for nm in ("gate_a_b", "gate_x_b"):
        o[nm] = f(np.asarray(inputs[nm]).transpose(0, 3, 1, 2))
    o["rglru_lambda"] = f(np.asarray(inputs["rglru_lambda"]).reshape(2, 2, 16, 128).transpose(0, 3, 1, 2))
    for nm in ("even_w_in", "fourier_w", "even_w_out", "odd_w_in", "odd_w_out", "final_norm"):
        o[nm] = f(inputs[nm])
    gw = np.stack([np.asarray(inputs["gate_a_w"]), np.asarray(inputs["gate_x_w"])], axis=0)
    o["gate_w"] = f(gw.transpose(1, 3, 4, 0, 2, 5).reshape(2, 16, 128, 4, 128))
    return o


_CACHE = {}


def _program(layers, do_final):
    key = (tuple(layers), do_final)
    if key not in _CACHE:
        _CACHE[key] = Builder(list(layers), do_final).nc
    return _CACHE[key]


def run_layers(x, shared, layers, do_final, core_ids):
    nc = _program(layers, do_final)
    in_maps = []
    for ci in core_ids:
        mp = dict(shared)
        mp["x"] = np.ascontiguousarray(x[ci * NB:(ci + 1) * NB])
        in_maps.append(mp)
    res = run_bass_kernel_spmd(nc, in_maps, core_ids=list(range(len(core_ids))))
    return [r["y"] for r in res.results]


def kernel(**inputs):
    x = np.asarray(inputs["x"], dtype=np.float32)
    shared = _layout(inputs)
    shared.update(_constants())
    outs = run_layers(x, shared, [0, 1, 2, 3], True, list(range(NCORES)))
    return np.concatenate(outs, axis=0).astype(np.float32)
```

```python
import math
from contextlib import ExitStack

import numpy as np
import ml_dtypes

import concourse.bass as bass
import concourse.mybir as mybir
from concourse.bass_utils import run_bass_kernel_spmd

F32 = mybir.dt.float32
BF16 = mybir.dt.bfloat16
AF = mybir.ActivationFunctionType
ALU = mybir.AluOpType

S = 2048
D = 1024
NCORES = 8
NB = 2
EPS = 1e-6
ATT_SCALE = 128 ** -0.5
ENGS = ("pe", "act", "dve", "pool", "sp")
ATTACH = True
NPS = 8
DBG = set()
NTT = 16


class Sched:
    def __init__(self):
        self.ops = {e: [] for e in ENGS}
        self.n = {e: 0 for e in ENGS}
        self.seen = {e: {} for e in ENGS}
        self.st = {}
        self.dcnt = {}
        self.window = 1 << 30

    def _need(self, eng, ev, raw):
        sem, val = ev
        if sem == eng:
            if eng == "pe" or not raw:
                return False
            return self.n[eng] - val < self.window
        return self.seen[eng].get(sem, 0) < val

    def op(self, eng, fn, reads=(), writes=(), dsem=None, attach=True):
        waits = {}

        def add(ev, raw):
            if ev is not None and self._need(eng, ev, raw):
                if waits.get(ev[0], 0) < ev[1]:
                    waits[ev[0]] = ev[1]

        for k in reads:
            s = self.st.get(k)
            if s is not None:
                add(s[0], True)
                if k[0] == "ps":
                    for sem, val in s[1].items():
                        if sem != eng:
                            add((sem, val), False)
        for k in writes:
            s = self.st.get(k)
            if s is not None:
                add(s[0], False)
                for sem, val in s[1].items():
                    add((sem, val), False)
        for sem, val in waits.items():
            if self.seen[eng].get(sem, 0) < val:
                self.seen[eng][sem] = val
        if dsem is None:
            self.n[eng] += 1
            ev = (eng, self.n[eng])
        else:
            self.dcnt[dsem] = self.dcnt.get(dsem, 0) + 16
            ev = (dsem, self.dcnt[dsem])
        self.ops[eng].append((list(waits.items()), fn, ev, ATTACH and attach))
        for k in reads:
            s = self.st.setdefault(k, [None, {}])
            if s[1].get(ev[0], 0) < ev[1]:
                s[1][ev[0]] = ev[1]
        for k in writes:
            self.st[k] = [ev, {}]
        return ev

    def barrier(self, engines=ENGS, clear=False):
        evs = [(e, self.n[e]) for e in ENGS if self.n[e] > 0]
        evs += list(self.dcnt.items())
        for e in engines:
            waits = []
            for sem, val in evs:
                if sem == e:
                    if e != "pe":
                        waits.append((sem, val))
                elif self.seen[e].get(sem, 0) < val:
                    waits.append((sem, val))
                    self.seen[e][sem] = val
            if waits:
                self.ops[e].append((waits, None, None, False))
        if clear:
            self.st = {k: v for k, v in self.st.items() if k[0] in ("w", "wh")}

    def final_wait(self, eng, events):
        waits = [(s, v) for s, v in events]
        self.ops[eng].append((waits, None, None, False))

    def emit(self, nc, sems):
        with nc.Block() as block:
            decos = {"pe": block.tensor, "act": block.scalar, "dve": block.vector,
                     "pool": block.gpsimd, "sp": block.sync}
            for e in ENGS:
                ops = self.ops[e]

                def body(engine, ops=ops):
                    for waits, fn, ev, attach in ops:
                        if fn is None:
                            for s, v in waits:
                                engine.wait_ge(sems[s], v)
                            continue
                        rest = waits
                        first = None
                        if attach and waits:
                            first = waits[0]
                            rest = waits[1:]
                        for s, v in rest:
                            engine.wait_ge(sems[s], v)
                        ins = fn(engine)
                        if first is not None:
                            ins._wait_ge(sems[first[0]], first[1])
                        ins.then_inc(sems[ev[0]], 1 if ev[0] in ENGS else 16)

                decos[e](body)


class Mem:
    def __init__(self, nc, limit):
        self.nc, self.off, self.limit, self.cnt = nc, 16512, limit, 0

    def alloc(self, name, free_shape, dtype):
        nbytes = int(np.prod(free_shape)) * (4 if dtype == F32 else 2)
        off = (self.off + 31) // 32 * 32
        assert off + nbytes <= self.limit, f"SBUF overflow for {name}: {off}+{nbytes} > {self.limit}"
        self.cnt += 1
        t = self.nc.alloc_sbuf_tensor_at(f"{name}_{self.cnt}", [128] + list(free_shape), dtype, offset=off)
        self.off = off + nbytes
        return t


class Ring:
    def __init__(self, mem, name, n, free_shape, dtype):
        self.t = [mem.alloc(f"{name}{i}", free_shape, dtype) for i in range(n)]
        self.name, self.i = name, 0

    def next(self):
        j = self.i % len(self.t)
        self.i += 1
        return self.t[j], (self.name, j)


DRAM_INPUTS = [
    ("x", [NB, S, D], F32),
    ("even_norm", [2, 128, 8], F32), ("even_w_in", [2, 1024, 4608], F32),
    ("fourier_w", [2, 4, 256, 256], F32), ("q_gain", [2, 128, 1], F32), ("k_gain", [2, 128, 1], F32),
    ("even_w_out", [2, 2048, 1024], F32),
    ("odd_norm", [2, 128, 8], F32), ("odd_w_in", [2, 1024, 4096], F32),
    ("conv_w", [2, 128, 16, 4], F32), ("conv_b", [2, 128, 16], F32),
    ("gate_w", [2, 16, 128, 4, 128], F32), ("gate_a_b", [2, 128, 2, 16], F32),
    ("gate_x_b", [2, 128, 2, 16], F32),
    ("rglru_lambda", [2, 128, 2, 16], F32), ("odd_w_out", [2, 2048, 1024], F32),
    ("final_norm", [1024], F32),
    ("c_ident", [128, 128], F32), ("c_perm", [128, 128], F32),
    ("c_cos", [128, S], F32), ("c_sin", [128, S], F32),
    ("c_dftc", [S, S], BF16), ("c_dfts", [S, S], BF16),
    ("c_cc", [256, 256], BF16), ("c_sc", [256, 256], BF16),
]


class Builder:
    def __init__(self, layers, do_final):
        self.layers = layers
        self.do_final = do_final
        nc = self.nc = bass.Bass("TRN2", target_bir_lowering=False)
        self.d = {}
        for name, shape, dt in DRAM_INPUTS:
            self.d[name] = nc.dram_tensor(name, shape, dt, kind="ExternalInput").ap()
        self.y = nc.dram_tensor("y", [NB, S, D], F32, kind="ExternalOutput").ap()
        self.c = Sched()
        limit = 229376
        self.mem = m = Mem(nc, limit)
        self.x = m.alloc("x", [8, S], F32)
        self.h = m.alloc("h", [8, S], BF16)
        self.ident_f = m.alloc("identf", [128], F32)
        self.ident_b = m.alloc("identb", [128], BF16)
        self.ones_b = m.alloc("onesb", [128], BF16)
        self.perm_b = m.alloc("permb", [128], BF16)
        self.gcol = m.alloc("gcol", [8], F32)
        self.qg = m.alloc("qg", [1], F32)
        self.kg = m.alloc("kg", [1], F32)
        self.cw = m.alloc("cw", [16, 4], F32)
        self.cb = m.alloc("cb", [16], F32)
        self.hba = m.alloc("hba", [2, 16], F32)
        self.hbx = m.alloc("hbx", [2, 16], F32)
        self.lam = m.alloc("lam", [2, 16], F32)
        self.cc1 = m.alloc("cc1", [2, 16], F32)
        self.cch = m.alloc("cch", [2, 16], F32)
        self.wslots = [m.alloc(f"w{i}", [2048], BF16) for i in range(3)]
        self.wi = 0
        self.arena0 = m.off
        self.psw = [nc.alloc_psum_tensor(f"psw{i}", [128, 1024], F32) for i in range(4)]
        self.ps = [self.psw[i // 2][:, (i % 2) * 512:(i % 2 + 1) * 512] for i in range(8)]
        self.psi = 0
        self.out_events = []
        self.build()
        with ExitStack() as es:
            sems = {}
            for e in ENGS:
                sems[e] = es.enter_context(nc.semaphore(f"s_{e}"))
            for dn in self.c.dcnt:
                sems[dn] = es.enter_context(nc.semaphore(f"d_{dn}"))
            self.c.emit(nc, sems)

    def psum(self):
        i = self.psi % NPS
        self.psi += 1
        return self.ps[i], ("ps", i)

    def arena_reset(self, full=False):
        self.c.barrier(engines=ENGS if full else ("pe", "act", "dve", "sp"), clear=True)
        self.mem.off = self.arena0

    def wring(self, n):
        pass

    def wload(self, dram_view, nk, cols):
        j = self.wi % len(self.wslots)
        self.wi += 1
        t = self.wslots[j]
        dst = t[:, 0:nk * cols].rearrange("p (k c) -> p k c", k=nk)
        self.c.op("pool", lambda e: e.dma_start(out=dst, in_=dram_view), writes=[("w", j)], dsem=f"w{j}")
        return dst, ("w", j)

    def dma_in(self, out_ap, in_ap, key, dsem, eng="sp"):
        self.c.op(eng, lambda e: e.dma_start(out=out_ap, in_=in_ap), writes=[key], dsem=dsem)

    def xk(self, kc, tb):
        return ("x", kc, tb)

    def hk(self, kc, tb):
        return ("h", kc, tb)

    def build(self):
        c = self.c
        self.dma_in(self.ident_f[:, :], self.d["c_ident"][:, :], ("identf",), "c0")
        self.dma_in(self.ident_b[:, :], self.d["c_ident"][:, :], ("identb",), "c1", eng="pool")
        self.dma_in(self.perm_b[:, :], self.d["c_perm"][:, :], ("permb",), "c2", eng="pool")
        ones_b = self.ones_b
        c.op("dve", lambda e: e.memset(ones_b[:, :], 1.0), writes=[("onesb",)])
        c.barrier(clear=False)
        for b in range(NB):
            self.phase_load_x(b)
            for L in self.layers:
                if L % 2 == 0:
                    self.even_layer(L // 2)
                else:
                    self.odd_layer(L // 2)
            self.phase_store(b)
        c.barrier(engines=("sp",))
        c.final_wait("sp", self.out_events)

    def phase_load_x(self, b):
        c = self.c
        self.arena_reset()
        stage = Ring(self.mem, "xin", 4, [D], F32)
        x, ident = self.x, self.ident_f
        for tt in range(NTT):
            st, sk = stage.next()
            self.dma_in(st[:, :], self.d["x"][b, tt * 128:(tt + 1) * 128, :], sk, f"xin{sk[1]}")
            for half in range(2):
                ps, pk = self.psum()
                for j in range(4):
                    kc = half * 4 + j
                    c.op("pe", lambda e, ps=ps, st=st, j=j, kc=kc: e.transpose(
                        ps[:, j * 128:(j + 1) * 128], st[:, kc * 128:(kc + 1) * 128], ident[:, :]),
                        reads=[sk, ("identf",)], writes=[pk])
                dst = x[:, half * 4:half * 4 + 4, tt * 128:(tt + 1) * 128]
                src = ps[:, :].rearrange("p (a b) -> p a b", a=4)
                wk = [self.xk(half * 4 + j, tt // 4) for j in range(4)]
                if half == 0:
                    c.op("act", lambda e, dst=dst, src=src: e.activation(out=dst, in_=src, func=AF.Copy),
                         reads=[pk], writes=wk)
                else:
                    c.op("dve", lambda e, dst=dst, src=src: e.tensor_copy(out=dst, in_=src),
                         reads=[pk], writes=wk)

    def phase_store(self, b):
        c = self.c
        self.arena_reset()
        m = self.mem
        stage = Ring(m, "ot", 4, [D], F32)
        x, ident = self.x, self.ident_f
        if self.do_final:
            gfin = m.alloc("gfin", [D], F32)
            junk = m.alloc("junk", [D], BF16)
            ssr = Ring(m, "ss", 4, [1], F32)
            sdr = Ring(m, "sd", 4, [1], F32)
            rsr = Ring(m, "rs", 4, [1], F32)
            self.dma_in(gfin[:, :], self.d["final_norm"].partition_broadcast(128), ("gfin",), "gfin")
        for tt in range(16):
            ot, ok = stage.next()
            for half in range(2):
                ps, pk = self.psum()
                for j in range(4):
                    kc = half * 4 + j
                    c.op("pe", lambda e, ps=ps, j=j, kc=kc, tt=tt: e.transpose(
                        ps[:, j * 128:(j + 1) * 128], x[:, kc, tt * 128:(tt + 1) * 128], ident[:, :]),
                        reads=[self.xk(kc, tt // 4), ("identf",)], writes=[pk])
                dst = ot[:, half * 512:(half + 1) * 512]
                if half == 0 and "noactcopy" not in DBG:
                    c.op("act", lambda e, dst=dst, ps=ps: e.activation(out=dst, in_=ps[:, :], func=AF.Copy),
                         reads=[pk], writes=[ok])
                else:
                    c.op("dve", lambda e, dst=dst, ps=ps: e.tensor_copy(out=dst, in_=ps[:, :]),
                         reads=[pk], writes=[ok])
            if self.do_final:
                ss, ssk = ssr.next()
                sd, sdk = sdr.next()
                rs, rsk = rsr.next()
                c.op("act", lambda e, ot=ot, ss=ss: e.activation(out=junk[:, :], in_=ot[:, :], func=AF.Square,
                                                                accum_out=ss[:, 0:1]),
                     reads=[ok], writes=[("junk",), ssk], attach=False)
                c.op("act", lambda e, sd=sd, ss=ss: e.activation(out=sd[:, :], in_=ss[:, :], func=AF.Sqrt,
                                                                scale=1.0 / D, bias=EPS),
                     reads=[ssk], writes=[sdk])
                c.op("dve", lambda e, rs=rs, sd=sd: e.reciprocal(out=rs[:, :], in_=sd[:, :]),
                     reads=[sdk], writes=[rsk])
                c.op("dve", lambda e, ot=ot, rs=rs: e.scalar_tensor_tensor(
                    out=ot[:, :], in0=ot[:, :], scalar=rs[:, 0:1], in1=gfin[:, :], op0=ALU.mult, op1=ALU.mult),
                    reads=[ok, rsk, ("gfin",)], writes=[ok])
            ev = c.op("sp", lambda e, ot=ot, tt=tt, b=b: e.dma_start(out=self.y[b, tt * 128:(tt + 1) * 128, :], in_=ot[:, :]),
                      reads=[ok], dsem=f"out{ok[1]}")
            self.out_events = [x_ for x_ in self.out_events if x_[0] != ev[0]] + [ev]

    def phase_norm(self, gsrc):
        c = self.c
        self.arena_reset()
        m = self.mem
        x, h, gcol, ones = self.x, self.h, self.gcol, self.ones_b
        self.dma_in(gcol[:, :], gsrc, ("gcol",), "gcol")
        sqr = Ring(m, "sq", 4, [512], BF16)
        sdr = Ring(m, "sd", 2, [512], F32)
        rsr = Ring(m, "rs", 2, [512], F32)
        for tb in range(4):
            ts = slice(tb * 512, (tb + 1) * 512)
            ps, pk = self.psum()
            for kc in range(8):
                sq, sqk = sqr.next()
                c.op("act", lambda e, sq=sq, kc=kc, ts=ts: e.activation(out=sq[:, :], in_=x[:, kc, ts], func=AF.Square),
                     reads=[self.xk(kc, tb)], writes=[sqk])
                c.op("pe", lambda e, ps=ps, sq=sq, kc=kc: e.matmul(ps[:, :], lhsT=ones[:, :], rhs=sq[:, :],
                                                                   start=(kc == 0), stop=(kc == 7)),
                     reads=[sqk, ("onesb",)], writes=[pk])
            sd, sdk = sdr.next()
            rs, rsk = rsr.next()
            c.op("act", lambda e, sd=sd, ps=ps: e.activation(out=sd[:, :], in_=ps[:, :], func=AF.Ln,
                                                            scale=1.0 / D, bias=EPS),
                 reads=[pk], writes=[sdk])
            c.op("act", lambda e, rs=rs, sd=sd: e.activation(out=rs[:, :], in_=sd[:, :], func=AF.Exp, scale=-0.5),
                 reads=[sdk], writes=[rsk])
            for kc in range(8):
                c.op("dve", lambda e, kc=kc, ts=ts, rs=rs: e.scalar_tensor_tensor(
                    out=h[:, kc, ts], in0=x[:, kc, ts], scalar=gcol[:, kc:kc + 1], in1=rs[:, :],
                    op0=ALU.mult, op1=ALU.mult),
                    reads=[self.xk(kc, tb), rsk, ("gcol",)], writes=[self.hk(kc, tb)])

    def proj_fm(self, w, wk, col0, tb, ps, pk):
        h = self.h
        ts = slice(tb * 512, (tb + 1) * 512)
        for kc in range(8):
            self.c.op("pe", lambda e, kc=kc: e.matmul(ps[:, :], lhsT=w[:, kc, col0:col0 + 128], rhs=h[:, kc, ts],
                                                      start=(kc == 0), stop=(kc == 7)),
                      reads=[wk, self.hk(kc, tb)], writes=[pk])

    def wout_partial(self, wsrc, r0, z, zkey):
        c = self.c
        x = self.x
        for ch in range(2):
            view = wsrc[r0:r0 + 512, ch * 512:(ch + 1) * 512].rearrange("(k p) c -> p k c", p=128)
            w, wk = self.wload(view, 4, 512)
            for nl in range(4):
                n = ch * 4 + nl
                for tb in range(4):
                    ts = slice(tb * 512, (tb + 1) * 512)
                    ps, pk = self.psum()
                    for ec in range(4):
                        c.op("pe", lambda e, ps=ps, w=w, ec=ec, nl=nl, ts=ts: e.matmul(
                            ps[:, :], lhsT=w[:, ec, nl * 128:(nl + 1) * 128], rhs=z[:, ec, ts],
                            start=(ec == 0), stop=(ec == 3)),
                            reads=[wk, zkey(ec, tb)], writes=[pk])
                    c.op("dve", lambda e, ps=ps, n=n, ts=ts: e.tensor_tensor(
                        out=x[:, n, ts], in0=ps[:, :], in1=x[:, n, ts], op=ALU.add),
                        reads=[pk, self.xk(n, tb)], writes=[self.xk(n, tb)])

    def gate_silu(self, wsrc_cols, z, zkey):
        c = self.c
        for u in range(2):
            w, wk = self.wload(wsrc_cols(u), 8, 256)
            for el in range(2):
                ec = u * 2 + el
                for tb in range(4):
                    ts = slice(tb * 512, (tb + 1) * 512)
                    ps, pk = self.psum()
                    self.proj_fm(w, wk, el * 128, tb, ps, pk)
                    c.op("act", lambda e, ps=ps, ec=ec, ts=ts: e.activation(out=z[:, ec, ts], in_=ps[:, :], func=AF.Silu),
                         reads=[pk], writes=[zkey(ec, tb)])

    def even_layer(self, i):
        d = self.d
        self.phase_norm(d["even_norm"][i, :, :])
        for p in range(2):
            self.phase_fourier(i, p)
        for g in range(2):
            self.phase_attn(i, g)

    def phase_fourier(self, i, p):
        c = self.c
        d = self.d
        m = self.mem
        h = self.h
        if p == 0:
            self.arena_reset()
            z = m.alloc("z", [4, S], BF16)
            AB = m.alloc("AB", [16, 1024], BF16)
            tabC = [m.alloc(f"tabC{j}", [16, 256], BF16) for j in range(2)]
            tabS = [m.alloc(f"tabS{j}", [16, 256], BF16) for j in range(2)]
            finr = Ring(m, "fin", 2, [2, 512], BF16)
            MAB = m.alloc("MAB", [2, 2, 512], BF16)
            ccb = m.alloc("ccb", [2, 256], BF16)
            scb = m.alloc("scb", [2, 256], BF16)
            self.dma_in(ccb[:, :, :], d["c_cc"].rearrange("(k p) c -> p k c", p=128), ("ccb",), "ccb")
            self.dma_in(scb[:, :, :], d["c_sc"].rearrange("(k p) c -> p k c", p=128), ("scb",), "scb")
            self._F = (z, AB, tabC, tabS, finr, MAB, ccb, scb)
        else:
            z, AB, tabC, tabS, finr, MAB, ccb, scb = self._F
        self.wring(3)
        zkey = lambda ec, tb: ("z", ec, tb)
        win = d["even_w_in"]
        for gl in range(2):
            g = 2 * p + gl
            fw, fwk = self.wload(d["fourier_w"][i, g, :, :].rearrange("(k p) c -> p k c", p=128), 2, 256)
            for mb in range(2):
                ps, pk = self.psum()
                for which, src, sk in ((0, ccb, ("ccb",)), (1, scb, ("scb",))):
                    for kc in range(2):
                        c.op("pe", lambda e, ps=ps, src=src, kc=kc, mb=mb, fw=fw, which=which: e.matmul(
                            ps[:, which * 256:(which + 1) * 256], lhsT=src[:, kc, mb * 128:(mb + 1) * 128],
                            rhs=fw[:, kc, :], start=(kc == 0), stop=(kc == 1)),
                            reads=[sk, fwk], writes=[pk])
                c.op("act", lambda e, ps=ps, gl=gl, mb=mb: e.activation(out=MAB[:, gl, mb, :], in_=ps[:, :], func=AF.Copy),
                     reads=[pk], writes=[("MAB", gl)])
        self.gate_silu(lambda u: win[i, :, 1024 + p * 512 + u * 256: 1024 + p * 512 + (u + 1) * 256]
                       .rearrange("(k p) c -> p k c", p=128), z, zkey)
        for gl in range(2):
            g = 2 * p + gl
            w, wk = self.wload(win[i, :, g * 256:(g + 1) * 256].rearrange("(k p) c -> p k c", p=128), 8, 256)
            for tb in range(4):
                fin, fk = finr.next()
                for half in range(2):
                    ps, pk = self.psum()
                    self.proj_fm(w, wk, half * 128, tb, ps, pk)
                    if half == 0:
                        c.op("act", lambda e, ps=ps, fin=fin: e.activation(out=fin[:, 0, :], in_=ps[:, :], func=AF.Copy),
                             reads=[pk], writes=[fk])
                    else:
                        c.op("dve", lambda e, ps=ps, fin=fin: e.tensor_copy(out=fin[:, 1, :], in_=ps[:, :]),
                             reads=[pk], writes=[fk])
                for tq in range(4):
                    tt = tb * 4 + tq
                    ps, pk = self.psum()
                    for half in range(2):
                        c.op("pe", lambda e, ps=ps, fin=fin, half=half, tq=tq, gl=gl: e.matmul(
                            ps[:, :], lhsT=fin[:, half, tq * 128:(tq + 1) * 128], rhs=MAB[:, gl, half, :],
                            start=(half == 0), stop=(half == 1)),
                            reads=[fk, ("MAB", gl)], writes=[pk])
                    dst = AB[:, tt, gl * 512:(gl + 1) * 512]
                    if tq % 2 == 0:
                        c.op("act", lambda e, ps=ps, dst=dst: e.activation(out=dst, in_=ps[:, :], func=AF.Copy),
                             reads=[pk], writes=[("AB", tt, gl)])
                    else:
                        c.op("dve", lambda e, ps=ps, dst=dst: e.tensor_copy(out=dst, in_=ps[:, :]),
                             reads=[pk], writes=[("AB", tt, gl)])
        for sb in range(8):
            tC, tS = tabC[sb % 2], tabS[sb % 2]
            kC, kS = ("tabC", sb % 2), ("tabS", sb % 2)
            self.dma_in(tC[:, :, :], d["c_dftc"][:, sb * 256:(sb + 1) * 256].rearrange("(j p) c -> p j c", p=128),
                        kC, f"tabC{sb % 2}")
            self.dma_in(tS[:, :, :], d["c_dfts"][:, sb * 256:(sb + 1) * 256].rearrange("(j p) c -> p j c", p=128),
                        kS, f"tabS{sb % 2}")
            ss = slice(sb * 256, (sb + 1) * 256)
            for ec in range(4):
                gl, half = ec // 2, ec % 2
                ps, pk = self.psum()
                for which, tab, tk in ((0, tC, kC), (1, tS, kS)):
                    c0 = gl * 512 + which * 256 + half * 128
                    for j in range(16):
                        c.op("pe", lambda e, ps=ps, tab=tab, j=j, c0=c0, which=which: e.matmul(
                            ps[:, 0:256], lhsT=AB[:, j, c0:c0 + 128], rhs=tab[:, j, :],
                            start=(which == 0 and j == 0), stop=(which == 1 and j == 15)),
                            reads=[("AB", j, gl), tk], writes=[pk])
                c.op("dve", lambda e, ps=ps, ec=ec, ss=ss: e.tensor_tensor(
                    out=z[:, ec, ss], in0=ps[:, 0:256], in1=z[:, ec, ss], op=ALU.mult),
                    reads=[pk, zkey(ec, sb // 2)], writes=[zkey(ec, sb // 2)])
        self.wout_partial(d["even_w_out"][i], p * 512, z, zkey)

    def normrope(self, ps, pk, gc, gk, out_ap, out_key, tb, R):
        c = self.c
        ts = slice(tb * 512, (tb + 1) * 512)
        ones, perm = self.ones_b, self.perm_b
        cos, sin = R["cos"], R["sin"]
        sq, sqk = R["sq"].next()
        qb, qbk = R["qb"].next()
        t1, t1k = R["t1"].next()
        t2, t2k = R["t2"].next()
        sd, sdk = R["sd"].next()
        rs, rsk = R["rs"].next()
        c.op("act", lambda e: e.activation(out=sq[:, :], in_=ps[:, :], func=AF.Square), reads=[pk], writes=[sqk])
        c.op("act", lambda e: e.activation(out=qb[:, :], in_=ps[:, :], func=AF.Identity, scale=gc[:, 0:1]),
             reads=[pk, gk], writes=[qbk])
        ps2, pk2 = self.psum()
        ps3, pk3 = self.psum()
        c.op("pe", lambda e: e.matmul(ps2[:, :], lhsT=ones[:, :], rhs=sq[:, :], start=True, stop=True),
             reads=[sqk, ("onesb",)], writes=[pk2])
        c.op("pe", lambda e: e.matmul(ps3[:, :], lhsT=perm[:, :], rhs=qb[:, :], start=True, stop=True),
             reads=[qbk, ("permb",)], writes=[pk3])
        c.op("dve", lambda e: e.scalar_tensor_tensor(out=t1[:, :], in0=ps[:, :], scalar=gc[:, 0:1], in1=cos[:, ts],
                                                     op0=ALU.mult, op1=ALU.mult),
             reads=[pk, gk, ("cos",)], writes=[t1k])
        c.op("act", lambda e: e.activation(out=sd[:, :], in_=ps2[:, :], func=AF.Ln, scale=1.0 / 128, bias=EPS),
             reads=[pk2], writes=[sdk])
        c.op("act", lambda e: e.activation(out=rs[:, :], in_=sd[:, :], func=AF.Exp, scale=-0.5),
             reads=[sdk], writes=[rsk])
        c.op("dve", lambda e: e.tensor_tensor(out=t2[:, :], in0=ps3[:, :], in1=sin[:, ts], op=ALU.mult),
             reads=[pk3, ("sin",)], writes=[t2k])
        c.op("pool", lambda e: e.tensor_tensor(out=t1[:, :], in0=t1[:, :], in1=t2[:, :], op=ALU.add),
             reads=[t1k, t2k], writes=[t1k])
        c.op("dve", lambda e: e.tensor_tensor(out=out_ap, in0=t1[:, :], in1=rs[:, :], op=ALU.mult),
             reads=[t1k, rsk], writes=[out_key])

    def phase_attn(self, i, g):
        c = self.c
        d = self.d
        m = self.mem
        h = self.h
        qg, kg = self.qg, self.kg
        if g == 0:
            self.arena_reset()
            z = m.alloc("z", [4, S], BF16)
            q = m.alloc("q", [4, S], BF16)
            k = m.alloc("k", [S], BF16)
            V = m.alloc("V", [16, 128], BF16)
            cos = m.alloc("cos", [S], F32)
            sin = m.alloc("sin", [S], F32)
            rdr = Ring(m, "rden", 2, [512], F32)
            ogr = Ring(m, "og", 2, [512], F32)
            off_R = m.off
            R = {"cos": cos, "sin": sin,
                 "sq": Ring(m, "sq", 2, [512], BF16), "qb": Ring(m, "qb", 2, [512], BF16),
                 "t1": Ring(m, "t1", 2, [512], F32), "t2": Ring(m, "t2", 2, [512], F32),
                 "sd": Ring(m, "sd", 2, [512], F32), "rs": Ring(m, "rs", 2, [512], F32)}
            self.dma_in(cos[:, :], d["c_cos"][:, :], ("cos",), "cos")
            self.dma_in(sin[:, :], d["c_sin"][:, :], ("sin",), "sin")
            self.dma_in(qg[:, :], d["q_gain"][i, :, :], ("qg",), "qg")
            self.dma_in(kg[:, :], d["k_gain"][i, :, :], ("kg",), "kg")
            self._A = (z, q, k, V, cos, sin, rdr, ogr, off_R, R)
        else:
            z, q, k, V, cos, sin, rdr, ogr, off_R, R = self._A
        zkey = lambda ec, tb: ("z", ec, tb)
        win = d["even_w_in"]
        a0 = 3584 + g * 512
        self.gate_silu(lambda u: win[i, :, a0 + u * 256: a0 + (u + 1) * 256].rearrange("(k p) c -> p k c", p=128),
                       z, zkey)
        w, wk = self.wload(win[i, :, 3072:3328].rearrange("(k p) c -> p k c", p=128), 8, 256)
        for tb in range(4):
            ps, pk = self.psum()
            self.proj_fm(w, wk, g * 128, tb, ps, pk)
            self.normrope(ps, pk, kg, ("kg",), k[:, tb * 512:(tb + 1) * 512], ("k", tb), tb, R)
        w, wk = self.wload(win[i, :, 3328:3584].rearrange("(k p) c -> p k c", p=128), 8, 256)
        for tt in range(16):
            ps, pk = self.psum()
            for kc in range(8):
                c.op("pe", lambda e, ps=ps, kc=kc, tt=tt, w=w: e.matmul(
                    ps[:, 0:128], lhsT=h[:, kc, tt * 128:(tt + 1) * 128], rhs=w[:, kc, g * 128:(g + 1) * 128],
                    start=(kc == 0), stop=(kc == 7)),
                    reads=[wk, self.hk(kc, tt // 4)], writes=[pk])
            if tt % 2 == 0:
                c.op("act", lambda e, ps=ps, tt=tt: e.activation(out=V[:, tt, :], in_=ps[:, 0:128], func=AF.Copy),
                     reads=[pk], writes=[("V", tt)])
            else:
                c.op("dve", lambda e, ps=ps, tt=tt: e.tensor_copy(out=V[:, tt, :], in_=ps[:, 0:128]),
                     reads=[pk], writes=[("V", tt)])
        q0 = 2048 + g * 512
        for u in range(2):
            w, wk = self.wload(win[i, :, q0 + u * 256: q0 + (u + 1) * 256].rearrange("(k p) c -> p k c", p=128), 8, 256)
            for el in range(2):
                hl = u * 2 + el
                for tb in range(4):
                    ps, pk = self.psum()
                    self.proj_fm(w, wk, el * 128, tb, ps, pk)
                    self.normrope(ps, pk, qg, ("qg",), q[:, hl, tb * 512:(tb + 1) * 512], ("q", hl, tb), tb, R)
        c.barrier(engines=("pe", "act", "dve", "pool"))
        m.off = off_R
        pTr = Ring(m, "pT", 4, [1024], BF16)
        qsr = Ring(m, "qsum", 12, [512], BF16)

        def finalize(fin):
            quads, o_ps, ok_, d_ps, dk_, hl, qb, qs = fin
            for qj, qsum, qsk in quads:
                c.op("pe", lambda e, qj=qj, qsum=qsum: e.matmul(
                    d_ps[:, :], lhsT=ones[:, :], rhs=qsum[:, :], start=(qj == 0), stop=(qj == 7)),
                    reads=[("onesb",), qsk], writes=[dk_])
            rden, rdk = rdr.next()
            og, ogk = ogr.next()
            c.op("act", lambda e: e.activation(out=rden[:, :], in_=d_ps[:, :], func=AF.Ln), reads=[dk_], writes=[rdk])
            c.op("act", lambda e: e.activation(out=rden[:, :], in_=rden[:, :], func=AF.Exp, scale=-1.0),
                 reads=[rdk], writes=[rdk])
            c.op("dve", lambda e: e.tensor_tensor(out=og[:, :], in0=o_ps[:, :], in1=rden[:, :], op=ALU.mult),
                 reads=[ok_, rdk], writes=[ogk])
            c.op("dve", lambda e: e.tensor_tensor(out=z[:, hl, qs], in0=og[:, :], in1=z[:, hl, qs], op=ALU.mult),
                 reads=[ogk, zkey(hl, qb)], writes=[zkey(hl, qb)])

        deferred = None
        ones = self.ones_b
        it = 0
        wcnt = 0
        for hl in range(4):
            for qb in range(4):
                qs = slice(qb * 512, (qb + 1) * 512)
                ob, db = (4, 5) if it % 2 == 0 else (6, 7)
                it += 1
                o_ps, ok_ = self.ps[ob], ("ps", ob)
                d_ps, dk_ = self.ps[db], ("ps", db)
                pend = None
                prevw = None
                quads = []
                for kp in range(9):
                    if kp < 8:
                        wi_ = wcnt % 2
                        wcnt += 1
                        wide = self.psw[wi_]
                        wkeys = [("ps", 2 * wi_), ("ps", 2 * wi_ + 1)]
                        for hh in range(2):
                            kt = 2 * kp + hh
                            c.op("pe", lambda e, wide=wide, kt=kt, hh=hh, hl=hl, qs=qs: e.matmul(
                                wide[:, hh * 512:(hh + 1) * 512], lhsT=k[:, kt * 128:(kt + 1) * 128], rhs=q[:, hl, qs],
                                start=True, stop=True),
                                reads=[("k", kt // 4), ("q", hl, qb)], writes=[wkeys[hh]])
                        pT, pTk = pTr.next()
                        c.op("act", lambda e, pT=pT, wide=wide: e.activation(out=pT[:, :], in_=wide[:, :], func=AF.Exp,
                                                                            scale=ATT_SCALE),
                             reads=wkeys, writes=[pTk])
                        qsum, qsk = qsr.next()
                        c.op("dve", lambda e, qsum=qsum, pT=pT: e.tensor_tensor(out=qsum[:, :], in0=pT[:, 0:512], in1=pT[:, 512:1024], op=ALU.add),
                             reads=[pTk], writes=[qsk])
                        quads.append((kp, qsum, qsk))
                    if pend is not None:
                        pkp, ppT, ppTk = pend
                        for hh in range(2):
                            pkt = 2 * pkp + hh
                            c.op("pe", lambda e, pkt=pkt, hh=hh, ppT=ppT, o_ps=o_ps: e.matmul(
                                o_ps[:, :], lhsT=V[:, pkt, :], rhs=ppT[:, hh * 512:(hh + 1) * 512],
                                start=(pkt == 0), stop=(pkt == 15)),
                                reads=[("V", pkt), ppTk], writes=[ok_])
                    if kp == 2 and deferred is not None:
                        finalize(deferred)
                        deferred = None
                    pend = (kp, pT, pTk) if kp < 8 else None
                deferred = (quads, o_ps, ok_, d_ps, dk_, hl, qb, qs)
        finalize(deferred)
        self.psi = 0
        c.barrier(engines=("pe", "act", "dve"))
        self.wout_partial(d["even_w_out"][i], 1024 + g * 512, z, zkey)

    def odd_layer(self, i):
        c = self.c
        d = self.d
        self.phase_norm(d["odd_norm"][i, :, :])
        cw, cb, hba, hbx, lam, cc1, cch = self.cw, self.cb, self.hba, self.hbx, self.lam, self.cc1, self.cch
        self.dma_in(cw[:, :, :], d["conv_w"][i], ("cw",), "cw")
        self.dma_in(cb[:, :], d["conv_b"][i], ("cb",), "cb")
        self.dma_in(hba[:, :, :], d["gate_a_b"][i], ("hba",), "hba")
        self.dma_in(hbx[:, :, :], d["gate_x_b"][i], ("hbx",), "hbx")
        self.dma_in(lam[:, :, :], d["rglru_lambda"][i], ("lam",), "lam")
        c.op("dve", lambda e: e.tensor_scalar(out=hba[:, :, :], in0=hba[:, :, :], scalar1=0.5, scalar2=None, op0=ALU.mult),
             reads=[("hba",)], writes=[("hba",)])
        c.op("dve", lambda e: e.tensor_scalar(out=hbx[:, :, :], in0=hbx[:, :, :], scalar1=0.5, scalar2=None, op0=ALU.mult),
             reads=[("hbx",)], writes=[("hbx",)])
        c.op("act", lambda e: e.activation(out=cc1[:, :, :], in_=lam[:, :, :], func=AF.Exp, scale=-1.0),
             reads=[("lam",)], writes=[("cc1",)])
        c.op("act", lambda e: e.activation(out=cc1[:, :, :], in_=cc1[:, :, :], func=AF.Ln, bias=1.0),
             reads=[("cc1",)], writes=[("cc1",)])
        c.op("dve", lambda e: e.tensor_scalar(out=cch[:, :, :], in0=cc1[:, :, :], scalar1=-4.0, scalar2=None, op0=ALU.mult),
             reads=[("cc1",)], writes=[("cch",)])
        c.op("dve", lambda e: e.tensor_scalar(out=cc1[:, :, :], in0=cc1[:, :, :], scalar1=-8.0, scalar2=None, op0=ALU.mult),
             reads=[("cc1",), ("cch",)], writes=[("cc1",)])
        self.phase_rglru_all(i)

    def phase_rglru_all(self, i):
        c = self.c
        d = self.d
        self.arena_reset(full=True)
        m = self.mem
        z = m.alloc("z", [2, S], BF16)
        xrb = m.alloc("xrb", [S + 16], BF16)
        sgs = [m.alloc(f"sg{j}", [S], BF16) for j in range(2)]
        xc = m.alloc("xc", [S], F32)
        xcb = m.alloc("xcb", [S], BF16)
        aa = [m.alloc(f"a{j}", [S], F32) for j in range(2)]
        uu = [m.alloc(f"u{j}", [S], F32) for j in range(2)]
        ixc = [m.alloc(f"ixc{j}", [S], F32) for j in range(2)]
        trr = Ring(m, "tr", 3, [512], F32)
        tir = Ring(m, "ti", 4, [512], F32)
        diag = m.alloc("diag", [4, 128], BF16)
        gwb = [m.alloc(f"gwb{j}", [4, 128], BF16) for j in range(2)]
        win = d["odd_w_in"]
        cw, cb, hba, hbx, cc1, cch = self.cw, self.cb, self.hba, self.hbx, self.cc1, self.cch
        identb = self.ident_b
        x = self.x
        c.op("dve", lambda e: e.memset(xrb[:, 0:2], 0.0), writes=[("xrb", 0)])
        c.op("dve", lambda e: e.memset(xrb[:, S + 2:S + 16], 0.0), writes=[("xrb", 3)])
        gws = {}
        st = c.st
        for j in range(3):
            old = st.pop(("w", j), None)
            for hh in range(2):
                if old is not None:
                    rd = dict(old[1])
                    if old[0] is not None and rd.get(old[0][0], 0) < old[0][1]:
                        rd[old[0][0]] = old[0][1]
                    st[("wh", 2 * j + hh)] = [None, rd]
        whi = [0]

        def wh_load(view, nk, cols, nh=1):
            if nh == 2 and whi[0] % 2 == 1:
                whi[0] += 1
            j = whi[0] % 6
            whi[0] += nh
            base = self.wslots[j // 2]
            off = (j % 2) * 1024
            dst = base[:, off:off + nk * cols].rearrange("p (k c) -> p k c", k=nk)
            keys = [("wh", j + t_) for t_ in range(nh)]
            c.op("pool", lambda e: e.dma_start(out=dst, in_=view), writes=keys, dsem=f"wh{j}")
            return dst, keys

        def proj_tile(w, wks, tb):
            ps, pk = self.psum()
            ts = slice(tb * 512, (tb + 1) * 512)
            h = self.h
            for kc in range(8):
                c.op("pe", lambda e, kc=kc: e.matmul(ps[:, :], lhsT=w[:, kc, 0:128], rhs=h[:, kc, ts],
                                                     start=(kc == 0), stop=(kc == 7)),
                     reads=wks + [self.hk(kc, tb)], writes=[pk])
            return ps, pk

        awts = {}

        def stage_a_load(k):
            wx = wh_load(win[i, :, k * 128:(k + 1) * 128].rearrange("(k p) c -> p k c", p=128), 8, 128)
            wg = wh_load(win[i, :, 2048 + k * 128: 2048 + (k + 1) * 128].rearrange("(k p) c -> p k c", p=128), 8, 128)
            awts[k] = (wx, wg)

        def stage_a_tile(k, t):
            (wx, wxk), (wg, wgk) = awts[k]
            tb = t % 4
            ts = slice(tb * 512, (tb + 1) * 512)
            sg = sgs[k % 2]
            if t < 4:
                ps, pk = proj_tile(wx, wxk, tb)
                c.op("act", lambda e: e.activation(out=xrb[:, 2 + tb * 512: 2 + (tb + 1) * 512], in_=ps[:, :], func=AF.Copy),
                     reads=[pk], writes=[("xrb", tb)])
            else:
                ps, pk = proj_tile(wg, wgk, tb)
                tg, tgk = trr.next()
                c.op("act", lambda e: e.activation(out=tg[:, :], in_=ps[:, :], func=AF.Tanh, scale=0.5),
                     reads=[pk], writes=[tgk])
                c.op("dve", lambda e: e.scalar_tensor_tensor(out=sg[:, ts], in0=tg[:, :], scalar=1.0, in1=ps[:, :],
                                                             op0=ALU.add, op1=ALU.mult),
                     reads=[pk, tgk], writes=[("sg", k % 2, tb)])
            if t == 7:
                awts.pop(k)

        def stage_b(k):
            gw = gwb[k % 2]
            gwk = [("gw", k % 2)]
            c.op("pool", lambda e, gw=gw: e.dma_start(out=gw[:, :, :], in_=d["gate_w"][i, k, :, :, :]),
                 writes=gwk, dsem=f"gw{k % 2}")
            gws[k] = (gw, gwk)
            for jt in range(4):
                c.op("dve" if "diag_dve" in DBG else "pool", lambda e, jt=jt: e.tensor_scalar(
                    out=diag[:, jt, :], in0=identb[:, :], scalar1=cw[:, k, jt:jt + 1], scalar2=0.0,
                    op0=ALU.mult, op1=ALU.add),
                    writes=[("diag", jt)])
            for tb in range(4):
                ts = slice(tb * 512, (tb + 1) * 512)
                ps, pk = self.psum()
                rk = [("xrb", t_) for t_ in range(max(0, tb - 1), min(3, tb + 1) + 1)]
                for jt in range(4):
                    c.op("pe", lambda e, ps=ps, jt=jt, tb=tb: e.matmul(
                        ps[:, :], lhsT=diag[:, jt, :], rhs=xrb[:, tb * 512 + jt: tb * 512 + jt + 512],
                        start=(jt == 0), stop=(jt == 3)),
                        reads=[("diag", jt)] + rk, writes=[pk])
                c.op("act", lambda e, ps=ps, ts=ts: e.activation(out=xc[:, ts], in_=ps[:, :], func=AF.Identity,
                                                                 bias=cb[:, k:k + 1]),
                     reads=[pk], writes=[("xc", tb)])
                c.op("dve" if "xcb_dve" in DBG else "pool", lambda e, ts=ts: e.tensor_scalar(out=xcb[:, ts], in0=xc[:, ts], scalar1=1.0, scalar2=0.0,
                                                              op0=ALU.mult, op1=ALU.add),
                     reads=[("xc", tb)], writes=[("xcb", tb)])

        def stage_c_step(k, dr, tb):
            gw, gwk = gws[k]
            a_t, u_t, x_t = aa[dr], uu[dr], ixc[dr]
            ts = slice(tb * 512, (tb + 1) * 512)
            psa, pka = self.psum()
            psx, pkx = self.psum()
            c.op("pe", lambda e: e.matmul(psx[:, :], lhsT=gw[:, 2 + dr, :], rhs=xcb[:, ts], start=True, stop=True),
                 reads=gwk + [("xcb", tb)], writes=[pkx])
            c.op("pe", lambda e: e.matmul(psa[:, :], lhsT=gw[:, dr, :], rhs=xcb[:, ts], start=True, stop=True),
                 reads=gwk + [("xcb", tb)], writes=[pka])
            tr, trk = trr.next()
            ti, tik = tir.next()
            c.op("act", lambda e: e.activation(out=ti[:, :], in_=psx[:, :], func=AF.Tanh, scale=0.5,
                                               bias=hbx[:, dr, k:k + 1]),
                 reads=[pkx], writes=[tik])
            c.op("act", lambda e: e.activation(out=tr[:, :], in_=psa[:, :], func=AF.Tanh, scale=0.5,
                                               bias=hba[:, dr, k:k + 1]),
                 reads=[pka], writes=[trk])
            c.op("act", lambda e: e.activation(out=a_t[:, ts], in_=tr[:, :], func=AF.Exp, scale=cch[:, dr, k:k + 1],
                                               bias=cch[:, dr, k:k + 1]),
                 reads=[trk], writes=[("a", dr, tb)])
            c.op("act", lambda e: e.activation(out=u_t[:, ts], in_=tr[:, :], func=AF.Exp, scale=cc1[:, dr, k:k + 1],
                                               bias=cc1[:, dr, k:k + 1]),
                 reads=[trk], writes=[("u", dr, tb)])
            c.op("dve" if "i_dve" in DBG else "pool", lambda e: e.tensor_scalar(out=ti[:, :], in0=ti[:, :], scalar1=0.5, scalar2=0.5,
                                                   op0=ALU.mult, op1=ALU.add),
                 reads=[tik], writes=[tik])
            c.op("dve" if "i_dve" in DBG else "pool", lambda e: e.tensor_tensor(out=x_t[:, ts], in0=ti[:, :], in1=xc[:, ts], op=ALU.mult),
                 reads=[tik, ("xc", tb)], writes=[("ixc", dr, tb)])
            if dr == 1 and tb == 0:
                gws.pop(k)

        ks = lambda nm, dr: [(nm, dr, tb) for tb in range(4)]

        def d_sqrt(dr):
            u_t = uu[dr]
            c.op("act", lambda e: e.activation(out=u_t[:, :], in_=u_t[:, :], func=AF.Sqrt, scale=-1.0, bias=1.0),
                 reads=ks("u", dr), writes=ks("u", dr))

        def u_piece(dr, tb):
            u_t, x_t = uu[dr], ixc[dr]
            lo, hi = tb * 512, (tb + 1) * 512
            c.op("pool", lambda e: e.tensor_tensor(out=u_t[:, lo:hi], in0=u_t[:, lo:hi], in1=x_t[:, lo:hi], op=ALU.mult),
                 reads=[("u", dr, tb), ("ixc", dr, tb)], writes=[("u", dr, tb)])

        def d_piece(dr, tb):
            a_t, u_t, x_t = aa[dr], uu[dr], ixc[dr]
            lo, hi = tb * 512, (tb + 1) * 512
            if dr == 0:
                init = 0.0 if tb == 0 else x_t[:, lo - 1:lo]
                rd = [("u", dr, tb), ("a", dr, tb)] + ([("ixc", dr, tb - 1)] if tb > 0 else [])
                c.op("dve", lambda e: e.tensor_tensor_scan(
                    out=x_t[:, lo:hi], data0=a_t[:, lo:hi], data1=u_t[:, lo:hi], initial=init,
                    op0=ALU.mult, op1=ALU.add),
                    reads=rd, writes=[("ixc", dr, tb)])
            else:
                init = 0.0 if tb == 3 else x_t[:, hi:hi + 1]
                rd = [("u", dr, tb), ("a", dr, tb)] + ([("ixc", dr, tb + 1)] if tb < 3 else [])
                c.op("dve", lambda e: e.tensor_tensor_scan(
                    out=x_t[:, lo:hi][:, ::-1], data0=a_t[:, lo:hi][:, ::-1], data1=u_t[:, lo:hi][:, ::-1], initial=init,
                    op0=ALU.mult, op1=ALU.add),
                    reads=rd, writes=[("ixc", dr, tb)])

        def yz_piece(k, tb):
            sg = sgs[k % 2]
            ts = slice(tb * 512, (tb + 1) * 512)
            c.op("dve", lambda e: e.scalar_tensor_tensor(out=ixc[0][:, ts], in0=ixc[0][:, ts], scalar=1.0, in1=ixc[1][:, ts],
                                                         op0=ALU.mult, op1=ALU.add),
                 reads=[("ixc", 0, tb), ("ixc", 1, tb)], writes=[("ixc", 0, tb)])
            c.op("dve", lambda e: e.scalar_tensor_tensor(out=z[:, k % 2, ts], in0=ixc[0][:, ts], scalar=0.5, in1=sg[:, ts],
                                                         op0=ALU.mult, op1=ALU.mult),
                 reads=[("ixc", 0, tb), ("sg", k % 2, tb)], writes=[("z", k % 2, tb)])

        def w_tiles(kp):
            r0 = kp * 256
            view = d["odd_w_out"][i][r0:r0 + 256, :].rearrange("(k p) c -> p k c", p=128)
            w, wk = wh_load(view, 2, 1024, nh=2)
            tiles = []
            for tb in (3, 2, 1, 0):
                for n in range(8):
                    def tile(n=n, tb=tb):
                        ts = slice(tb * 512, (tb + 1) * 512)
                        ps, pk = self.psum()
                        for ec in range(2):
                            c.op("pe", lambda e, ec=ec: e.matmul(
                                ps[:, :], lhsT=w[:, ec, n * 128:(n + 1) * 128], rhs=z[:, ec, ts],
                                start=(ec == 0), stop=(ec == 1)),
                                reads=wk + [("z", ec, tb)], writes=[pk])
                        c.op("dve", lambda e: e.tensor_tensor(out=x[:, n, ts], in0=ps[:, :], in1=x[:, n, ts], op=ALU.add),
                             reads=[pk, self.xk(n, tb)], writes=[self.xk(n, tb)])
                    tiles.append(tile)
            return tiles

        NBLK = 16
        stage_a_load(0)
        stage_a_load(1)
        for t in range(8):
            stage_a_tile(0, t)
        stage_b(0)
        wq = []
        for k in range(NBLK):
            if k + 2 < NBLK:
                stage_a_load(k + 2)
            for s_ in range(8):
                if s_ < 4:
                    tbp = 3 - s_
                    if k >= 1:
                        if tbp > 0:
                            u_piece(1, tbp - 1)
                        d_piece(1, tbp)
                        yz_piece(k - 1, tbp)
                    stage_c_step(k, 0, tbp)
                else:
                    if s_ == 4 and k % 2 == 0 and k >= 2:
                        wq.extend(w_tiles(k // 2 - 1))
                    if s_ == 4:
                        d_sqrt(0)
                        u_piece(0, 0)
                    if s_ < 7:
                        u_piece(0, s_ - 3)
                    d_piece(0, s_ - 4)
                    stage_c_step(k, 1, 7 - s_)
                if k + 1 < NBLK:
                    stage_a_tile(k + 1, s_)
                for _ in range(5):
                    if wq:
                        wq.pop(0)()
            d_sqrt(1)
            u_piece(1, 3)
            if k + 1 < NBLK:
                stage_b(k + 1)
        for s_ in range(4):
            if s_ < 3:
                u_piece(1, 2 - s_)
            d_piece(1, 3 - s_)
            yz_piece(NBLK - 1, 3 - s_)
        while wq:
            wq.pop(0)()
        for tile in w_tiles(NBLK // 2 - 1):
            tile()
        for j in range(3):
            rd = {}
            for hh in range(2):
                o = st.pop(("wh", 2 * j + hh), None)
                if o is None:
                    continue
                for sem, val in o[1].items():
                    if rd.get(sem, 0) < val:
                        rd[sem] = val
                if o[0] is not None and rd.get(o[0][0], 0) < o[0][1]:
                    rd[o[0][0]] = o[0][1]
            st[("w", j)] = [None, rd]


def _constants():
    ident = np.eye(128, dtype=np.float32)
    perm = np.zeros((128, 128), np.float32)
    for mm_ in range(128):
        base = (mm_ // 64) * 64
        r = mm_ - base
        kk = base + (r + 32 if r < 32 else r - 32)
        perm[kk, mm_] = 1.0
    t = np.arange(S)
    row = (t // 64).astype(np.float32)
    col = (t % 64).astype(np.float32)
    inv_freq = (np.float32(10000.0) ** (-np.arange(0, 64, 2, dtype=np.float32) / np.float32(64))).astype(np.float32)
    ang_r = row[:, None] * inv_freq[None, :]
    ang_c = col[:, None] * inv_freq[None, :]
    cos = np.zeros((128, S), np.float32)
    sin = np.zeros((128, S), np.float32)
    cos[0:32] = np.cos(ang_r).T
    cos[32:64] = np.cos(ang_r).T
    cos[64:96] = np.cos(ang_c).T
    cos[96:128] = np.cos(ang_c).T
    sin[0:32] = -np.sin(ang_r).T
    sin[32:64] = np.sin(ang_r).T
    sin[64:96] = -np.sin(ang_c).T
    sin[96:128] = np.sin(ang_c).T
    ss = np.arange(S, dtype=np.int64)
    ph = (np.outer(ss, ss) % S).astype(np.float64) * (2.0 * np.pi / S)
    dftc = (np.cos(ph) / math.sqrt(S)).astype(ml_dtypes.bfloat16)
    dfts = (-np.sin(ph) / math.sqrt(S)).astype(ml_dtypes.bfloat16)
    cs = np.arange(256, dtype=np.int64)
    phc = (np.outer(cs, cs) % 256).astype(np.float64) * (2.0 * np.pi / 256)
    cc = (np.cos(phc) / 16.0).astype(ml_dtypes.bfloat16)
    sc = (np.sin(phc) / 16.0).astype(ml_dtypes.bfloat16)
    return {"c_ident": ident, "c_perm": perm, "c_cos": cos, "c_sin": sin,
            "c_dftc": dftc, "c_dfts": dfts, "c_cc": cc, "c_sc": sc}


def _layout(inputs):
    f = lambda a: np.ascontiguousarray(np.asarray(a, dtype=np.float32))
    o = {}
    o["even_norm"] = f(np.asarray(inputs["even_norm"]).reshape(2, 8, 128).transpose(0, 2, 1))
    o["odd_norm"] = f(np.asarray(inputs["odd_norm"]).reshape(2, 8, 128).transpose(0, 2, 1))
    o["q_gain"] = f(np.asarray(inputs["q_gain"]).reshape(2, 128, 1))
    o["k_gain"] = f(np.asarray(inputs["k_gain"]).reshape(2, 128, 1))
    o["conv_w"] = f(np.asarray(inputs["conv_w"]).reshape(2, 4, 16, 128).transpose(0, 3, 2, 1))
    o["conv_b"] = f(np.asarray(inputs["conv_b"]).reshape(2, 16, 128).transpose(0, 2, 1))
    for nm in ("gate_a_b", "gate_x_b"):
        o[nm] = f(np.asarray(inputs[nm]).transpose(0, 3, 1, 2))
    o["rglru_lambda"] = f(np.asarray(inputs["rglru_lambda"]).reshape(2, 2, 16, 128).transpose(0, 3, 1, 2))
    for nm in ("even_w_in", "fourier_w", "even_w_out", "odd_w_in", "odd_w_out", "final_norm"):
        o[nm] = f(inputs[nm])
    gw = np.stack([np.asarray(inputs["gate_a_w"]), np.asarray(inputs["gate_x_w"])], axis=0)
    o["gate_w"] = f(gw.transpose(1, 3, 4, 0, 2, 5).reshape(2, 16, 128, 4, 128))
    return o


_CACHE = {}


def _program(layers, do_final):
    key = (tuple(layers), do_final)
    if key not in _CACHE:
        _CACHE[key] = Builder(list(layers), do_final).nc
    return _CACHE[key]


def run_layers(x, shared, layers, do_final, core_ids):
    nc = _program(layers, do_final)
    in_maps = []
    for ci in core_ids:
        mp = dict(shared)
        mp["x"] = np.ascontiguousarray(x[ci * NB:(ci + 1) * NB])
        in_maps.append(mp)
    res = run_bass_kernel_spmd(nc, in_maps, core_ids=list(range(len(core_ids))))
    return [r["y"] for r in res.results]


def kernel(**inputs):
    x = np.asarray(inputs["x"], dtype=np.float32)
    shared = _layout(inputs)
    shared.update(_constants())
    outs = run_layers(x, shared, [0, 1, 2, 3], True, list(range(NCORES)))
    return np.concatenate(outs, axis=0).astype(np.float32)
```

```python
import math
from contextlib import ExitStack

import numpy as np
import ml_dtypes

import concourse.bass as bass
import concourse.mybir as mybir
from concourse.bass_utils import run_bass_kernel_spmd

F32 = mybir.dt.float32
BF16 = mybir.dt.bfloat16
AF = mybir.ActivationFunctionType
ALU = mybir.AluOpType

S = 2048
D = 1024
NCORES = 8
NB = 2
EPS = 1e-6
ATT_SCALE = 128 ** -0.5
ENGS = ("pe", "act", "dve", "pool", "sp")
ATTACH = True
NPS = 8
DBG = set()
NTT = 16


class Sched:
    def __init__(self):
        self.ops = {e: [] for e in ENGS}
        self.n = {e: 0 for e in ENGS}
        self.seen = {e: {} for e in ENGS}
        self.st = {}
        self.dcnt = {}
        self.window = 1 << 30

    def _need(self, eng, ev, raw):
        sem, val = ev
        if sem == eng:
            if eng == "pe" or not raw:
                return False
            return self.n[eng] - val < self.window
        return self.seen[eng].get(sem, 0) < val

    def op(self, eng, fn, reads=(), writes=(), dsem=None, attach=True):
        waits = {}

        def add(ev, raw):
            if ev is not None and self._need(eng, ev, raw):
                if waits.get(ev[0], 0) < ev[1]:
                    waits[ev[0]] = ev[1]

        for k in reads:
            s = self.st.get(k)
            if s is not None:
                add(s[0], True)
                if k[0] == "ps":
                    for sem, val in s[1].items():
                        if sem != eng:
                            add((sem, val), False)
        for k in writes:
            s = self.st.get(k)
            if s is not None:
                add(s[0], False)
                for sem, val in s[1].items():
                    add((sem, val), False)
        for sem, val in waits.items():
            if self.seen[eng].get(sem, 0) < val:
                self.seen[eng][sem] = val
        if dsem is None:
            self.n[eng] += 1
            ev = (eng, self.n[eng])
        else:
            self.dcnt[dsem] = self.dcnt.get(dsem, 0) + 16
            ev = (dsem, self.dcnt[dsem])
        self.ops[eng].append((list(waits.items()), fn, ev, ATTACH and attach))
        for k in reads:
            s = self.st.setdefault(k, [None, {}])
            if s[1].get(ev[0], 0) < ev[1]:
                s[1][ev[0]] = ev[1]
        for k in writes:
            self.st[k] = [ev, {}]
        return ev

    def barrier(self, engines=ENGS, clear=False):
        evs = [(e, self.n[e]) for e in ENGS if self.n[e] > 0]
        evs += list(self.dcnt.items())
        for e in engines:
            waits = []
            for sem, val in evs:
                if sem == e:
                    if e != "pe":
                        waits.append((sem, val))
                elif self.seen[e].get(sem, 0) < val:
                    waits.append((sem, val))
                    self.seen[e][sem] = val
            if waits:
                self.ops[e].append((waits, None, None, False))
        if clear:
            self.st = {k: v for k, v in self.st.items() if k[0] in ("w", "wh")}

    def final_wait(self, eng, events):
        waits = [(s, v) for s, v in events]
        self.ops[eng].append((waits, None, None, False))

    def emit(self, nc, sems):
        with nc.Block() as block:
            decos = {"pe": block.tensor, "act": block.scalar, "dve": block.vector,
                     "pool": block.gpsimd, "sp": block.sync}
            for e in ENGS:
                ops = self.ops[e]

                def body(engine, ops=ops):
                    for waits, fn, ev, attach in ops:
                        if fn is None:
                            for s, v in waits:
                                engine.wait_ge(sems[s], v)
                            continue
                        rest = waits
                        first = None
                        if attach and waits:
                            first = waits[0]
                            rest = waits[1:]
                        for s, v in rest:
                            engine.wait_ge(sems[s], v)
                        ins = fn(engine)
                        if first is not None:
                            ins._wait_ge(sems[first[0]], first[1])
                        ins.then_inc(sems[ev[0]], 1 if ev[0] in ENGS else 16)

                decos[e](body)


class Mem:
    def __init__(self, nc, limit):
        self.nc, self.off, self.limit, self.cnt = nc, 16512, limit, 0

    def alloc(self, name, free_shape, dtype):
        nbytes = int(np.prod(free_shape)) * (4 if dtype == F32 else 2)
        off = (self.off + 31) // 32 * 32
        assert off + nbytes <= self.limit, f"SBUF overflow for {name}: {off}+{nbytes} > {self.limit}"
        self.cnt += 1
        t = self.nc.alloc_sbuf_tensor_at(f"{name}_{self.cnt}", [128] + list(free_shape), dtype, offset=off)
        self.off = off + nbytes
        return t


class Ring:
    def __init__(self, mem, name, n, free_shape, dtype):
        self.t = [mem.alloc(f"{name}{i}", free_shape, dtype) for i in range(n)]
        self.name, self.i = name, 0

    def next(self):
        j = self.i % len(self.t)
        self.i += 1
        return self.t[j], (self.name, j)


DRAM_INPUTS = [
    ("x", [NB, S, D], F32),
    ("even_norm", [2, 128, 8], F32), ("even_w_in", [2, 1024, 4608], F32),
    ("fourier_w", [2, 4, 256, 256], F32), ("q_gain", [2, 128, 1], F32), ("k_gain", [2, 128, 1], F32),
    ("even_w_out", [2, 2048, 1024], F32),
    ("odd_norm", [2, 128, 8], F32), ("odd_w_in", [2, 1024, 4096], F32),
    ("conv_w", [2, 128, 16, 4], F32), ("conv_b", [2, 128, 16], F32),
    ("gate_w", [2, 16, 128, 4, 128], F32), ("gate_a_b", [2, 128, 2, 16], F32),
    ("gate_x_b", [2, 128, 2, 16], F32),
    ("rglru_lambda", [2, 128, 2, 16], F32), ("odd_w_out", [2, 2048, 1024], F32),
    ("final_norm", [1024], F32),
    ("c_ident", [128, 128], F32), ("c_perm", [128, 128], F32),
    ("c_cos", [128, S], F32), ("c_sin", [128, S], F32),
    ("c_dftc", [S, S], BF16), ("c_dfts", [S, S], BF16),
    ("c_cc", [256, 256], BF16), ("c_sc", [256, 256], BF16),
]


class Builder:
    def __init__(self, layers, do_final):
        self.layers = layers
        self.do_final = do_final
        nc = self.nc = bass.Bass("TRN2", target_bir_lowering=False)
        self.d = {}
        for name, shape, dt in DRAM_INPUTS:
            self.d[name] = nc.dram_tensor(name, shape, dt, kind="ExternalInput").ap()
        self.y = nc.dram_tensor("y", [NB, S, D], F32, kind="ExternalOutput").ap()
        self.c = Sched()
        limit = 229376
        self.mem = m = Mem(nc, limit)
        self.x = m.alloc("x", [8, S], F32)
        self.h = m.alloc("h", [8, S], BF16)
        self.ident_f = m.alloc("identf", [128], F32)
        self.ident_b = m.alloc("identb", [128], BF16)
        self.ones_b = m.alloc("onesb", [128], BF16)
        self.perm_b = m.alloc("permb", [128], BF16)
        self.gcol = m.alloc("gcol", [8], F32)
        self.qg = m.alloc("qg", [1], F32)
        self.kg = m.alloc("kg", [1], F32)
        self.cw = m.alloc("cw", [16, 4], F32)
        self.cb = m.alloc("cb", [16], F32)
        self.hba = m.alloc("hba", [2, 16], F32)
        self.hbx = m.alloc("hbx", [2, 16], F32)
        self.lam = m.alloc("lam", [2, 16], F32)
        self.cc1 = m.alloc("cc1", [2, 16], F32)
        self.cch = m.alloc("cch", [2, 16], F32)
        self.wslots = [m.alloc(f"w{i}", [2048], BF16) for i in range(3)]
        self.wi = 0
        self.arena0 = m.off
        self.psw = [nc.alloc_psum_tensor(f"psw{i}", [128, 1024], F32) for i in range(4)]
        self.ps = [self.psw[i // 2][:, (i % 2) * 512:(i % 2 + 1) * 512] for i in range(8)]
        self.psi = 0
        self.out_events = []
        self.build()
        with ExitStack() as es:
            sems = {}
            for e in ENGS:
                sems[e] = es.enter_context(nc.semaphore(f"s_{e}"))
            for dn in self.c.dcnt:
                sems[dn] = es.enter_context(nc.semaphore(f"d_{dn}"))
            self.c.emit(nc, sems)

    def psum(self):
        i = self.psi % NPS
        self.psi += 1
        return self.ps[i], ("ps", i)

    def arena_reset(self, full=False):
        self.c.barrier(engines=ENGS if full else ("pe", "act", "dve", "sp"), clear=True)
        self.mem.off = self.arena0

    def wring(self, n):
        pass

    def wload(self, dram_view, nk, cols):
        j = self.wi % len(self.wslots)
        self.wi += 1
        t = self.wslots[j]
        dst = t[:, 0:nk * cols].rearrange("p (k c) -> p k c", k=nk)
        self.c.op("pool", lambda e: e.dma_start(out=dst, in_=dram_view), writes=[("w", j)], dsem=f"w{j}")
        return dst, ("w", j)

    def dma_in(self, out_ap, in_ap, key, dsem, eng="sp"):
        self.c.op(eng, lambda e: e.dma_start(out=out_ap, in_=in_ap), writes=[key], dsem=dsem)

    def xk(self, kc, tb):
        return ("x", kc, tb)

    def hk(self, kc, tb):
        return ("h", kc, tb)

    def build(self):
        c = self.c
        self.dma_in(self.ident_f[:, :], self.d["c_ident"][:, :], ("identf",), "c0")
        self.dma_in(self.ident_b[:, :], self.d["c_ident"][:, :], ("identb",), "c1", eng="pool")
        self.dma_in(self.perm_b[:, :], self.d["c_perm"][:, :], ("permb",), "c2", eng="pool")
        ones_b = self.ones_b
        c.op("dve", lambda e: e.memset(ones_b[:, :], 1.0), writes=[("onesb",)])
        c.barrier(clear=False)
        for b in range(NB):
            self.phase_load_x(b)
            for L in self.layers:
                if L % 2 == 0:
                    self.even_layer(L // 2)
                else:
                    self.odd_layer(L // 2)
            self.phase_store(b)
        c.barrier(engines=("sp",))
        c.final_wait("sp", self.out_events)

    def phase_load_x(self, b):
        c = self.c
        self.arena_reset()
        stage = Ring(self.mem, "xin", 4, [D], F32)
        x, ident = self.x, self.ident_f
        for tt in range(NTT):
            st, sk = stage.next()
            self.dma_in(st[:, :], self.d["x"][b, tt * 128:(tt + 1) * 128, :], sk, f"xin{sk[1]}")
            for half in range(2):
                ps, pk = self.psum()
                for j in range(4):
                    kc = half * 4 + j
                    c.op("pe", lambda e, ps=ps, st=st, j=j, kc=kc: e.transpose(
                        ps[:, j * 128:(j + 1) * 128], st[:, kc * 128:(kc + 1) * 128], ident[:, :]),
                        reads=[sk, ("identf",)], writes=[pk])
                dst = x[:, half * 4:half * 4 + 4, tt * 128:(tt + 1) * 128]
                src = ps[:, :].rearrange("p (a b) -> p a b", a=4)
                wk = [self.xk(half * 4 + j, tt // 4) for j in range(4)]
                if half == 0:
                    c.op("act", lambda e, dst=dst, src=src: e.activation(out=dst, in_=src, func=AF.Copy),
                         reads=[pk], writes=wk)
                else:
                    c.op("dve", lambda e, dst=dst, src=src: e.tensor_copy(out=dst, in_=src),
                         reads=[pk], writes=wk)

    def phase_store(self, b):
        c = self.c
        self.arena_reset()
        m = self.mem
        stage = Ring(m, "ot", 4, [D], F32)
        x, ident = self.x, self.ident_f
        if self.do_final:
            gfin = m.alloc("gfin", [D], F32)
            junk = m.alloc("junk", [D], BF16)
            ssr = Ring(m, "ss", 4, [1], F32)
            sdr = Ring(m, "sd", 4, [1], F32)
            rsr = Ring(m, "rs", 4, [1], F32)
            self.dma_in(gfin[:, :], self.d["final_norm"].partition_broadcast(128), ("gfin",), "gfin")
        for tt in range(16):
            ot, ok = stage.next()
            for half in range(2):
                ps, pk = self.psum()
                for j in range(4):
                    kc = half * 4 + j
                    c.op("pe", lambda e, ps=ps, j=j, kc=kc, tt=tt: e.transpose(
                        ps[:, j * 128:(j + 1) * 128], x[:, kc, tt * 128:(tt + 1) * 128], ident[:, :]),
                        reads=[self.xk(kc, tt // 4), ("identf",)], writes=[pk])
                dst = ot[:, half * 512:(half + 1) * 512]
                if half == 0 and "noactcopy" not in DBG:
                    c.op("act", lambda e, dst=dst, ps=ps: e.activation(out=dst, in_=ps[:, :], func=AF.Copy),
                         reads=[pk], writes=[ok])
                else:
                    c.op("dve", lambda e, dst=dst, ps=ps: e.tensor_copy(out=dst, in_=ps[:, :]),
                         reads=[pk], writes=[ok])
            if self.do_final:
                ss, ssk = ssr.next()
                sd, sdk = sdr.next()
                rs, rsk = rsr.next()
                c.op("act", lambda e, ot=ot, ss=ss: e.activation(out=junk[:, :], in_=ot[:, :], func=AF.Square,
                                                                accum_out=ss[:, 0:1]),
                     reads=[ok], writes=[("junk",), ssk], attach=False)
                c.op("act", lambda e, sd=sd, ss=ss: e.activation(out=sd[:, :], in_=ss[:, :], func=AF.Sqrt,
                                                                scale=1.0 / D, bias=EPS),
                     reads=[ssk], writes=[sdk])
                c.op("dve", lambda e, rs=rs, sd=sd: e.reciprocal(out=rs[:, :], in_=sd[:, :]),
                     reads=[sdk], writes=[rsk])
                c.op("dve", lambda e, ot=ot, rs=rs: e.scalar_tensor_tensor(
                    out=ot[:, :], in0=ot[:, :], scalar=rs[:, 0:1], in1=gfin[:, :], op0=ALU.mult, op1=ALU.mult),
                    reads=[ok, rsk, ("gfin",)], writes=[ok])
            ev = c.op("sp", lambda e, ot=ot, tt=tt, b=b: e.dma_start(out=self.y[b, tt * 128:(tt + 1) * 128, :], in_=ot[:, :]),
                      reads=[ok], dsem=f"out{ok[1]}")
            self.out_events = [x_ for x_ in self.out_events if x_[0] != ev[0]] + [ev]

    def phase_norm(self, gsrc):
        c = self.c
        self.arena_reset()
        m = self.mem
        x, h, gcol, ones = self.x, self.h, self.gcol, self.ones_b
        self.dma_in(gcol[:, :], gsrc, ("gcol",), "gcol")
        sqr = Ring(m, "sq", 4, [512], BF16)
        sdr = Ring(m, "sd", 2, [512], F32)
        rsr = Ring(m, "rs", 2, [512], F32)
        for tb in range(4):
            ts = slice(tb * 512, (tb + 1) * 512)
            ps, pk = self.psum()
            for kc in range(8):
                sq, sqk = sqr.next()
                c.op("act", lambda e, sq=sq, kc=kc, ts=ts: e.activation(out=sq[:, :], in_=x[:, kc, ts], func=AF.Square),
                     reads=[self.xk(kc, tb)], writes=[sqk])
                c.op("pe", lambda e, ps=ps, sq=sq, kc=kc: e.matmul(ps[:, :], lhsT=ones[:, :], rhs=sq[:, :],
                                                                   start=(kc == 0), stop=(kc == 7)),
                     reads=[sqk, ("onesb",)], writes=[pk])
            sd, sdk = sdr.next()
            rs, rsk = rsr.next()
            c.op("act", lambda e, sd=sd, ps=ps: e.activation(out=sd[:, :], in_=ps[:, :], func=AF.Ln,
                                                            scale=1.0 / D, bias=EPS),
                 reads=[pk], writes=[sdk])
            c.op("act", lambda e, rs=rs, sd=sd: e.activation(out=rs[:, :], in_=sd[:, :], func=AF.Exp, scale=-0.5),
                 reads=[sdk], writes=[rsk])
            for kc in range(8):
                c.op("dve", lambda e, kc=kc, ts=ts, rs=rs: e.scalar_tensor_tensor(
                    out=h[:, kc, ts], in0=x[:, kc, ts], scalar=gcol[:, kc:kc + 1], in1=rs[:, :],
                    op0=ALU.mult, op1=ALU.mult),
                    reads=[self.xk(kc, tb), rsk, ("gcol",)], writes=[self.hk(kc, tb)])

    def proj_fm(self, w, wk, col0, tb, ps, pk):
        h = self.h
        ts = slice(tb * 512, (tb + 1) * 512)
        for kc in range(8):
            self.c.op("pe", lambda e, kc=kc: e.matmul(ps[:, :], lhsT=w[:, kc, col0:col0 + 128], rhs=h[:, kc, ts],
                                                      start=(kc == 0), stop=(kc == 7)),
                      reads=[wk, self.hk(kc, tb)], writes=[pk])

    def wout_partial(self, wsrc, r0, z, zkey):
        c = self.c
        x = self.x
        for ch in range(2):
            view = wsrc[r0:r0 + 512, ch * 512:(ch + 1) * 512].rearrange("(k p) c -> p k c", p=128)
            w, wk = self.wload(view, 4, 512)
            for nl in range(4):
                n = ch * 4 + nl
                for tb in range(4):
                    ts = slice(tb * 512, (tb + 1) * 512)
                    ps, pk = self.psum()
                    for ec in range(4):
                        c.op("pe", lambda e, ps=ps, w=w, ec=ec, nl=nl, ts=ts: e.matmul(
                            ps[:, :], lhsT=w[:, ec, nl * 128:(nl + 1) * 128], rhs=z[:, ec, ts],
                            start=(ec == 0), stop=(ec == 3)),
                            reads=[wk, zkey(ec, tb)], writes=[pk])
                    c.op("dve", lambda e, ps=ps, n=n, ts=ts: e.tensor_tensor(
                        out=x[:, n, ts], in0=ps[:, :], in1=x[:, n, ts], op=ALU.add),
                        reads=[pk, self.xk(n, tb)], writes=[self.xk(n, tb)])

    def gate_silu(self, wsrc_cols, z, zkey):
        c = self.c
        for u in range(2):
            w, wk = self.wload(wsrc_cols(u), 8, 256)
            for el in range(2):
                ec = u * 2 + el
                for tb in range(4):
                    ts = slice(tb * 512, (tb + 1) * 512)
                    ps, pk = self.psum()
                    self.proj_fm(w, wk, el * 128, tb, ps, pk)
                    c.op("act", lambda e, ps=ps, ec=ec, ts=ts: e.activation(out=z[:, ec, ts], in_=ps[:, :], func=AF.Silu),
                         reads=[pk], writes=[zkey(ec, tb)])

    def even_layer(self, i):
        d = self.d
        self.phase_norm(d["even_norm"][i, :, :])
        for p in range(2):
            self.phase_fourier(i, p)
        for g in range(2):
            self.phase_attn(i, g)

    def phase_fourier(self, i, p):
        c = self.c
        d = self.d
        m = self.mem
        h = self.h
        if p == 0:
            self.arena_reset()
            z = m.alloc("z", [4, S], BF16)
            AB = m.alloc("AB", [16, 1024], BF16)
            tabC = [m.alloc(f"tabC{j}", [16, 256], BF16) for j in range(2)]
            tabS = [m.alloc(f"tabS{j}", [16, 256], BF16) for j in range(2)]
            finr = Ring(m, "fin", 2, [2, 512], BF16)
            MAB = m.alloc("MAB", [2, 2, 512], BF16)
            ccb = m.alloc("ccb", [2, 256], BF16)
            scb = m.alloc("scb", [2, 256], BF16)
            self.dma_in(ccb[:, :, :], d["c_cc"].rearrange("(k p) c -> p k c", p=128), ("ccb",), "ccb")
            self.dma_in(scb[:, :, :], d["c_sc"].rearrange("(k p) c -> p k c", p=128), ("scb",), "scb")
            self._F = (z, AB, tabC, tabS, finr, MAB, ccb, scb)
        else:
            z, AB, tabC, tabS, finr, MAB, ccb, scb = self._F
        self.wring(3)
        zkey = lambda ec, tb: ("z", ec, tb)
        win = d["even_w_in"]
        for gl in range(2):
            g = 2 * p + gl
            fw, fwk = self.wload(d["fourier_w"][i, g, :, :].rearrange("(k p) c -> p k c", p=128), 2, 256)
            for mb in range(2):
                ps, pk = self.psum()
                for which, src, sk in ((0, ccb, ("ccb",)), (1, scb, ("scb",))):
                    for kc in range(2):
                        c.op("pe", lambda e, ps=ps, src=src, kc=kc, mb=mb, fw=fw, which=which: e.matmul(
                            ps[:, which * 256:(which + 1) * 256], lhsT=src[:, kc, mb * 128:(mb + 1) * 128],
                            rhs=fw[:, kc, :], start=(kc == 0), stop=(kc == 1)),
                            reads=[sk, fwk], writes=[pk])
                c.op("act", lambda e, ps=ps, gl=gl, mb=mb: e.activation(out=MAB[:, gl, mb, :], in_=ps[:, :], func=AF.Copy),
                     reads=[pk], writes=[("MAB", gl)])
        self.gate_silu(lambda u: win[i, :, 1024 + p * 512 + u * 256: 1024 + p * 512 + (u + 1) * 256]
                       .rearrange("(k p) c -> p k c", p=128), z, zkey)
        for gl in range(2):
            g = 2 * p + gl
            w, wk = self.wload(win[i, :, g * 256:(g + 1) * 256].rearrange("(k p) c -> p k c", p=128), 8, 256)
            for tb in range(4):
                fin, fk = finr.next()
                for half in range(2):
                    ps, pk = self.psum()
                    self.proj_fm(w, wk, half * 128, tb, ps, pk)
                    if half == 0:
                        c.op("act", lambda e, ps=ps, fin=fin: e.activation(out=fin[:, 0, :], in_=ps[:, :], func=AF.Copy),
                             reads=[pk], writes=[fk])
                    else:
                        c.op("dve", lambda e, ps=ps, fin=fin: e.tensor_copy(out=fin[:, 1, :], in_=ps[:, :]),
                             reads=[pk], writes=[fk])
                for tq in range(4):
                    tt = tb * 4 + tq
                    ps, pk = self.psum()
                    for half in range(2):
                        c.op("pe", lambda e, ps=ps, fin=fin, half=half, tq=tq, gl=gl: e.matmul(
                            ps[:, :], lhsT=fin[:, half, tq * 128:(tq + 1) * 128], rhs=MAB[:, gl, half, :],
                            start=(half == 0), stop=(half == 1)),
                            reads=[fk, ("MAB", gl)], writes=[pk])
                    dst = AB[:, tt, gl * 512:(gl + 1) * 512]
                    if tq % 2 == 0:
                        c.op("act", lambda e, ps=ps, dst=dst: e.activation(out=dst, in_=ps[:, :], func=AF.Copy),
                             reads=[pk], writes=[("AB", tt, gl)])
                    else:
                        c.op("dve", lambda e, ps=ps, dst=dst: e.tensor_copy(out=dst, in_=ps[:, :]),
                             reads=[pk], writes=[("AB", tt, gl)])
        for sb in range(8):
            tC, tS = tabC[sb % 2], tabS[sb % 2]
            kC, kS = ("tabC", sb % 2), ("tabS", sb % 2)
            self.dma_in(tC[:, :, :], d["c_dftc"][:, sb * 256:(sb + 1) * 256].rearrange("(j p) c -> p j c", p=128),
                        kC, f"tabC{sb % 2}")
            self.dma_in(tS[:, :, :], d["c_dfts"][:, sb * 256:(sb + 1) * 256].rearrange("(j p) c -> p j c", p=128),
                        kS, f"tabS{sb % 2}")
            ss = slice(sb * 256, (sb + 1) * 256)
            for ec in range(4):
                gl, half = ec // 2, ec % 2
                ps, pk = self.psum()
                for which, tab, tk in ((0, tC, kC), (1, tS, kS)):
                    c0 = gl * 512 + which * 256 + half * 128
                    for j in range(16):
                        c.op("pe", lambda e, ps=ps, tab=tab, j=j, c0=c0, which=which: e.matmul(
                            ps[:, 0:256], lhsT=AB[:, j, c0:c0 + 128], rhs=tab[:, j, :],
                            start=(which == 0 and j == 0), stop=(which == 1 and j == 15)),
                            reads=[("AB", j, gl), tk], writes=[pk])
                c.op("dve", lambda e, ps=ps, ec=ec, ss=ss: e.tensor_tensor(
                    out=z[:, ec, ss], in0=ps[:, 0:256], in1=z[:, ec, ss], op=ALU.mult),
                    reads=[pk, zkey(ec, sb // 2)], writes=[zkey(ec, sb // 2)])
        self.wout_partial(d["even_w_out"][i], p * 512, z, zkey)

    def normrope(self, ps, pk, gc, gk, out_ap, out_key, tb, R):
        c = self.c
        ts = slice(tb * 512, (tb + 1) * 512)
        ones, perm = self.ones_b, self.perm_b
        cos, sin = R["cos"], R["sin"]
        sq, sqk = R["sq"].next()
        qb, qbk = R["qb"].next()
        t1, t1k = R["t1"].next()
        t2, t2k = R["t2"].next()
        sd, sdk = R["sd"].next()
        rs, rsk = R["rs"].next()
        c.op("act", lambda e: e.activation(out=sq[:, :], in_=ps[:, :], func=AF.Square), reads=[pk], writes=[sqk])
        c.op("act", lambda e: e.activation(out=qb[:, :], in_=ps[:, :], func=AF.Identity, scale=gc[:, 0:1]),
             reads=[pk, gk], writes=[qbk])
        ps2, pk2 = self.psum()
        ps3, pk3 = self.psum()
        c.op("pe", lambda e: e.matmul(ps2[:, :], lhsT=ones[:, :], rhs=sq[:, :], start=True, stop=True),
             reads=[sqk, ("onesb",)], writes=[pk2])
        c.op("pe", lambda e: e.matmul(ps3[:, :], lhsT=perm[:, :], rhs=qb[:, :], start=True, stop=True),
             reads=[qbk, ("permb",)], writes=[pk3])
        c.op("dve", lambda e: e.scalar_tensor_tensor(out=t1[:, :], in0=ps[:, :], scalar=gc[:, 0:1], in1=cos[:, ts],
                                                     op0=ALU.mult, op1=ALU.mult),
             reads=[pk, gk, ("cos",)], writes=[t1k])
        c.op("act", lambda e: e.activation(out=sd[:, :], in_=ps2[:, :], func=AF.Ln, scale=1.0 / 128, bias=EPS),
             reads=[pk2], writes=[sdk])
        c.op("act", lambda e: e.activation(out=rs[:, :], in_=sd[:, :], func=AF.Exp, scale=-0.5),
             reads=[sdk], writes=[rsk])
        c.op("dve", lambda e: e.tensor_tensor(out=t2[:, :], in0=ps3[:, :], in1=sin[:, ts], op=ALU.mult),
             reads=[pk3, ("sin",)], writes=[t2k])
        c.op("pool", lambda e: e.tensor_tensor(out=t1[:, :], in0=t1[:, :], in1=t2[:, :], op=ALU.add),
             reads=[t1k, t2k], writes=[t1k])
        c.op("dve", lambda e: e.tensor_tensor(out=out_ap, in0=t1[:, :], in1=rs[:, :], op=ALU.mult),
             reads=[t1k, rsk], writes=[out_key])

    def phase_attn(self, i, g):
        c = self.c
        d = self.d
        m = self.mem
        h = self.h
        qg, kg = self.qg, self.kg
        if g == 0:
            self.arena_reset()
            z = m.alloc("z", [4, S], BF16)
            q = m.alloc("q", [4, S], BF16)
            k = m.alloc("k", [S], BF16)
            V = m.alloc("V", [16, 128], BF16)
            cos = m.alloc("cos", [S], F32)
            sin = m.alloc("sin", [S], F32)
            rdr = Ring(m, "rden", 2, [512], F32)
            ogr = Ring(m, "og", 2, [512], F32)
            off_R = m.off
            R = {"cos": cos, "sin": sin,
                 "sq": Ring(m, "sq", 2, [512], BF16), "qb": Ring(m, "qb", 2, [512], BF16),
                 "t1": Ring(m, "t1", 2, [512], F32), "t2": Ring(m, "t2", 2, [512], F32),
                 "sd": Ring(m, "sd", 2, [512], F32), "rs": Ring(m, "rs", 2, [512], F32)}
            self.dma_in(cos[:, :], d["c_cos"][:, :], ("cos",), "cos")
            self.dma_in(sin[:, :], d["c_sin"][:, :], ("sin",), "sin")
            self.dma_in(qg[:, :], d["q_gain"][i, :, :], ("qg",), "qg")
            self.dma_in(kg[:, :], d["k_gain"][i, :, :], ("kg",), "kg")
            self._A = (z, q, k, V, cos, sin, rdr, ogr, off_R, R)
        else:
            z, q, k, V, cos, sin, rdr, ogr, off_R, R = self._A
        zkey = lambda ec, tb: ("z", ec, tb)
        win = d["even_w_in"]
        a0 = 3584 + g * 512
        self.gate_silu(lambda u: win[i, :, a0 + u * 256: a0 + (u + 1) * 256].rearrange("(k p) c -> p k c", p=128),
                       z, zkey)
        w, wk = self.wload(win[i, :, 3072:3328].rearrange("(k p) c -> p k c", p=128), 8, 256)
        for tb in range(4):
            ps, pk = self.psum()
            self.proj_fm(w, wk, g * 128, tb, ps, pk)
            self.normrope(ps, pk, kg, ("kg",), k[:, tb * 512:(tb + 1) * 512], ("k", tb), tb, R)
        w, wk = self.wload(win[i, :, 3328:3584].rearrange("(k p) c -> p k c", p=128), 8, 256)
        for tt in range(16):
            ps, pk = self.psum()
            for kc in range(8):
                c.op("pe", lambda e, ps=ps, kc=kc, tt=tt, w=w: e.matmul(
                    ps[:, 0:128], lhsT=h[:, kc, tt * 128:(tt + 1) * 128], rhs=w[:, kc, g * 128:(g + 1) * 128],
                    start=(kc == 0), stop=(kc == 7)),
                    reads=[wk, self.hk(kc, tt // 4)], writes=[pk])
            if tt % 2 == 0:
                c.op("act", lambda e, ps=ps, tt=tt: e.activation(out=V[:, tt, :], in_=ps[:, 0:128], func=AF.Copy),
                     reads=[pk], writes=[("V", tt)])
            else:
                c.op("dve", lambda e, ps=ps, tt=tt: e.tensor_copy(out=V[:, tt, :], in_=ps[:, 0:128]),
                     reads=[pk], writes=[("V", tt)])
        q0 = 2048 + g * 512
        for u in range(2):
            w, wk = self.wload(win[i, :, q0 + u * 256: q0 + (u + 1) * 256].rearrange("(k p) c -> p k c", p=128), 8, 256)
            for el in range(2):
                hl = u * 2 + el
                for tb in range(4):
                    ps, pk = self.psum()
                    self.proj_fm(w, wk, el * 128, tb, ps, pk)
                    self.normrope(ps, pk, qg, ("qg",), q[:, hl, tb * 512:(tb + 1) * 512], ("q", hl, tb), tb, R)
        c.barrier(engines=("pe", "act", "dve", "pool"))
        m.off = off_R
        pTr = Ring(m, "pT", 4, [1024], BF16)
        qsr = Ring(m, "qsum", 12, [512], BF16)

        def finalize(fin):
            quads, o_ps, ok_, d_ps, dk_, hl, qb, qs = fin
            for qj, qsum, qsk in quads:
                c.op("pe", lambda e, qj=qj, qsum=qsum: e.matmul(
                    d_ps[:, :], lhsT=ones[:, :], rhs=qsum[:, :], start=(qj == 0), stop=(qj == 7)),
                    reads=[("onesb",), qsk], writes=[dk_])
            rden, rdk = rdr.next()
            og, ogk = ogr.next()
            c.op("act", lambda e: e.activation(out=rden[:, :], in_=d_ps[:, :], func=AF.Ln), reads=[dk_], writes=[rdk])
            c.op("act", lambda e: e.activation(out=rden[:, :], in_=rden[:, :], func=AF.Exp, scale=-1.0),
                 reads=[rdk], writes=[rdk])
            c.op("dve", lambda e: e.tensor_tensor(out=og[:, :], in0=o_ps[:, :], in1=rden[:, :], op=ALU.mult),
                 reads=[ok_, rdk], writes=[ogk])
            c.op("dve", lambda e: e.tensor_tensor(out=z[:, hl, qs], in0=og[:, :], in1=z[:, hl, qs], op=ALU.mult),
                 reads=[ogk, zkey(hl, qb)], writes=[zkey(hl, qb)])

        deferred = None
        ones = self.ones_b
        it = 0
        wcnt = 0
        for hl in range(4):
            for qb in range(4):
                qs = slice(qb * 512, (qb + 1) * 512)
                ob, db = (4, 5) if it % 2 == 0 else (6, 7)
                it += 1
                o_ps, ok_ = self.ps[ob], ("ps", ob)
                d_ps, dk_ = self.ps[db], ("ps", db)
                pend = None
                prevw = None
                quads = []
                for kp in range(9):
                    if kp < 8:
                        wi_ = wcnt % 2
                        wcnt += 1
                        wide = self.psw[wi_]
                        wkeys = [("ps", 2 * wi_), ("ps", 2 * wi_ + 1)]
                        for hh in range(2):
                            kt = 2 * kp + hh
                            c.op("pe", lambda e, wide=wide, kt=kt, hh=hh, hl=hl, qs=qs: e.matmul(
                                wide[:, hh * 512:(hh + 1) * 512], lhsT=k[:, kt * 128:(kt + 1) * 128], rhs=q[:, hl, qs],
                                start=True, stop=True),
                                reads=[("k", kt // 4), ("q", hl, qb)], writes=[wkeys[hh]])
                        pT, pTk = pTr.next()
                        c.op("act", lambda e, pT=pT, wide=wide: e.activation(out=pT[:, :], in_=wide[:, :], func=AF.Exp,
                                                                            scale=ATT_SCALE),
                             reads=wkeys, writes=[pTk])
                        qsum, qsk = qsr.next()
                        c.op("dve", lambda e, qsum=qsum, pT=pT: e.tensor_tensor(out=qsum[:, :], in0=pT[:, 0:512], in1=pT[:, 512:1024], op=ALU.add),
                             reads=[pTk], writes=[qsk])
                        quads.append((kp, qsum, qsk))
                    if pend is not None:
                        pkp, ppT, ppTk = pend
                        for hh in range(2):
                            pkt = 2 * pkp + hh
                            c.op("pe", lambda e, pkt=pkt, hh=hh, ppT=ppT, o_ps=o_ps: e.matmul(
                                o_ps[:, :], lhsT=V[:, pkt, :], rhs=ppT[:, hh * 512:(hh + 1) * 512],
                                start=(pkt == 0), stop=(pkt == 15)),
                                reads=[("V", pkt), ppTk], writes=[ok_])
                    if kp == 2 and deferred is not None:
                        finalize(deferred)
                        deferred = None
                    pend = (kp, pT, pTk) if kp < 8 else None
                deferred = (quads, o_ps, ok_, d_ps, dk_, hl, qb, qs)
        finalize(deferred)
        self.psi = 0
        self.wout_partial(d["even_w_out"][i], 1024 + g * 512, z, zkey)

    def odd_layer(self, i):
        c = self.c
        d = self.d
        self.phase_norm(d["odd_norm"][i, :, :])
        cw, cb, hba, hbx, lam, cc1, cch = self.cw, self.cb, self.hba, self.hbx, self.lam, self.cc1, self.cch
        self.dma_in(cw[:, :, :], d["conv_w"][i], ("cw",), "cw")
        self.dma_in(cb[:, :], d["conv_b"][i], ("cb",), "cb")
        self.dma_in(hba[:, :, :], d["gate_a_b"][i], ("hba",), "hba")
        self.dma_in(hbx[:, :, :], d["gate_x_b"][i], ("hbx",), "hbx")
        self.dma_in(lam[:, :, :], d["rglru_lambda"][i], ("lam",), "lam")
        c.op("dve", lambda e: e.tensor_scalar(out=hba[:, :, :], in0=hba[:, :, :], scalar1=0.5, scalar2=None, op0=ALU.mult),
             reads=[("hba",)], writes=[("hba",)])
        c.op("dve", lambda e: e.tensor_scalar(out=hbx[:, :, :], in0=hbx[:, :, :], scalar1=0.5, scalar2=None, op0=ALU.mult),
             reads=[("hbx",)], writes=[("hbx",)])
        c.op("act", lambda e: e.activation(out=cc1[:, :, :], in_=lam[:, :, :], func=AF.Exp, scale=-1.0),
             reads=[("lam",)], writes=[("cc1",)])
        c.op("act", lambda e: e.activation(out=cc1[:, :, :], in_=cc1[:, :, :], func=AF.Ln, bias=1.0),
             reads=[("cc1",)], writes=[("cc1",)])
        c.op("dve", lambda e: e.tensor_scalar(out=cch[:, :, :], in0=cc1[:, :, :], scalar1=-4.0, scalar2=None, op0=ALU.mult),
             reads=[("cc1",)], writes=[("cch",)])
        c.op("dve", lambda e: e.tensor_scalar(out=cc1[:, :, :], in0=cc1[:, :, :], scalar1=-8.0, scalar2=None, op0=ALU.mult),
             reads=[("cc1",), ("cch",)], writes=[("cc1",)])
        self.phase_rglru_all(i)

    def phase_rglru_all(self, i):
        c = self.c
        d = self.d
        self.arena_reset(full=True)
        m = self.mem
        z = m.alloc("z", [2, S], BF16)
        xrb = m.alloc("xrb", [S + 16], BF16)
        sgs = [m.alloc(f"sg{j}", [S], BF16) for j in range(2)]
        xc = m.alloc("xc", [S], F32)
        xcb = m.alloc("xcb", [S], BF16)
        aa = [m.alloc(f"a{j}", [S], F32) for j in range(2)]
        uu = [m.alloc(f"u{j}", [S], F32) for j in range(2)]
        ixc = [m.alloc(f"ixc{j}", [S], F32) for j in range(2)]
        trr = Ring(m, "tr", 3, [512], F32)
        tir = Ring(m, "ti", 4, [512], F32)
        diag = m.alloc("diag", [4, 128], BF16)
        gwb = [m.alloc(f"gwb{j}", [4, 128], BF16) for j in range(2)]
        win = d["odd_w_in"]
        cw, cb, hba, hbx, cc1, cch = self.cw, self.cb, self.hba, self.hbx, self.cc1, self.cch
        identb = self.ident_b
        x = self.x
        c.op("dve", lambda e: e.memset(xrb[:, 0:2], 0.0), writes=[("xrb", 0)])
        c.op("dve", lambda e: e.memset(xrb[:, S + 2:S + 16], 0.0), writes=[("xrb", 3)])
        gws = {}
        st = c.st
        for j in range(3):
            old = st.pop(("w", j), None)
            for hh in range(2):
                if old is not None:
                    rd = dict(old[1])
                    if old[0] is not None and rd.get(old[0][0], 0) < old[0][1]:
                        rd[old[0][0]] = old[0][1]
                    st[("wh", 2 * j + hh)] = [None, rd]
        whi = [0]

        def wh_load(view, nk, cols, nh=1):
            if nh == 2 and whi[0] % 2 == 1:
                whi[0] += 1
            j = whi[0] % 6
            whi[0] += nh
            base = self.wslots[j // 2]
            off = (j % 2) * 1024
            dst = base[:, off:off + nk * cols].rearrange("p (k c) -> p k c", k=nk)
            keys = [("wh", j + t_) for t_ in range(nh)]
            c.op("pool", lambda e: e.dma_start(out=dst, in_=view), writes=keys, dsem=f"wh{j}")
            return dst, keys

        def proj_tile(w, wks, tb):
            ps, pk = self.psum()
            ts = slice(tb * 512, (tb + 1) * 512)
            h = self.h
            for kc in range(8):
                c.op("pe", lambda e, kc=kc: e.matmul(ps[:, :], lhsT=w[:, kc, 0:128], rhs=h[:, kc, ts],
                                                     start=(kc == 0), stop=(kc == 7)),
                     reads=wks + [self.hk(kc, tb)], writes=[pk])
            return ps, pk

        awts = {}

        def stage_a_load(k):
            wx = wh_load(win[i, :, k * 128:(k + 1) * 128].rearrange("(k p) c -> p k c", p=128), 8, 128)
            wg = wh_load(win[i, :, 2048 + k * 128: 2048 + (k + 1) * 128].rearrange("(k p) c -> p k c", p=128), 8, 128)
            awts[k] = (wx, wg)

        def stage_a_tile(k, t):
            (wx, wxk), (wg, wgk) = awts[k]
            tb = t % 4
            ts = slice(tb * 512, (tb + 1) * 512)
            sg = sgs[k % 2]
            if t < 4:
                ps, pk = proj_tile(wx, wxk, tb)
                c.op("act", lambda e: e.activation(out=xrb[:, 2 + tb * 512: 2 + (tb + 1) * 512], in_=ps[:, :], func=AF.Copy),
                     reads=[pk], writes=[("xrb", tb)])
            else:
                ps, pk = proj_tile(wg, wgk, tb)
                tg, tgk = trr.next()
                c.op("act", lambda e: e.activation(out=tg[:, :], in_=ps[:, :], func=AF.Tanh, scale=0.5),
                     reads=[pk], writes=[tgk])
                c.op("dve", lambda e: e.scalar_tensor_tensor(out=sg[:, ts], in0=tg[:, :], scalar=1.0, in1=ps[:, :],
                                                             op0=ALU.add, op1=ALU.mult),
                     reads=[pk, tgk], writes=[("sg", k % 2, tb)])
            if t == 7:
                awts.pop(k)

        def stage_b(k):
            gw = gwb[k % 2]
            gwk = [("gw", k % 2)]
            c.op("pool", lambda e, gw=gw: e.dma_start(out=gw[:, :, :], in_=d["gate_w"][i, k, :, :, :]),
                 writes=gwk, dsem=f"gw{k % 2}")
            gws[k] = (gw, gwk)
            for jt in range(4):
                c.op("dve" if "diag_dve" in DBG else "pool", lambda e, jt=jt: e.tensor_scalar(
                    out=diag[:, jt, :], in0=identb[:, :], scalar1=cw[:, k, jt:jt + 1], scalar2=0.0,
                    op0=ALU.mult, op1=ALU.add),
                    writes=[("diag", jt)])
            for tb in range(4):
                ts = slice(tb * 512, (tb + 1) * 512)
                ps, pk = self.psum()
                rk = [("xrb", t_) for t_ in range(max(0, tb - 1), min(3, tb + 1) + 1)]
                for jt in range(4):
                    c.op("pe", lambda e, ps=ps, jt=jt, tb=tb: e.matmul(
                        ps[:, :], lhsT=diag[:, jt, :], rhs=xrb[:, tb * 512 + jt: tb * 512 + jt + 512],
                        start=(jt == 0), stop=(jt == 3)),
                        reads=[("diag", jt)] + rk, writes=[pk])
                c.op("act", lambda e, ps=ps, ts=ts: e.activation(out=xc[:, ts], in_=ps[:, :], func=AF.Identity,
                                                                 bias=cb[:, k:k + 1]),
                     reads=[pk], writes=[("xc", tb)])
                c.op("dve" if "xcb_dve" in DBG else "pool", lambda e, ts=ts: e.tensor_scalar(out=xcb[:, ts], in0=xc[:, ts], scalar1=1.0, scalar2=0.0,
                                                              op0=ALU.mult, op1=ALU.add),
                     reads=[("xc", tb)], writes=[("xcb", tb)])

        def stage_c_step(k, dr, tb):
            gw, gwk = gws[k]
            a_t, u_t, x_t = aa[dr], uu[dr], ixc[dr]
            ts = slice(tb * 512, (tb + 1) * 512)
            psa, pka = self.psum()
            psx, pkx = self.psum()
            c.op("pe", lambda e: e.matmul(psx[:, :], lhsT=gw[:, 2 + dr, :], rhs=xcb[:, ts], start=True, stop=True),
                 reads=gwk + [("xcb", tb)], writes=[pkx])
            c.op("pe", lambda e: e.matmul(psa[:, :], lhsT=gw[:, dr, :], rhs=xcb[:, ts], start=True, stop=True),
                 reads=gwk + [("xcb", tb)], writes=[pka])
            tr, trk = trr.next()
            ti, tik = tir.next()
            c.op("act", lambda e: e.activation(out=ti[:, :], in_=psx[:, :], func=AF.Tanh, scale=0.5,
                                               bias=hbx[:, dr, k:k + 1]),
                 reads=[pkx], writes=[tik])
            c.op("act", lambda e: e.activation(out=tr[:, :], in_=psa[:, :], func=AF.Tanh, scale=0.5,
                                               bias=hba[:, dr, k:k + 1]),
                 reads=[pka], writes=[trk])
            c.op("act", lambda e: e.activation(out=a_t[:, ts], in_=tr[:, :], func=AF.Exp, scale=cch[:, dr, k:k + 1],
                                               bias=cch[:, dr, k:k + 1]),
                 reads=[trk], writes=[("a", dr, tb)])
            c.op("act", lambda e: e.activation(out=u_t[:, ts], in_=tr[:, :], func=AF.Exp, scale=cc1[:, dr, k:k + 1],
                                               bias=cc1[:, dr, k:k + 1]),
                 reads=[trk], writes=[("u", dr, tb)])
            c.op("dve" if "i_dve" in DBG else "pool", lambda e: e.tensor_scalar(out=ti[:, :], in0=ti[:, :], scalar1=0.5, scalar2=0.5,
                                                   op0=ALU.mult, op1=ALU.add),
                 reads=[tik], writes=[tik])
            c.op("dve" if "i_dve" in DBG else "pool", lambda e: e.tensor_tensor(out=x_t[:, ts], in0=ti[:, :], in1=xc[:, ts], op=ALU.mult),
                 reads=[tik, ("xc", tb)], writes=[("ixc", dr, tb)])
            if dr == 1 and tb == 0:
                gws.pop(k)

        ks = lambda nm, dr: [(nm, dr, tb) for tb in range(4)]

        def d_sqrt(dr):
            u_t = uu[dr]
            c.op("act", lambda e: e.activation(out=u_t[:, :], in_=u_t[:, :], func=AF.Sqrt, scale=-1.0, bias=1.0),
                 reads=ks("u", dr), writes=ks("u", dr))

        def u_piece(dr, tb):
            u_t, x_t = uu[dr], ixc[dr]
            lo, hi = tb * 512, (tb + 1) * 512
            c.op("pool", lambda e: e.tensor_tensor(out=u_t[:, lo:hi], in0=u_t[:, lo:hi], in1=x_t[:, lo:hi], op=ALU.mult),
                 reads=[("u", dr, tb), ("ixc", dr, tb)], writes=[("u", dr, tb)])

        def d_piece(dr, tb):
            a_t, u_t, x_t = aa[dr], uu[dr], ixc[dr]
            lo, hi = tb * 512, (tb + 1) * 512
            if dr == 0:
                init = 0.0 if tb == 0 else x_t[:, lo - 1:lo]
                rd = [("u", dr, tb), ("a", dr, tb)] + ([("ixc", dr, tb - 1)] if tb > 0 else [])
                c.op("dve", lambda e: e.tensor_tensor_scan(
                    out=x_t[:, lo:hi], data0=a_t[:, lo:hi], data1=u_t[:, lo:hi], initial=init,
                    op0=ALU.mult, op1=ALU.add),
                    reads=rd, writes=[("ixc", dr, tb)])
            else:
                init = 0.0 if tb == 3 else x_t[:, hi:hi + 1]
                rd = [("u", dr, tb), ("a", dr, tb)] + ([("ixc", dr, tb + 1)] if tb < 3 else [])
                c.op("dve", lambda e: e.tensor_tensor_scan(
                    out=x_t[:, lo:hi][:, ::-1], data0=a_t[:, lo:hi][:, ::-1], data1=u_t[:, lo:hi][:, ::-1], initial=init,
                    op0=ALU.mult, op1=ALU.add),
                    reads=rd, writes=[("ixc", dr, tb)])

        def yz_piece(k, tb):
            sg = sgs[k % 2]
            ts = slice(tb * 512, (tb + 1) * 512)
            c.op("dve", lambda e: e.scalar_tensor_tensor(out=ixc[0][:, ts], in0=ixc[0][:, ts], scalar=1.0, in1=ixc[1][:, ts],
                                                         op0=ALU.mult, op1=ALU.add),
                 reads=[("ixc", 0, tb), ("ixc", 1, tb)], writes=[("ixc", 0, tb)])
            c.op("dve", lambda e: e.scalar_tensor_tensor(out=z[:, k % 2, ts], in0=ixc[0][:, ts], scalar=0.5, in1=sg[:, ts],
                                                         op0=ALU.mult, op1=ALU.mult),
                 reads=[("ixc", 0, tb), ("sg", k % 2, tb)], writes=[("z", k % 2, tb)])

        def w_tiles(kp):
            r0 = kp * 256
            view = d["odd_w_out"][i][r0:r0 + 256, :].rearrange("(k p) c -> p k c", p=128)
            w, wk = wh_load(view, 2, 1024, nh=2)
            tiles = []
            for tb in (3, 2, 1, 0):
                for n in range(8):
                    def tile(n=n, tb=tb):
                        ts = slice(tb * 512, (tb + 1) * 512)
                        ps, pk = self.psum()
                        for ec in range(2):
                            c.op("pe", lambda e, ec=ec: e.matmul(
                                ps[:, :], lhsT=w[:, ec, n * 128:(n + 1) * 128], rhs=z[:, ec, ts],
                                start=(ec == 0), stop=(ec == 1)),
                                reads=wk + [("z", ec, tb)], writes=[pk])
                        c.op("dve", lambda e: e.tensor_tensor(out=x[:, n, ts], in0=ps[:, :], in1=x[:, n, ts], op=ALU.add),
                             reads=[pk, self.xk(n, tb)], writes=[self.xk(n, tb)])
                    tiles.append(tile)
            return tiles

        NBLK = 16
        stage_a_load(0)
        stage_a_load(1)
        for t in range(8):
            stage_a_tile(0, t)
        stage_b(0)
        wq = []
        for k in range(NBLK):
            if k + 2 < NBLK:
                stage_a_load(k + 2)
            for s_ in range(8):
                if s_ < 4:
                    tbp = 3 - s_
                    if k >= 1:
                        if tbp > 0:
                            u_piece(1, tbp - 1)
                        d_piece(1, tbp)
                        yz_piece(k - 1, tbp)
                    stage_c_step(k, 0, tbp)
                else:
                    if s_ == 4 and k % 2 == 0 and k >= 2:
                        wq.extend(w_tiles(k // 2 - 1))
                    if s_ == 4:
                        d_sqrt(0)
                        u_piece(0, 0)
                    if s_ < 7:
                        u_piece(0, s_ - 3)
                    d_piece(0, s_ - 4)
                    stage_c_step(k, 1, 7 - s_)
                if k + 1 < NBLK:
                    stage_a_tile(k + 1, s_)
                for _ in range(5):
                    if wq:
                        wq.pop(0)()
            d_sqrt(1)
            u_piece(1, 3)
            if k + 1 < NBLK:
                stage_b(k + 1)
        for s_ in range(4):
            if s_ < 3:
                u_piece(1, 2 - s_)
            d_piece(1, 3 - s_)
            yz_piece(NBLK - 1, 3 - s_)
        while wq:
            wq.pop(0)()
        for tile in w_tiles(NBLK // 2 - 1):
            tile()
        for j in range(3):
            rd = {}
            for hh in range(2):
                o = st.pop(("wh", 2 * j + hh), None)
                if o is None:
                    continue
                for sem, val in o[1].items():
                    if rd.get(sem, 0) < val:
                        rd[sem] = val
                if o[0] is not None and rd.get(o[0][0], 0) < o[0][1]:
                    rd[o[0][0]] = o[0][1]
            st[("w", j)] = [None, rd]


def _constants():
    ident = np.eye(128, dtype=np.float32)
    perm = np.zeros((128, 128), np.float32)
    for mm_ in range(128):
        base = (mm_ // 64) * 64
        r = mm_ - base
        kk = base + (r + 32 if r < 32 else r - 32)
        perm[kk, mm_] = 1.0
    t = np.arange(S)
    row = (t // 64).astype(np.float32)
    col = (t % 64).astype(np.float32)
    inv_freq = (np.float32(10000.0) ** (-np.arange(0, 64, 2, dtype=np.float32) / np.float32(64))).astype(np.float32)
    ang_r = row[:, None] * inv_freq[None, :]
    ang_c = col[:, None] * inv_freq[None, :]
    cos = np.zeros((128, S), np.float32)
    sin = np.zeros((128, S), np.float32)
    cos[0:32] = np.cos(ang_r).T
    cos[32:64] = np.cos(ang_r).T
    cos[64:96] = np.cos(ang_c).T
    cos[96:128] = np.cos(ang_c).T
    sin[0:32] = -np.sin(ang_r).T
    sin[32:64] = np.sin(ang_r).T
    sin[64:96] = -np.sin(ang_c).T
    sin[96:128] = np.sin(ang_c).T
    ss = np.arange(S, dtype=np.int64)
    ph = (np.outer(ss, ss) % S).astype(np.float64) * (2.0 * np.pi / S)
    dftc = (np.cos(ph) / math.sqrt(S)).astype(ml_dtypes.bfloat16)
    dfts = (-np.sin(ph) / math.sqrt(S)).astype(ml_dtypes.bfloat16)
    cs = np.arange(256, dtype=np.int64)
    phc = (np.outer(cs, cs) % 256).astype(np.float64) * (2.0 * np.pi / 256)
    cc = (np.cos(phc) / 16.0).astype(ml_dtypes.bfloat16)
    sc = (np.sin(phc) / 16.0).astype(ml_dtypes.bfloat16)
    return {"c_ident": ident, "c_perm": perm, "c_cos": cos, "c_sin": sin,
            "c_dftc": dftc, "c_dfts": dfts, "c_cc": cc, "c_sc": sc}


def _layout(inputs):
    f = lambda a: np.ascontiguousarray(np.asarray(a, dtype=np.float32))
    o = {}
    o["even_norm"] = f(np.asarray(inputs["even_norm"]).reshape(2, 8, 128).transpose(0, 2, 1))
    o["odd_norm"] = f(np.asarray(inputs["odd_norm"]).reshape(2, 8, 128).transpose(0, 2, 1))
    o["q_gain"] = f(np.asarray(inputs["q_gain"]).reshape(2, 128, 1))
    o["k_gain"] = f(np.asarray(inputs["k_gain"]).reshape(2, 128, 1))
    o["conv_w"] = f(np.asarray(inputs["conv_w"]).reshape(2, 4, 16, 128).transpose(0, 3, 2, 1))
    o["conv_b"] = f(np.asarray(inputs["conv_b"]).reshape(2, 16, 128).transpose(0, 2, 1))
    for nm in ("gate_a_b", "gate_x_b"):
        o[nm] = f(np.asarray(inputs[nm]).transpose(0, 3, 1, 2))
    o["rglru_lambda"] = f(np.asarray(inputs["rglru_lambda"]).reshape(2, 2, 16, 128).transpose(0, 3, 1, 2))
    for nm in ("even_w_in", "fourier_w", "even_w_out", "odd_w_in", "odd_w_out", "final_norm"):
        o[nm] = f(inputs[nm])
    gw = np.stack([np.asarray(inputs["gate_a_w"]), np.asarray(inputs["gate_x_w"])], axis=0)
    o["gate_w"] = f(gw.transpose(1, 3, 4, 0, 2, 5).reshape(2, 16, 128, 4, 128))
    return o


_CACHE = {}


def _program(layers, do_final):
    key = (tuple(layers), do_final)
    if key not in _CACHE:
        _CACHE[key] = Builder(list(layers), do_final).nc
    return _CACHE[key]


def run_layers(x, shared, layers, do_final, core_ids):
    nc = _program(layers, do_final)
    in_maps = []
    for ci in core_ids:
        mp = dict(shared)
        mp["x"] = np.ascontiguousarray(x[ci * NB:(ci + 1) * NB])
        in_maps.append(mp)
    res = run_bass_kernel_spmd(nc, in_maps, core_ids=list(range(len(core_ids))))
    return [r["y"] for r in res.results]


def kernel(**inputs):
    x = np.asarray(inputs["x"], dtype=np.float32)
    shared = _layout(inputs)
    shared.update(_constants())
    outs = run_layers(x, shared, [0, 1, 2, 3], True, list(range(NCORES)))
    return np.concatenate(outs, axis=0).astype(np.float32)
```
